# Optimizing a Trainium2 kernel written in Bass

```python
import jax, jax.numpy as jnp
from jax import lax
import numpy as np

D_MODEL = 1024
BATCH = 2
SEQ = 8192
DEPTH = 2

CHUNK = 64
M_HEADS = 4
M_WIDTH = D_MODEL // 2
M_HEAD_DIM = M_WIDTH // M_HEADS
CONV_K = 4
P_GROUPS = 4
P_WIDTH = D_MODEL // 4
P_GROUP_DIM = P_WIDTH // P_GROUPS
POOL_WINDOWS = (2, 4, 8, 16)
S_HEADS = 4
S_WIDTH = D_MODEL // 4
S_HEAD_DIM = S_WIDTH // S_HEADS
SB_BLOCK = 128
N_BRANCH = 3
N_IN = 5 * M_WIDTH + 2 * M_HEADS + 2 * P_WIDTH + 4 * S_WIDTH + N_BRANCH * D_MODEL
EPS = 1e-6

kernel_name = "hybrid_mlstm_pool_stickbreak_block"


def _split_points():
    sizes = (M_WIDTH, M_WIDTH, M_WIDTH, M_HEADS, M_HEADS, M_WIDTH, M_WIDTH,
             P_WIDTH, P_WIDTH, S_WIDTH, S_WIDTH, S_WIDTH, S_WIDTH)
    pts, acc = [], 0
    for s in sizes:
        acc += s
        pts.append(acc)
    return pts


def _rmsnorm(x, g):
    xf = x.astype(jnp.float32)
    y = xf * lax.rsqrt(jnp.mean(xf * xf, axis=-1, keepdims=True) + EPS)
    return (y * g.astype(jnp.float32)).astype(x.dtype)


def _causal_conv(x, w, b):
    k_w = w.shape[0]
    s = x.shape[1]
    xp = jnp.pad(x, ((0, 0), (k_w - 1, 0), (0, 0)))
    return sum(xp[:, j:j + s] * w[j] for j in range(k_w)) + b


def _mlstm_chunkwise(q, k, v, i_pre, f_pre):
    b_, h_, s_, dh = q.shape
    nc = s_ // CHUNK
    q = q.reshape(b_, h_, nc, CHUNK, dh)
    k = k.reshape(b_, h_, nc, CHUNK, dh)
    v = v.reshape(b_, h_, nc, CHUNK, dh)
    ig = i_pre.reshape(b_, h_, nc, CHUNK)
    bcum = jnp.cumsum(jax.nn.log_sigmoid(f_pre).reshape(b_, h_, nc, CHUNK), axis=-1)
    b_end = bcum[..., -1]

    a = b_end[..., None] - bcum + ig
    a_max = jnp.max(a, axis=-1)
    wa = jnp.exp(a - a_max[..., None])
    d_c = jnp.einsum('bhcsv,bhcsk->bhcvk', v * wa[..., None], k)
    d_n = jnp.einsum('bhcs,bhcsk->bhck', wa, k)

    def step(carry, xs):
        c_st, n_st, m_st = carry
        be, am, dc, dn = xs
        m_new = jnp.maximum(be + m_st, am)
        decay = jnp.exp(be + m_st - m_new)
        inw = jnp.exp(am - m_new)
        c_new = decay[..., None, None] * c_st + inw[..., None, None] * dc
        n_new = decay[..., None] * n_st + inw[..., None] * dn
        return (c_new, n_new, m_new), (c_st, n_st, m_st)

    init = (jnp.zeros((b_, h_, dh, dh), jnp.float32),
            jnp.zeros((b_, h_, dh), jnp.float32),
            jnp.zeros((b_, h_), jnp.float32))
    xs = (jnp.moveaxis(b_end, 2, 0), jnp.moveaxis(a_max, 2, 0),
          jnp.moveaxis(d_c, 2, 0), jnp.moveaxis(d_n, 2, 0))
    _, (c_prev, n_prev, m_prev) = lax.scan(step, init, xs)
    c_prev = jnp.moveaxis(c_prev, 0, 2)
    n_prev = jnp.moveaxis(n_prev, 0, 2)
    m_prev = jnp.moveaxis(m_prev, 0, 2)

    tri = jnp.tril(jnp.ones((CHUNK, CHUNK), dtype=bool))
    log_d = jnp.where(tri, bcum[..., :, None] - bcum[..., None, :] + ig[..., None, :], -jnp.inf)
    log_inter = bcum + m_prev[..., None]
    m_t = jnp.maximum(log_inter, jnp.max(log_d, axis=-1))
    w_inter = jnp.exp(log_inter - m_t)
    w_intra = jnp.exp(log_d - m_t[..., None]) * jnp.einsum('bhctd,bhcsd->bhcts', q, k)
    num = (w_inter[..., None] * jnp.einsum('bhcvk,bhctk->bhctv', c_prev, q)
           + jnp.einsum('bhcts,bhcsv->bhctv', w_intra, v))
    den = w_inter * jnp.einsum('bhck,bhctk->bhct', n_prev, q) + jnp.sum(w_intra, axis=-1)
    h = num / jnp.maximum(jnp.abs(den), jnp.exp(-m_t))[..., None]
    return h.reshape(b_, h_, s_, dh)


def _mlstm_branch(mq, mk, mv, mi, mf, mo, mz, gate_b, conv_w, conv_b, norm_g):
    b_, s_, _ = mq.shape
    dt = mq.dtype
    qk = jax.nn.silu(_causal_conv(jnp.concatenate([mq, mk], axis=-1), conv_w, conv_b))
    q, k = jnp.split(qk, 2, axis=-1)

    def heads(t):
        return t.astype(jnp.float32).reshape(b_, s_, M_HEADS, M_HEAD_DIM).transpose(0, 2, 1, 3)

    i_pre = (mi + gate_b[:M_HEADS]).astype(jnp.float32).transpose(0, 2, 1)
    f_pre = (mf + gate_b[M_HEADS:]).astype(jnp.float32).transpose(0, 2, 1)
    h = _mlstm_chunkwise(heads(q), heads(k) * (M_HEAD_DIM ** -0.5), heads(mv), i_pre, f_pre)
    h = h.transpose(0, 2, 1, 3) * jax.nn.sigmoid(mo.astype(jnp.float32)).reshape(
        b_, s_, M_HEADS, M_HEAD_DIM)
    h = h * lax.rsqrt(jnp.mean(h * h, axis=-1, keepdims=True) + EPS)
    h = h.reshape(b_, s_, M_WIDTH) * norm_g.astype(jnp.float32)
    return (h * jax.nn.silu(mz.astype(jnp.float32))).astype(dt)


def _pool_branch(pu, pz, pool_w, pool_scale):
    b_, s_, _ = pu.shape
    dt = pu.dtype
    u = pu.astype(jnp.float32).reshape(b_, s_, P_GROUPS, P_GROUP_DIM)
    cs = jnp.concatenate([jnp.zeros((b_, 1, P_GROUPS, P_GROUP_DIM), jnp.float32),
                          jnp.cumsum(u, axis=1)], axis=1)
    t1 = jnp.arange(1, s_ + 1)[:, None]
    lo = jnp.maximum(t1 - jnp.array(POOL_WINDOWS)[None, :], 0)
    idx = jnp.broadcast_to(lo[None, :, :, None], (b_, s_, P_GROUPS, P_GROUP_DIM))
    cs_lo = jnp.take_along_axis(cs, idx, axis=1)
    count = (t1 - lo).astype(jnp.float32)[None, :, :, None]
    p = (cs[:, 1:] - cs_lo) / count - u
    p = jnp.einsum('bsgc,gcd->bsgd', p, pool_w.astype(jnp.float32))
    p = p.reshape(b_, s_, P_WIDTH) * pool_scale.astype(jnp.float32)
    return (p * jax.nn.silu(pz.astype(jnp.float32))).astype(dt)


def _stick_breaking_branch(sq, sk, sv, sz):
    b_, s_, _ = sq.shape
    dt = sq.dtype

    def heads(t):
        return t.astype(jnp.float32).reshape(b_, s_, S_HEADS, S_HEAD_DIM).transpose(0, 2, 1, 3)

    q = heads(sq) * (S_HEAD_DIM ** -0.5)
    k = heads(sk)
    v = heads(sv)
    outs = []
    for blk in range(s_ // SB_BLOCK):
        q0 = blk * SB_BLOCK
        kend = q0 + SB_BLOCK
        z = jnp.einsum('bhtd,bhsd->bhts', q[:, :, q0:kend], k[:, :, :kend])
        t_pos = q0 + jnp.arange(SB_BLOCK)
        s_pos = jnp.arange(kend)
        causal = s_pos[None, :] < t_pos[:, None]
        log_fail = jnp.where(causal, jax.nn.log_sigmoid(-z), 0.0)
        log_surv = lax.cumsum(log_fail, axis=3, reverse=True) - log_fail
        attn = jnp.where(causal, jnp.exp(jax.nn.log_sigmoid(z) + log_surv), 0.0)
        outs.append(jnp.einsum('bhts,bhsd->bhtd', attn, v[:, :, :kend]))
    o = jnp.concatenate(outs, axis=2).transpose(0, 2, 1, 3).reshape(b_, s_, S_WIDTH)
    return (o * jax.nn.silu(sz.astype(jnp.float32))).astype(dt)


def setup_inputs(seed: int = 0) -> dict:
    key = jax.random.key(seed)
    ks = jax.random.split(key, 20)
    f32 = jnp.float32
    nrm = lambda k, shape: jax.random.normal(k, shape, f32)
    i_bias = 0.1 * nrm(ks[6], (DEPTH, M_HEADS))
    f_bias = jnp.linspace(3.0, 6.0, M_HEADS, dtype=f32)[None, :] + 0.1 * nrm(ks[7], (DEPTH, M_HEADS))
    return {
        "x": nrm(ks[0], (BATCH, SEQ, D_MODEL)),
        "c": nrm(ks[1], (BATCH, D_MODEL)),
        "norm_g": 1.0 + 0.05 * nrm(ks[2], (DEPTH, D_MODEL)),
        "w_ada": nrm(ks[3], (DEPTH, D_MODEL, 3 * D_MODEL)) * (0.5 * D_MODEL ** -0.5),
        "b_ada": 0.02 * nrm(ks[4], (DEPTH, 3 * D_MODEL)),
        "w_in": nrm(ks[5], (DEPTH, D_MODEL, N_IN)) * D_MODEL ** -0.5,
        "m_gate_b": jnp.concatenate([i_bias, f_bias], axis=-1),
        "conv_w": nrm(ks[8], (DEPTH, CONV_K, 2 * M_WIDTH)) * CONV_K ** -0.5,
        "conv_b": 0.02 * nrm(ks[9], (DEPTH, 2 * M_WIDTH)),
        "m_norm_g": 1.0 + 0.05 * nrm(ks[10], (DEPTH, M_WIDTH)),
        "pool_w": nrm(ks[11], (DEPTH, P_GROUPS, P_GROUP_DIM, P_GROUP_DIM)) * P_GROUP_DIM ** -0.5,
        "pool_scale": 1.0 + 0.05 * nrm(ks[12], (DEPTH, P_WIDTH)),
        "w_br_m": nrm(ks[13], (DEPTH, M_WIDTH, D_MODEL)) * M_WIDTH ** -0.5,
        "w_br_p": nrm(ks[14], (DEPTH, P_WIDTH, D_MODEL)) * P_WIDTH ** -0.5,
        "w_br_s": nrm(ks[15], (DEPTH, S_WIDTH, D_MODEL)) * S_WIDTH ** -0.5,
        "gate_b": 0.02 * nrm(ks[16], (DEPTH, N_BRANCH * D_MODEL)),
        "w_out": nrm(ks[17], (DEPTH, D_MODEL, D_MODEL)) * D_MODEL ** -0.5,
        "final_g": 1.0 + 0.05 * nrm(ks[18], (D_MODEL,)),
    }


def reference(x, c, norm_g, w_ada, b_ada, w_in, m_gate_b, conv_w, conv_b, m_norm_g,
              pool_w, pool_scale, w_br_m, w_br_p, w_br_s, gate_b, w_out, final_g):
    b_, s_, _ = x.shape
    for l in range(DEPTH):
        mod = c @ w_ada[l] + b_ada[l]
        shift, scale, gate = jnp.split(mod, 3, axis=-1)
        h = _rmsnorm(x, norm_g[l]) * (1.0 + scale[:, None, :]) + shift[:, None, :]
        proj = h @ w_in[l]
        (mq, mk, mv, mi, mf, mo, mz, pu, pz, sq, sk, sv, sz, gpre) = jnp.split(
            proj, _split_points(), axis=-1)
        y_m = _mlstm_branch(mq, mk, mv, mi, mf, mo, mz, m_gate_b[l], conv_w[l], conv_b[l], m_norm_g[l])
        y_p = _pool_branch(pu, pz, pool_w[l], pool_scale[l])
        y_s = _stick_breaking_branch(sq, sk, sv, sz)
        g = jax.nn.sigmoid((gpre + gate_b[l]).astype(jnp.float32)).astype(x.dtype)
        g = g.reshape(b_, s_, N_BRANCH, D_MODEL)
        merged = (g[:, :, 0] * (y_m @ w_br_m[l])
                  + g[:, :, 1] * (y_p @ w_br_p[l])
                  + g[:, :, 2] * (y_s @ w_br_s[l]))
        x = x + gate[:, None, :] * (merged @ w_out[l])
    return _rmsnorm(x, final_g)
```

```python
import contextlib
import numpy as np
import ml_dtypes
import concourse.bass as bass
import concourse.mybir as mybir
from concourse.bass_utils import run_bass_kernel_spmd

F32 = mybir.dt.float32
BF16 = mybir.dt.bfloat16
AF = mybir.ActivationFunctionType
ALU = mybir.AluOpType
AX = mybir.AxisListType

D = 1024
SEQ = 8192
NB = 2
DEPTH = 2
EPS = 1e-6
EPOCH = 12000
POOL_WINDOWS = (2, 4, 8, 16)


class _Rec:
    def __init__(self):
        self.calls = []

    def __getattr__(self, name):
        def f(*a, **kw):
            self.calls.append((name, a, kw))
        return f


class K:
    def __init__(self, nc, stack, n_dma_sems=12):
        self.nc = nc
        self.stack = stack
        self.engs = ['pe', 'act', 'dve', 'pool', 'sp']
        self.q = {e: [] for e in self.engs}
        self.cnt = {e: 0 for e in self.engs}
        self.epoch = {e: 0 for e in self.engs}
        self.sems = {}
        self.known = {e: {} for e in self.engs}
        self.last_w = {}
        self.readers = {}
        self.dma_sems = {q: [stack.enter_context(nc.semaphore(f"dma_{q}{i}")) for i in range(n)]
                         for q, n in (('sp', 10), ('pool', 6), ('act', 2))}
        self.dma_cnt = {q: [0] * len(v) for q, v in self.dma_sems.items()}
        self.dma_rr = {q: 0 for q in self.dma_sems}

    def _sem(self, e):
        key = (e, self.epoch[e])
        if key not in self.sems:
            self.sems[key] = self.stack.enter_context(self.nc.semaphore(f"s_{e}_{self.epoch[e]}"))
        return self.sems[key]

    def _need(self, e, tok):
        if tok is None:
            return
        sem, val, src = tok[:3]
        if src == e and e == 'pe':
            return
        kn = self.known[e]
        if kn.get(id(sem), 0) >= val:
            return
        kn[id(sem)] = val
        self.q[e].append(('wait', sem, val))

    def _deps(self, e, reads, writes):
        for k in reads:
            self._need(e, self.last_w.get(k))
            if k.startswith('bank') or k == 'tpb':
                for r in self.readers.get(k, ()):
                    if r[2] != e:
                        self._need(e, r)
        for k in writes:
            self._need(e, self.last_w.get(k))
            for r in self.readers.get(k, ()):
                if r[2] == e:
                    continue
                self._need(e, r)

    def _commit(self, tok, reads, writes):
        for k in writes:
            self.last_w[k] = tok
            self.readers[k] = []
        for k in reads:
            self.readers.setdefault(k, []).append(tok)

    def op(self, e, fn, reads=(), writes=()):
        self._deps(e, reads, writes)
        if self.cnt[e] >= EPOCH:
            self.epoch[e] += 1
            self.cnt[e] = 0
        sem = self._sem(e)
        self.cnt[e] += 1
        tok = (sem, self.cnt[e], e)
        rec = _Rec()
        fn(rec)
        assert len(rec.calls) == 1
        name, a, kw = rec.calls[0]
        self.q[e].append(('op', (lambda eng, name=name, a=a, kw=kw: getattr(eng, name)(*a, **kw)), sem, 1))
        self._commit(tok, reads, writes)
        return tok

    def dma(self, e, out, in_, reads=(), writes=(), **kw):
        self._deps(e, reads, writes)
        i = self.dma_rr[e]
        self.dma_rr[e] = (i + 1) % len(self.dma_sems[e])
        sem = self.dma_sems[e][i]
        cnts = self.dma_cnt[e]
        if cnts[i] > 0:
            self._need(e, (sem, cnts[i] * 16, 'dma'))
        cnts[i] += 1
        tok = (sem, cnts[i] * 16, 'dma')
        self.q[e].append(('op', lambda eng: eng.dma_start(out=out, in_=in_, **kw), sem, 16))
        self._commit(tok, reads, writes)
        return tok

    def wait_all(self, e, keys):
        for k in keys:
            self._need(e, self.last_w.get(k))

    def emit(self):
        nc = self.nc
        with nc.Block() as block:
            def replay(name, eng):
                for it in self.q[name]:
                    if it[0] == 'wait':
                        eng.wait_ge(it[1], it[2])
                    else:
                        it[1](eng).then_inc(it[2], it[3])

            @block.sync
            def _(eng):
                replay('sp', eng)

            @block.scalar
            def _(eng):
                replay('act', eng)

            @block.vector
            def _(eng):
                replay('dve', eng)

            @block.gpsimd
            def _(eng):
                replay('pool', eng)

            @block.tensor
            def _(eng):
                replay('pe', eng)


class Ctx:
    def __init__(self, nc, st):
        self.nc = nc
        self.st = st
        self.k = K(nc, st)
        self.n = 0
        self.outkeys = []

    def sb(self, name, shape, dt):
        return self.st.enter_context(self.nc.sbuf_tensor("s_" + name, list(shape), dt))

    def ps(self, name, shape, dt):
        return self.st.enter_context(self.nc.psum_tensor("p_" + name, list(shape), dt))


def emit_mod(cx, w_ada, b_ada, cT_d, ng_d, banks, need_gate):
    k = cx.k
    ncol = 3 if need_gate else 2
    cT = cx.sb("cT", [128, 8], F32)
    ng = cx.sb("ng", [128, 8], F32)
    modrow = cx.sb("modrow", [1, 3072], F32)
    one11 = cx.sb("one11", [1, 128], F32)
    s1 = cx.sb("s1", [128, 8], F32)
    s2 = cx.sb("s2", [128, 8], F32)
    gate_bc = cx.sb("gate_bc", [128, 1024], F32) if need_gate else None
    NWA = 4
    wa = [cx.sb(f"wa{i}", [128, 512], F32) for i in range(NWA)]
    k.dma('sp', cT[:], cT_d, writes=['cT'])
    k.dma('sp', ng[:], ng_d, writes=['ng'])
    k.op('dve', lambda e: e.memset(one11[:], 1.0), writes=['one11'])
    ngrp = ncol * 2
    i = 0
    for kc in range(8):
        for cg in range(ngrp):
            buf = wa[i % NWA]
            bk = f"wa{i % NWA}"
            i += 1
            k.dma('sp', buf[:], w_ada[kc * 128:(kc + 1) * 128, cg * 512:(cg + 1) * 512], writes=[bk])
            k.op('pe', lambda e, buf=buf, cg=cg, kc=kc: e.matmul(
                banks[cg][0:1, :], lhsT=cT[:, kc:kc + 1], rhs=buf[:], start=(kc == 0), stop=(kc == 7)),
                reads=[bk, 'cT'], writes=[f"bank{cg}"])
    for cg in range(ngrp):
        buf = wa[i % NWA]
        bk = f"wa{i % NWA}"
        i += 1
        k.dma('sp', buf[0:1, :], b_ada[0:1, cg * 512:(cg + 1) * 512], writes=[bk])
        k.op('dve', lambda e, cg=cg, buf=buf: e.tensor_tensor(
            out=modrow[0:1, cg * 512:(cg + 1) * 512], in0=banks[cg][0:1, :],
            in1=buf[0:1, :], op=ALU.add),
            reads=[f"bank{cg}", bk], writes=['modrow'])
    colb = banks[6]
    for cc in range(8):
        k.op('pe', lambda e, cc=cc: e.matmul(
            colb[:, cc:cc + 1], lhsT=modrow[0:1, 1024 + cc * 128:1024 + (cc + 1) * 128],
            rhs=one11[0:1, 0:1], start=True, stop=True), reads=['modrow', 'one11'], writes=['bank6'])
        k.op('pe', lambda e, cc=cc: e.matmul(
            colb[:, 8 + cc:9 + cc], lhsT=modrow[0:1, cc * 128:(cc + 1) * 128],
            rhs=one11[0:1, 0:1], start=True, stop=True), reads=['modrow', 'one11'], writes=['bank6'])
    k.op('dve', lambda e: e.scalar_tensor_tensor(
        out=s1[:], in0=colb[:, 0:8], scalar=1.0, in1=ng[:], op0=ALU.add, op1=ALU.mult),
        reads=['bank6', 'ng'], writes=['s1'])
    k.op('dve', lambda e: e.tensor_copy(out=s2[:], in_=colb[:, 8:16]), reads=['bank6'], writes=['s2'])
    if need_gate:
        for hh in range(2):
            k.op('pe', lambda e, hh=hh: e.matmul(
                banks[hh][:, :], lhsT=one11[0:1, 0:128], rhs=modrow[0:1, 2048 + hh * 512:2048 + (hh + 1) * 512],
                start=True, stop=True), reads=['modrow', 'one11'], writes=[f"bank{hh}"])
            k.op('dve', lambda e, hh=hh: e.tensor_copy(out=gate_bc[:, hh * 512:(hh + 1) * 512], in_=banks[hh][:, :]),
                 reads=[f"bank{hh}"], writes=['gate_bc'])
    return s1, s2, gate_bc


def emit_norm_tile(cx, xt, xkey, tb, s1, s2, ident, tp, tpkey, hT, hkey, tagn):
    k = cx.k
    i = cx.n
    cx.n += 1
    r = i % 2
    if not hasattr(cx, 'nrm'):
        cx.nrm = dict(
            sq=[cx.sb("nsq", [128, 1024], BF16)] * 2,
            st=[cx.sb(f"nst{j}", [128, 4], F32) for j in range(2)],
            xn=[cx.sb(f"nxn{j}", [128, 1024], BF16) for j in range(2)],
        )
    sq, stt, xn = cx.nrm['sq'][r], cx.nrm['st'][r], cx.nrm['xn'][r]
    ksq, kst, kxn = "nsq", f"nst{r}", f"nxn{r}"
    k.op('pool', lambda e: e.memset(stt[:], 0.0), writes=[kst])
    k.op('act', lambda e: e.activation(out=sq[:], in_=xt, func=AF.Square, accum_out=stt[:, 0:1]),
         reads=[xkey, kst], writes=[ksq, kst])
    k.op('act', lambda e: e.activation(out=stt[:, 1:2], in_=stt[:, 0:1], func=AF.Ln, scale=1.0 / D, bias=EPS),
         reads=[kst], writes=[kst])
    k.op('act', lambda e: e.activation(out=stt[:, 2:3], in_=stt[:, 1:2], func=AF.Exp, scale=-0.5),
         reads=[kst], writes=[kst])
    k.op('dve', lambda e: e.tensor_scalar(out=xn[:], in0=xt, scalar1=stt[:, 2:3], scalar2=None, op0=ALU.mult),
         reads=[xkey, kst], writes=[kxn])
    for kc in range(8):
        k.op('pe', lambda e, kc=kc: e.transpose(tp[:, kc * 128:(kc + 1) * 128], xn[:, kc * 128:(kc + 1) * 128], ident[:]),
             reads=[kxn, 'ident'], writes=[tpkey])
    for kc in range(8):
        eng = 'dve' if kc % 2 == 0 else 'act'
        if eng == 'dve':
            k.op('dve', lambda e, kc=kc: e.tensor_scalar(
                out=hT[:, kc, tb * 128:(tb + 1) * 128], in0=tp[:, kc * 128:(kc + 1) * 128],
                scalar1=s1[:, kc:kc + 1], scalar2=s2[:, kc:kc + 1], op0=ALU.mult, op1=ALU.add),
                reads=[tpkey, 's1', 's2'], writes=[hkey])
        else:
            k.op('act', lambda e, kc=kc: e.activation(
                out=hT[:, kc, tb * 128:(tb + 1) * 128], in_=tp[:, kc * 128:(kc + 1) * 128],
                func=AF.Identity, scale=s1[:, kc:kc + 1], bias=s2[:, kc:kc + 1]),
                reads=[tpkey, 's1', 's2'], writes=[hkey])


NTB = 2048


def build_B(last):
    nc = bass.Bass("TRN2", target_bir_lowering=False)
    dt_in = lambda name, shape, dt=F32: nc.dram_tensor(name, list(shape), dt, kind="ExternalInput").ap()
    x_d = dt_in("x", [NTB, D])
    yT_d = dt_in("yT", [D, NTB], BF16)
    cT_d = dt_in("cT", [128, 8])
    ng_d = dt_in("ng", [128, 8])
    wada_d = dt_in("w_ada", [D, 3 * D])
    bada_d = dt_in("b_ada", [1, 3 * D])
    wg_d = dt_in("wg", [D, 3 * D])
    gb_d = dt_in("gb", [128, 24])
    wbr_d = dt_in("wbr", [D, D])
    wout_d = dt_in("wout", [D, D])
    fg_d = dt_in("fg", [1, D])
    id_d = dt_in("ident", [128, 128])
    xo_d = nc.dram_tensor("xo", [NTB, D], F32, kind="ExternalOutput").ap()
    with contextlib.ExitStack() as st:
        cx = Ctx(nc, st)
        emit_B(cx, last, x_d, yT_d, cT_d, ng_d, wada_d, bada_d, wg_d, gb_d, wbr_d, wout_d, fg_d, id_d, xo_d)
        cx.k.wait_all('sp', cx.outkeys)
        cx.k.emit()
    return nc


def emit_B(cx, last, x_d, yT_d, cT_d, ng_d, wada_d, bada_d, wg_d, gb_d, wbr_d, wout_d, fg_d, id_d, xo_d):
    k = cx.k
    nc = cx.nc
    banks = [cx.ps(f"bank{i}", [128, 512], F32) for i in range(7)]
    tp = cx.ps("tpb", [128, 1024], BF16)
    ident = cx.sb("ident", [128, 128], BF16)
    k.dma('pool', ident[:], id_d, writes=['ident'])
    s1, s2, gate_bc = emit_mod(cx, wada_d, bada_d, cT_d, ng_d, banks, True)
    wg = cx.sb("wg", [128, 8, 3 * D], BF16)
    wbr = cx.sb("wbr", [128, 8, D], BF16)
    wout = cx.sb("wout", [128, 8, D], BF16)
    gb = cx.sb("gb", [128, 24], F32)
    k.dma('sp', gb[:], gb_d, writes=['gb'])
    for kc in range(8):
        k.dma('pool', wg[:, kc, :], wg_d[kc * 128:(kc + 1) * 128, :], writes=['wg'])
    for kc in range(8):
        k.dma('pool', wbr[:, kc, :], wbr_d[kc * 128:(kc + 1) * 128, :], writes=['wbr'])
    for kc in range(8):
        k.dma('pool', wout[:, kc, :], wout_d[kc * 128:(kc + 1) * 128, :], writes=['wout'])
    if last:
        fg_bc = cx.sb("fg_bc", [128, D], F32)
        k.dma('sp', fg_bc[:], fg_d.partition_broadcast(128), writes=['fg_bc'])
    xres = cx.sb("xres", [128, 4, D], F32)
    hT = [cx.sb(f"hT{i}", [128, 8, 512], BF16) for i in range(2)]
    yT = [cx.sb(f"yT{i}", [128, 8, 512], BF16) for i in range(2)]
    mT = cx.sb("mT", [128, 8, 512], BF16)
    sig = [cx.sb(f"sig{i}", [128, 512], F32) for i in range(3)]
    tmp = [cx.sb(f"tmp{i}", [128, 512], F32) for i in range(3)]
    xn_o = [cx.sb(f"xno{i}", [128, D], F32) for i in range(2)]
    fst = [cx.sb(f"fst{i}", [128, 4], F32) for i in range(2)]
    yo = [cx.sb(f"yo{i}", [128, D], F32) for i in range(2)]
    fsq = cx.sb("fsq", [128, D], BF16)
    yT_v = yT_d.rearrange("(k p) t -> p k t", p=128)
    nG = 0
    nP = 0
    nO = 0
    ntile = NTB // 512
    for tl in range(ntile):
        r = tl % 2
        k.dma('sp', yT[r][:], yT_v[:, :, tl * 512:(tl + 1) * 512], writes=[f"yT{r}"])
        for tb in range(4):
            t0 = tl * 512 + tb * 128
            k.dma('sp', xres[:, tb, :], x_d[t0:t0 + 128, :], writes=[f"xres{tb}"])
            emit_norm_tile(cx, xres[:, tb, :], f"xres{tb}", tb, s1, s2, ident, tp, 'tpb', hT[r], f"hT{r}", 'b')
        for dc in range(8):
            for gi in range(3):
                gbk = 0 + (nG % 2)
                nG += 1
                for kc in range(8):
                    k.op('pe', lambda e, gbk=gbk, gi=gi, kc=kc, dc=dc, r=r: e.matmul(
                        banks[gbk][:, :], lhsT=wg[:, kc, gi * D + dc * 128:gi * D + (dc + 1) * 128],
                        rhs=hT[r][:, kc, :], start=(kc == 0), stop=(kc == 7)),
                        reads=['wg', f"hT{r}"], writes=[f"bank{gbk}"])
                k.op('act', lambda e, gbk=gbk, gi=gi, dc=dc: e.activation(
                    out=sig[gi][:], in_=banks[gbk][:, :], func=AF.Sigmoid,
                    bias=gb[:, gi * 8 + dc:gi * 8 + dc + 1], scale=1.0),
                    reads=[f"bank{gbk}", 'gb'], writes=[f"sig{gi}"])
            for bi, (k0, k1) in enumerate(((0, 4), (4, 6), (6, 8))):
                pbk = 2 + (nP % 2)
                nP += 1
                for kc in range(k0, k1):
                    k.op('pe', lambda e, pbk=pbk, kc=kc, dc=dc, r=r, k0=k0, k1=k1: e.matmul(
                        banks[pbk][:, :], lhsT=wbr[:, kc, dc * 128:(dc + 1) * 128],
                        rhs=yT[r][:, kc, :], start=(kc == k0), stop=(kc == k1 - 1)),
                        reads=['wbr', f"yT{r}"], writes=[f"bank{pbk}"])
                k.op('dve', lambda e, pbk=pbk, bi=bi: e.tensor_tensor(
                    out=tmp[bi][:], in0=banks[pbk][:, :], in1=sig[bi][:], op=ALU.mult),
                    reads=[f"bank{pbk}", f"sig{bi}"], writes=[f"tmp{bi}"])
            k.op('pool', lambda e: e.tensor_tensor(out=tmp[0][:], in0=tmp[0][:], in1=tmp[1][:], op=ALU.add),
                 reads=['tmp0', 'tmp1'], writes=['tmp0'])
            k.op('pool', lambda e, dc=dc: e.tensor_tensor(out=mT[:, dc, :], in0=tmp[0][:], in1=tmp[2][:], op=ALU.add),
                 reads=['tmp0', 'tmp2'], writes=['mT'])
        for tb in range(4):
            t0 = tl * 512 + tb * 128
            ro = nO % 2
            nO += 1
            for ch in range(2):
                obk = 4 + ch
                for kc in range(8):
                    k.op('pe', lambda e, obk=obk, kc=kc, tb=tb, ch=ch: e.matmul(
                        banks[obk][:, :], lhsT=mT[:, kc, tb * 128:(tb + 1) * 128],
                        rhs=wout[:, kc, ch * 512:(ch + 1) * 512], start=(kc == 0), stop=(kc == 7)),
                        reads=['mT', 'wout'], writes=[f"bank{obk}"])
                k.op('dve', lambda e, obk=obk, ch=ch, ro=ro: e.tensor_tensor(
                    out=xn_o[ro][:, ch * 512:(ch + 1) * 512], in0=banks[obk][:, :],
                    in1=gate_bc[:, ch * 512:(ch + 1) * 512], op=ALU.mult),
                    reads=[f"bank{obk}", 'gate_bc'], writes=[f"xno{ro}"])
            k.op('pool', lambda e, ro=ro, tb=tb: e.tensor_tensor(
                out=xn_o[ro][:], in0=xn_o[ro][:], in1=xres[:, tb, :], op=ALU.add),
                reads=[f"xno{ro}", f"xres{tb}"], writes=[f"xno{ro}"])
            if not last:
                k.dma('sp', xo_d[t0:t0 + 128, :], xn_o[ro][:], reads=[f"xno{ro}"], writes=[f"xo{t0}"])
                cx.outkeys.append(f"xo{t0}")
            else:
                k.op('pool', lambda e, ro=ro: e.memset(fst[ro][:], 0.0), writes=[f"fst{ro}"])
                k.op('act', lambda e, ro=ro: e.activation(out=fsq[:], in_=xn_o[ro][:], func=AF.Square,
                                                           accum_out=fst[ro][:, 0:1]),
                     reads=[f"xno{ro}", f"fst{ro}"], writes=['fsq', f"fst{ro}"])
                k.op('act', lambda e, ro=ro: e.activation(out=fst[ro][:, 1:2], in_=fst[ro][:, 0:1], func=AF.Ln,
                                                           scale=1.0 / D, bias=EPS),
                     reads=[f"fst{ro}"], writes=[f"fst{ro}"])
                k.op('act', lambda e, ro=ro: e.activation(out=fst[ro][:, 2:3], in_=fst[ro][:, 1:2], func=AF.Exp, scale=-0.5),
                     reads=[f"fst{ro}"], writes=[f"fst{ro}"])
                k.op('dve', lambda e, ro=ro: e.scalar_tensor_tensor(
                    out=yo[ro][:], in0=xn_o[ro][:], scalar=fst[ro][:, 2:3], in1=fg_bc[:],
                    op0=ALU.mult, op1=ALU.mult), reads=[f"xno{ro}", f"fst{ro}", 'fg_bc'], writes=[f"yo{ro}"])
                k.dma('sp', xo_d[t0:t0 + 128, :], yo[ro][:], reads=[f"yo{ro}"], writes=[f"xo{t0}"])
                cx.outkeys.append(f"xo{t0}")


def build_A(ntile=16):
    nc = bass.Bass("TRN2", target_bir_lowering=False)
    dt_in = lambda name, shape, dt=F32: nc.dram_tensor(name, list(shape), dt, kind="ExternalInput").ap()
    a = dict(
        x=dt_in("x", [SEQ, D]), cT=dt_in("cT", [128, 8]), ng=dt_in("ng", [128, 8]),
        w_ada=dt_in("w_ada", [D, 3 * D]), b_ada=dt_in("b_ada", [1, 3 * D]),
        wtm=dt_in("wtm", [D, 512]), wgt=dt_in("wgt", [D, 16]), wfm=dt_in("wfm", [D, 512]),
        mgb=dt_in("mgb", [128, 2]), cw=dt_in("cw", [128, 8]), cb=dt_in("cb", [128, 2]),
        mng=dt_in("mng", [1, 128]), poolw=dt_in("poolw", [64, 64]), pscale=dt_in("pscale", [64, 1]),
        bands=dt_in("bands", [3, 128, 128]), mtri=dt_in("mtri", [128, 128]),
        sbm=dt_in("sbm", [4, 128, 512]), nui=dt_in("nui", [128, 128]), ident=dt_in("ident", [128, 128]),
    )
    yT_o = nc.dram_tensor("yTo", [256, SEQ], BF16, kind="ExternalOutput").ap()
    with contextlib.ExitStack() as st:
        cx = Ctx(nc, st)
        emit_A(cx, a, yT_o, ntile)
        cx.k.wait_all('sp', cx.outkeys)
        cx.k.emit()
    return nc


def emit_A(cx, a, yT_o, ntile):
    import os
    STAGE = int(os.environ.get('A_STAGE', '9'))
    SUB = int(os.environ.get('A_SUB', '9'))
    DIS = os.environ.get('A_DIS', '')
    k = cx.k
    banks = [cx.ps(f"bank{i}", [128, 512], F32) for i in range(7)]
    tp = cx.ps("tpb", [128, 1024], BF16)
    ZB = (0, 1)
    LB = (2, 3)
    OB = 4
    PB = 5
    MB = 6
    ident = cx.sb("ident", [128, 128], BF16)
    k.dma('pool', ident[:], a['ident'], writes=['ident'])
    s1, s2, _ = emit_mod(cx, a['w_ada'], a['b_ada'], a['cT'], a['ng'], banks, False)
    wtm = cx.sb("wtm", [128, 8, 512], BF16)
    wfm = cx.sb("wfm", [128, 8, 512], BF16)
    wgt = cx.sb("wgt", [128, 8, 16], BF16)
    for kc in range(8):
        k.dma('pool', wtm[:, kc, :], a['wtm'][kc * 128:(kc + 1) * 128, :], writes=['wtm'])
        k.dma('pool', wfm[:, kc, :], a['wfm'][kc * 128:(kc + 1) * 128, :], writes=['wfm'])
        k.dma('pool', wgt[:, kc, :], a['wgt'][kc * 128:(kc + 1) * 128, :], writes=['wgt'])
    mgb = cx.sb("mgb", [128, 2], F32)
    nmgb = cx.sb("nmgb", [128, 2], F32)
    cw = cx.sb("cw", [128, 8], F32)
    cb = cx.sb("cb", [128, 2], F32)
    mng = cx.sb("mng", [128, 128], F32)
    poolw = cx.sb("poolw", [64, 64], BF16)
    pscale = cx.sb("pscale", [64, 1], F32)
    bands = cx.sb("bands", [128, 3, 128], BF16)
    mtri = cx.sb("mtri", [128, 128], F32)
    onesf = cx.sb("onesf", [128, 128], F32)
    sbm = cx.sb("sbm", [128, 4, 512], BF16)
    nui = cx.sb("nui", [128, 128], BF16)
    nones = cx.sb("nones", [128, 128], BF16)
    k.dma('sp', mgb[:], a['mgb'], writes=['mgb'])
    k.dma('sp', cw[:], a['cw'], writes=['cw'])
    k.dma('sp', cb[:], a['cb'], writes=['cb'])
    k.dma('sp', mng[:], a['mng'].partition_broadcast(128), writes=['mng'])
    k.dma('pool', poolw[:], a['poolw'], writes=['poolw'])
    k.dma('sp', pscale[:], a['pscale'], writes=['pscale'])
    for i in range(3):
        k.dma('pool', bands[:, i, :], a['bands'][i], writes=['bands'])
    k.dma('sp', mtri[:], a['mtri'], writes=['mtri'])
    for i in range(4):
        k.dma('pool', sbm[:, i, :], a['sbm'][i], writes=['sbm'])
    k.dma('pool', nui[:], a['nui'], writes=['nui'])
    k.op('dve', lambda e: e.memset(onesf[:], 1.0), writes=['onesf'])
    k.op('dve', lambda e: e.memset(nones[:], -1.0), writes=['nones'])
    k.op('dve', lambda e: e.tensor_scalar(out=nmgb[:], in0=mgb[:], scalar1=-1.0, scalar2=None, op0=ALU.mult),
         reads=['mgb'], writes=['nmgb'])
    sqT = cx.sb("sqT", [64, SEQ], BF16)
    skT = cx.sb("skT", [64, SEQ], BF16)
    SV = cx.sb("SV", [128, SEQ // 128, 64], BF16)
    Cn32 = cx.sb("Cn32", [128, 132], F32)
    Cnb = cx.sb("Cnb", [128, 132], BF16)
    k.op('dve', lambda e: e.memset(Cn32[:], 0.0), writes=['Cn32'])
    k.op('dve', lambda e: e.memset(Cnb[:], 0.0), writes=['Cnb'])
    xt = [cx.sb(f"xt{i}", [128, D], F32) for i in range(2)]
    hT = [cx.sb(f"hT{i}", [128, 8, 512], BF16) for i in range(2)]
    qkr = [cx.sb(f"qkr{i}", [128, 516], F32) for i in range(2)]
    for g in range(2):
        k.op('pool', lambda e, g=g: e.memset(qkr[g][:], 0.0), writes=[f"qkr{g}"])
    cacc = [cx.sb(f"cacc{i}", [128, 512], F32) for i in range(2)]
    csg = [cx.sb(f"csg{i}", [128, 512], F32) for i in range(2)]
    qT = cx.sb("qT", [128, 512], BF16)
    kT = cx.sb("kT", [128, 512], BF16)
    spz = cx.sb("spz", [64, 512], F32)
    ssz = cx.sb("ssz", [64, 512], F32)
    sgz = cx.sb("sgz", [64, 512], F32)
    Ub = [cx.sb(f"Ub{i}", [128, 64], BF16) for i in range(3)]
    tmS = [cx.sb(f"tmS{i}", [128, 512], F32) for i in range(2)]
    gsb = [cx.sb(f"gsb{i}", [128, 16], F32) for i in range(2)]
    sgo = [cx.sb(f"sgo{i}", [128, 256], F32) for i in range(2)]
    gz = [cx.sb(f"gz{i}", [128, 128], F32) for i in range(2)]
    V2 = [cx.sb(f"V2{i}", [128, 132], BF16) for i in range(2)]
    Ktm = [cx.sb(f"Ktm{i}", [128, 128], BF16) for i in range(2)]
    for i in range(2):
        k.op('pool', lambda e, i=i: e.memset(V2[i][:], 0.0), writes=[f"V2{i}"])
    Sm = [cx.sb(f"Sm{i}", [128, 128], BF16) for i in range(2)]
    t1 = [cx.sb(f"t1{i}", [128, 128], F32) for i in range(2)]
    t1sq = cx.sb("t1sq", [128, 128], BF16)
    ymb = [cx.sb(f"ymb{i}", [128, 128], BF16) for i in range(2)]
    ymT = [cx.sb(f"ymT{i}", [128, 512], BF16) for i in range(2)]
    pTs = cx.sb("pTs", [64, 512], BF16)
    ypT = [cx.sb(f"ypT{i}", [64, 512], BF16) for i in range(2)]
    ysT = [cx.sb(f"ysT{i}", [64, 512], BF16) for i in range(2)]
    Eb = [cx.sb(f"Eb{i}", [128, 512], F32) for i in range(2)]
    L32 = cx.sb("L32", [128, 512], F32)
    Lb = [cx.sb(f"Lb{i}", [128, 512], BF16) for i in range(2)]
    S32 = cx.sb("S32", [128, 512], F32)
    Sb = [cx.sb(f"Sb{i}", [128, 512], BF16) for i in range(2)]
    At = [cx.sb(f"At{i}", [128, 512], BF16) for i in range(2)]
    Am = [cx.sb(f"Am{i}", [128, 512], BF16) for i in range(2)]
    mb = banks[MB]
    cnt = dict(z=0, l=0, u=0, g=0, p=0)
    PR = [PB, 0, 1, 2, 3]

    def nextpb():
        i = PR[cnt['p'] % len(PR)]
        cnt['p'] += 1
        return banks[i], f"bank{i}"
    KSCALE = 128.0 ** -0.5
    nx = 0
    for tl in range(ntile if STAGE >= 2 else 0):
        r = tl % 2
        for tb in range(4):
            t0 = tl * 512 + tb * 128
            xr = nx % 2
            nx += 1
            k.dma('sp', xt[xr][:], a['x'][t0:t0 + 128, :], writes=[f"xt{xr}"])
            emit_norm_tile(cx, xt[xr][:], f"xt{xr}", tb, s1, s2, ident, tp, 'tpb', hT[r], f"hT{r}", 'a')
        hk = f"hT{r}"
        for g in range(2):
            pb, pk = nextpb()
            for kc in range(8):
                k.op('pe', lambda e, g=g, kc=kc, r=r: e.matmul(
                    pb[:, :], lhsT=wfm[:, kc, g * 128:(g + 1) * 128], rhs=hT[r][:, kc, :],
                    start=(kc == 0), stop=(kc == 7)), reads=['wfm', hk], writes=[pk])
            k.op('pool', lambda e, g=g: e.tensor_copy(out=qkr[g][:, 0:3], in_=qkr[g][:, 512:515]),
                 reads=[f"qkr{g}"], writes=[f"qkr{g}h"])
            k.op('act', lambda e, g=g: e.activation(out=qkr[g][:, 3:515], in_=pb[:, :], func=AF.Identity),
                 reads=[pk, f"qkr{g}h"], writes=[f"qkr{g}"])
            k.op('pool', lambda e, g=g: e.tensor_scalar(
                out=cacc[g][:], in0=qkr[g][:, 0:512], scalar1=cw[:, 4 * g:4 * g + 1], scalar2=cb[:, g:g + 1],
                op0=ALU.mult, op1=ALU.add), reads=[f"qkr{g}", f"qkr{g}h", 'cw', 'cb'], writes=[f"cacc{g}"])
            for j in range(1, 4):
                k.op('dve', lambda e, g=g, j=j: e.scalar_tensor_tensor(
                    out=cacc[g][:], in0=qkr[g][:, j:j + 512], scalar=cw[:, 4 * g + j:4 * g + j + 1],
                    in1=cacc[g][:], op0=ALU.mult, op1=ALU.add),
                    reads=[f"qkr{g}", f"qkr{g}h", f"cacc{g}", 'cw'], writes=[f"cacc{g}"])
            k.op('act', lambda e, g=g: e.activation(out=csg[g][:], in_=cacc[g][:], func=AF.Sigmoid),
                 reads=[f"cacc{g}"], writes=[f"csg{g}"])
            dst, dk, scl = (qT, 'qT', 1.0) if g == 0 else (kT, 'kT', KSCALE)
            k.op('dve', lambda e, g=g, dst=dst, scl=scl: e.scalar_tensor_tensor(
                out=dst[:], in0=cacc[g][:], scalar=scl, in1=csg[g][:], op0=ALU.mult, op1=ALU.mult),
                reads=[f"cacc{g}", f"csg{g}"], writes=[dk])
        for i4, nm in enumerate(('pz', 'sz', 'sq', 'sk')):
            c0 = 256 + i4 * 64
            pb, pk = nextpb()
            for kc in range(8):
                k.op('pe', lambda e, kc=kc, r=r, c0=c0: e.matmul(
                    pb[0:64, :], lhsT=wfm[:, kc, c0:c0 + 64], rhs=hT[r][:, kc, :],
                    start=(kc == 0), stop=(kc == 7)), reads=['wfm', hk], writes=[pk])
            if nm in ('pz', 'sz'):
                dst, dk = (spz, 'spz') if nm == 'pz' else (ssz, 'ssz')
                k.op('act', lambda e: e.activation(out=sgz[:], in_=pb[0:64, :], func=AF.Sigmoid),
                     reads=[pk], writes=['sgz'])
                k.op('dve', lambda e, dst=dst: e.tensor_tensor(out=dst[:], in0=pb[0:64, :], in1=sgz[:], op=ALU.mult),
                     reads=[pk, 'sgz'], writes=[dk])
            elif nm == 'sq':
                k.op('act', lambda e, tl=tl: e.activation(out=sqT[:, tl * 512:(tl + 1) * 512], in_=pb[0:64, :],
                                                           func=AF.Identity, scale=0.125),
                     reads=[pk], writes=[f"sqT{tl}"])
            else:
                k.op('dve', lambda e, tl=tl: e.tensor_copy(out=skT[:, tl * 512:(tl + 1) * 512], in_=pb[0:64, :]),
                     reads=[pk], writes=[f"skT{tl}"])
        if STAGE < 3:
            continue
        for tb in range(4):
            n = tl * 4 + tb
            tsl = slice(tb * 128, (tb + 1) * 128)
            pb, pk = nextpb()
            for kc in range(8):
                k.op('pe', lambda e, kc=kc, r=r, tsl=tsl: e.matmul(
                    pb[:, :], lhsT=hT[r][:, kc, tsl], rhs=wtm[:, kc, :], start=(kc == 0), stop=(kc == 7)),
                    reads=['wtm', hk], writes=[pk])
            for kc in range(8):
                k.op('pe', lambda e, kc=kc, r=r, tsl=tsl: e.matmul(
                    mb[:, 400:416], lhsT=hT[r][:, kc, tsl], rhs=wgt[:, kc, :], start=(kc == 0), stop=(kc == 7)),
                    reads=['wgt', hk], writes=['bank6'])
            gi = cnt['g'] % 2
            cnt['g'] += 1
            G = gsb[gi]
            gk = f"gsb{gi}"
            ts = tmS[n % 2]
            tk = f"tmS{n % 2}"
            k.op('act', lambda e, ts=ts, pb=pb: e.activation(out=ts[:], in_=pb[:, :], func=AF.Identity),
                 reads=[pk], writes=[tk])
            k.op('pool', lambda e, n=n, ts=ts: e.tensor_copy(out=SV[:, n, :], in_=ts[:, 448:512]),
                 reads=[tk], writes=[f"SV{n}"])
            ui = n % 3
            k.op('pool', lambda e, ui=ui, ts=ts: e.tensor_copy(out=Ub[ui][:], in_=ts[:, 384:448]),
                 reads=[tk], writes=[f"Ub{ui}"])
            k.op('act', lambda e, gi=gi, ts=ts: e.activation(out=sgo[gi][:], in_=ts[:, 128:384], func=AF.Sigmoid),
                 reads=[tk], writes=[f"sgo{gi}"])
            k.op('dve', lambda e, gi=gi, ts=ts: e.tensor_tensor(out=gz[gi][:], in0=ts[:, 256:384], in1=sgo[gi][:, 128:256], op=ALU.mult),
                 reads=[tk, f"sgo{gi}"], writes=[f"gz{gi}"])
            k.op('pool', lambda e, gi=gi: e.tensor_tensor(out=gz[gi][:], in0=gz[gi][:], in1=mng[:], op=ALU.mult),
                 reads=[f"gz{gi}", 'mng'], writes=[f"gz{gi}"])
            if SUB < 1:
                continue
            k.op('act', lambda e, G=G: e.activation(out=G[:, 0:2], in_=mb[:, 400:402], func=AF.Identity),
                 reads=['bank6'], writes=[gk])
            k.op('act', lambda e, G=G: e.activation(out=G[:, 2:3], in_=G[:, 1:2], func=AF.Exp, scale=-1.0, bias=nmgb[:, 1:2]),
                 reads=[gk, 'nmgb'], writes=[gk])
            k.op('act', lambda e, G=G: e.activation(out=G[:, 3:4], in_=G[:, 2:3], func=AF.Ln, scale=1.0, bias=1.0),
                 reads=[gk], writes=[gk])
            k.op('pe', lambda e, G=G: e.matmul(mb[:, 404:405], lhsT=mtri[:], rhs=G[:, 3:4], start=True, stop=True),
                 reads=[gk, 'mtri'], writes=['bank6'])
            k.op('pe', lambda e, G=G: e.matmul(mb[:, 405:406], lhsT=onesf[:], rhs=G[:, 3:4], start=True, stop=True),
                 reads=[gk, 'onesf'], writes=['bank6'])
            k.op('act', lambda e, G=G: e.activation(out=G[:, 4:6], in_=mb[:, 404:406], func=AF.Exp, scale=-1.0),
                 reads=['bank6'], writes=[gk])
            k.op('dve', lambda e, G=G: e.tensor_tensor(out=G[:, 6:7], in0=mb[:, 404:405], in1=G[:, 0:1], op=ALU.add),
                 reads=['bank6', gk], writes=[gk])
            k.op('act', lambda e, G=G: e.activation(out=G[:, 7:8], in_=G[:, 6:7], func=AF.Exp, scale=1.0, bias=mgb[:, 0:1]),
                 reads=[gk, 'mgb'], writes=[gk])
            if SUB < 2:
                continue
            vi = n % 2
            k.op('dve', lambda e, vi=vi, G=G, ts=ts: e.tensor_scalar(out=V2[vi][:, 0:128], in0=ts[:, 0:128], scalar1=G[:, 7:8],
                                                              scalar2=None, op0=ALU.mult),
                 reads=[tk, gk], writes=[f"V2{vi}"])
            k.op('dve', lambda e, vi=vi, G=G: e.tensor_copy(out=V2[vi][:, 128:129], in_=G[:, 7:8]),
                 reads=[gk], writes=[f"V2{vi}"])
            k.op('pe', lambda e, tsl=tsl: e.transpose(tp[:, 0:128], kT[:, tsl], ident[:]),
                 reads=['kT', 'ident'], writes=['tpb'])
            k.op('dve', lambda e, vi=vi: e.tensor_copy(out=Ktm[vi][:], in_=tp[:, 0:128]),
                 reads=['tpb'], writes=[f"Ktm{vi}"])
            k.op('pe', lambda e, tsl=tsl: e.matmul(mb[:, 0:128], lhsT=kT[:, tsl], rhs=qT[:, tsl], start=True, stop=True),
                 reads=['kT', 'qT'], writes=['bank6'])
            k.op('dve', lambda e, vi=vi: e.tensor_tensor(out=Sm[vi][:], in0=mb[:, 0:128], in1=mtri[:], op=ALU.mult),
                 reads=['bank6', 'mtri'], writes=[f"Sm{vi}"])
            if SUB < 3:
                continue
            k.op('pe', lambda e, tsl=tsl: e.matmul(mb[:, 128:258], lhsT=qT[:, tsl], rhs=Cnb[:, 0:130], start=True, stop=False),
                 reads=['qT', 'Cnb'], writes=['bank6'])
            k.op('pe', lambda e, vi=vi: e.matmul(mb[:, 128:258], lhsT=Sm[vi][:], rhs=V2[vi][:, 0:130], start=False, stop=True),
                 reads=[f"Sm{vi}", f"V2{vi}"], writes=['bank6'])
            k.op('pe', lambda e, vi=vi: e.matmul(mb[:, 260:390], lhsT=Ktm[vi][:], rhs=V2[vi][:, 0:130], start=True, stop=True),
                 reads=[f"Ktm{vi}", f"V2{vi}"], writes=['bank6'])
            k.op('dve', lambda e: e.tensor_tensor(out=Cn32[:, 0:130], in0=mb[:, 260:390], in1=Cn32[:, 0:130], op=ALU.add),
                 reads=['bank6', 'Cn32'], writes=['Cn32'])
            k.op('dve', lambda e, G=G: e.tensor_scalar(out=Cn32[:, 0:130], in0=Cn32[:, 0:130], scalar1=G[:, 5:6],
                                                       scalar2=None, op0=ALU.mult),
                 reads=['Cn32', gk], writes=['Cn32'])
            k.op('pool', lambda e: e.tensor_copy(out=Cnb[:, 0:130], in_=Cn32[:, 0:130]),
                 reads=['Cn32'], writes=['Cnb'])
            if SUB < 4:
                continue
            k.op('dve', lambda e, G=G: e.tensor_tensor(out=G[:, 8:9], in0=mb[:, 256:257], in1=G[:, 4:5], op=ALU.mult),
                 reads=['bank6', gk], writes=[gk])
            k.op('dve', lambda e, G=G: e.tensor_tensor(out=G[:, 8:9], in0=G[:, 8:9], in1=G[:, 8:9], op=ALU.mult),
                 reads=[gk], writes=[gk])
            k.op('dve', lambda e, G=G: e.tensor_scalar(out=G[:, 8:9], in0=G[:, 8:9], scalar1=1.0, scalar2=None, op0=ALU.max),
                 reads=[gk], writes=[gk])
            k.op('act', lambda e, G=G: e.activation(out=G[:, 14:15], in_=G[:, 8:9], func=AF.Ln),
                 reads=[gk], writes=[gk])
            k.op('act', lambda e, G=G: e.activation(out=G[:, 9:10], in_=G[:, 14:15], func=AF.Exp, scale=-0.5),
                 reads=[gk], writes=[gk])
            k.op('dve', lambda e, G=G: e.tensor_tensor(out=G[:, 10:11], in0=G[:, 9:10], in1=G[:, 4:5], op=ALU.mult),
                 reads=[gk], writes=[gk])
            k.op('dve', lambda e, G=G, gi=gi: e.scalar_tensor_tensor(
                out=t1[gi][:], in0=mb[:, 128:256], scalar=G[:, 10:11], in1=sgo[gi][:, 0:128], op0=ALU.mult, op1=ALU.mult),
                reads=['bank6', gk, f"sgo{gi}"], writes=[f"t1{gi}"])
            k.op('pool', lambda e, G=G: e.memset(G[:, 11:12], 0.0), reads=[], writes=[gk + 'a'])
            k.op('act', lambda e, G=G, gi=gi: e.activation(out=t1sq[:], in_=t1[gi][:], func=AF.Square, accum_out=G[:, 11:12]),
                 reads=[f"t1{gi}", gk + 'a'], writes=['t1sq', gk + 'a'])
            k.op('act', lambda e, G=G: e.activation(out=G[:, 12:13], in_=G[:, 11:12], func=AF.Ln, scale=1.0 / 128, bias=EPS),
                 reads=[gk + 'a'], writes=[gk + 'b'])
            k.op('act', lambda e, G=G: e.activation(out=G[:, 13:14], in_=G[:, 12:13], func=AF.Exp, scale=-0.5),
                 reads=[gk + 'b'], writes=[gk + 'c'])
            k.op('dve', lambda e, G=G, gi=gi: e.scalar_tensor_tensor(
                out=ymb[gi][:], in0=t1[gi][:], scalar=G[:, 13:14], in1=gz[gi][:], op0=ALU.mult, op1=ALU.mult),
                reads=[f"t1{gi}", gk + 'c', f"gz{gi}"], writes=[f"ymb{gi}"])
            k.op('pe', lambda e, gi=gi: e.transpose(tp[:, 128:256], ymb[gi][:], ident[:]),
                 reads=[f"ymb{gi}", 'ident'], writes=['tpb'])
            k.op('act', lambda e, r=r, tsl=tsl: e.activation(out=ymT[r][:, tsl], in_=tp[:, 128:256], func=AF.Identity),
                 reads=['tpb'], writes=[f"ymT{r}"])
            if SUB < 5:
                continue
            bi = 0 if n == 0 else 1
            k.op('pe', lambda e, ui=ui, bi=bi, tsl=tsl, n=n: e.matmul(
                banks[OB][0:64, 0:128], lhsT=Ub[ui][:], rhs=bands[:, bi, :], start=True, stop=(n == 0)),
                reads=[f"Ub{ui}", 'bands'], writes=['bank4'])
            if n > 0:
                up = (n - 1) % 3
                k.op('pe', lambda e, up=up: e.matmul(
                    banks[OB][0:64, 0:128], lhsT=Ub[up][:], rhs=bands[:, 2, :], start=False, stop=True),
                    reads=[f"Ub{up}", 'bands'], writes=['bank4'])
            k.op('dve', lambda e, tsl=tsl: e.tensor_copy(out=pTs[:, tsl], in_=banks[OB][0:64, 0:128]),
                 reads=['bank4'], writes=['pTs'])
        if os.environ.get('A_SKIPPOST'):
            continue
        k.dma('sp', yT_o[0:128, tl * 512:(tl + 1) * 512], ymT[r][:], reads=[f"ymT{r}"], writes=[f"oym{tl}"])
        cx.outkeys.append(f"oym{tl}")
        pb, pk = nextpb()
        k.op('pe', lambda e: e.matmul(pb[0:64, :], lhsT=poolw[:], rhs=pTs[:], start=True, stop=True),
             reads=['poolw', 'pTs'], writes=[pk])
        k.op('dve', lambda e, r=r: e.scalar_tensor_tensor(
            out=ypT[r][:], in0=pb[0:64, :], scalar=pscale[:, 0:1], in1=spz[:], op0=ALU.mult, op1=ALU.mult),
            reads=[pk, 'pscale', 'spz'], writes=[f"ypT{r}"])
        k.dma('sp', yT_o[128:192, tl * 512:(tl + 1) * 512], ypT[r][:], reads=[f"ypT{r}"], writes=[f"oyp{tl}"])
        cx.outkeys.append(f"oyp{tl}")
        if STAGE < 4:
            continue
        top = 4 * tl + 3
        qsl = slice(tl * 512, (tl + 1) * 512)
        sq_keys = [f"sqT{tl}"]
        U = top + 1

        def st_Z(u):
            kb = top - u
            ci = u % 2
            zi = ZB[ci]
            zb = banks[zi]
            ksl = slice(kb * 128, (kb + 1) * 128)
            kkey = f"skT{kb // 4}"
            diag = kb >= 4 * tl
            k.op('pe', lambda e: e.matmul(zb[:, :], lhsT=skT[:, ksl], rhs=sqT[:, qsl], start=True, stop=True),
                 reads=[kkey] + sq_keys, writes=[f"bank{zi}"])
            k.op('act', lambda e: e.activation(out=Eb[ci][:], in_=zb[:, :], func=AF.Exp),
                 reads=[f"bank{zi}"], writes=[f"Eb{ci}"])
            if diag:
                k.op('act', lambda e: e.activation(out=L32[:], in_=Eb[ci][:], func=AF.Ln, scale=1.0, bias=1.0),
                     reads=[f"Eb{ci}"], writes=['L32'])
                k.op('dve', lambda e: e.tensor_tensor(out=Lb[ci][:], in0=L32[:], in1=sbm[:, kb - 4 * tl, :], op=ALU.mult),
                     reads=['L32', 'sbm'], writes=[f"Lb{ci}"])
            else:
                k.op('act', lambda e: e.activation(out=Lb[ci][:], in_=Eb[ci][:], func=AF.Ln, scale=1.0, bias=1.0),
                     reads=[f"Eb{ci}"], writes=[f"Lb{ci}"])
        def st_S(u, tl=tl, top=top):
            kb = top - u
            ci = u % 2
            if kb > 0:
                if kb == top:
                    k.op('pool', lambda e: e.tensor_copy(out=S32[:], in_=Lb[ci][:]), reads=[f"Lb{ci}"], writes=['S32'])
                else:
                    k.op('pool', lambda e: e.tensor_tensor(out=S32[:], in0=S32[:], in1=Lb[ci][:], op=ALU.add),
                         reads=['S32', f"Lb{ci}"], writes=['S32'])
                k.op('pool', lambda e: e.tensor_copy(out=Sb[1 - ci][:], in_=S32[:]), reads=['S32'], writes=[f"Sb{1 - ci}"])

        def st_L(u):
            kb = top - u
            ci = u % 2
            li = LB[ci]
            lb = banks[li]
            ksl = slice(kb * 128, (kb + 1) * 128)
            kkey = f"skT{kb // 4}"
            diag = kb >= 4 * tl
            k.op('pe', lambda e: e.matmul(lb[:, :], lhsT=skT[:, ksl], rhs=sqT[:, qsl], start=True, stop=False),
                 reads=[kkey] + sq_keys, writes=[f"bank{li}"])
            k.op('pe', lambda e: e.matmul(lb[:, :], lhsT=nui[:], rhs=Lb[ci][:], start=False, stop=(kb == top)),
                 reads=['nui', f"Lb{ci}"], writes=[f"bank{li}"])
            if kb != top:
                k.op('pe', lambda e: e.matmul(lb[:, :], lhsT=nones[:], rhs=Sb[ci][:], start=False, stop=True),
                     reads=['nones', f"Sb{ci}"], writes=[f"bank{li}"])
            k.op('act', lambda e: e.activation(out=At[ci][:], in_=lb[:, :], func=AF.Exp),
                 reads=[f"bank{li}"], writes=[f"At{ci}"])
            if diag:
                k.op('dve', lambda e: e.tensor_tensor(out=Am[ci][:], in0=At[ci][:], in1=sbm[:, kb - 4 * tl, :], op=ALU.mult),
                     reads=[f"At{ci}", 'sbm'], writes=[f"Am{ci}"])

        def st_V(u):
            kb = top - u
            ci = u % 2
            diag = kb >= 4 * tl
            asrc, akey = (Am[ci], f"Am{ci}") if diag else (At[ci], f"At{ci}")
            k.op('pe', lambda e: e.matmul(banks[OB][0:64, :], lhsT=SV[:, kb, :], rhs=asrc[:],
                                          start=(kb == top), stop=(kb == 0)),
                 reads=[f"SV{kb}", akey], writes=['bank4'])

        for step in range(U + 2):
            if step < U:
                st_Z(step)
            if 1 <= step <= U:
                st_L(step - 1)
            if step < U:
                st_S(step)
            if step >= 2:
                st_V(step - 2)
        k.op('dve', lambda e, r=r: e.tensor_tensor(out=ysT[r][:], in0=banks[OB][0:64, :], in1=ssz[:], op=ALU.mult),
             reads=['bank4', 'ssz'], writes=[f"ysT{r}"])
        k.dma('sp', yT_o[192:256, tl * 512:(tl + 1) * 512], ysT[r][:], reads=[f"ysT{r}"], writes=[f"oys{tl}"])
        cx.outkeys.append(f"oys{tl}")


def _consts():
    s = np.arange(128)
    mtri = (s[:, None] <= s[None, :]).astype(np.float32)
    nui = -(s[:, None] >= s[None, :]).astype(np.float32)
    t = np.arange(512)
    sbm = np.stack([((i * 128 + s)[:, None] < t[None, :]).astype(np.float32) for i in range(4)])
    return mtri, nui, sbm


def _bands(w):
    s = np.arange(128)
    out = np.zeros((3, 128, 128), np.float32)
    for t in range(128):
        lo = max(t + 1 - w, 0)
        out[0, lo:t + 1, t] += 1.0 / (t + 1 - lo)
        out[0, t, t] -= 1.0
        lo = max(t + 1 - w, 0)
        out[1, lo:t + 1, t] += 1.0 / w
        out[1, t, t] -= 1.0
        nprev = w - (t + 1)
        if nprev > 0:
            out[2, 128 - nprev:, t] += 1.0 / w
    return out


def hostA_inputs(inp, l, x_full):
    mtri, nui, sbm = _consts()
    ident = np.eye(128, dtype=np.float32)
    w_in = inp["w_in"][l]
    maps = []
    for core in range(8):
        b, hh = core // 4, core % 4
        c128 = slice(hh * 128, (hh + 1) * 128)
        c64 = slice(hh * 64, (hh + 1) * 64)
        off = dict(mq=0, mk=512, mv=1024, mi=1536, mf=1540, mo=1544, mz=2056, pu=2568, pz=2824,
                   sq=3080, sk=3336, sv=3592, sz=3848)
        col = lambda nm, sl: w_in[:, off[nm] + sl.start: off[nm] + sl.stop]
        wtm = np.concatenate([col('mv', c128), col('mo', c128), col('mz', c128), col('pu', c64), col('sv', c64)], 1)
        wgt = np.zeros((D, 16), np.float32)
        wgt[:, 0] = w_in[:, off['mi'] + hh]
        wgt[:, 1] = w_in[:, off['mf'] + hh]
        wfm = np.concatenate([col('mq', c128), col('mk', c128), col('pz', c64), col('sz', c64),
                              col('sq', c64), col('sk', c64)], 1)
        mg = inp["m_gate_b"][l]
        mgb = np.tile(np.array([[mg[hh], mg[4 + hh]]], np.float32), (128, 1))
        cwl = inp["conv_w"][l]
        cw = np.concatenate([cwl[:, c128].T, cwl[:, 512 + hh * 128: 512 + (hh + 1) * 128].T], 1)
        cbl = inp["conv_b"][l]
        cb = np.stack([cbl[c128], cbl[512 + hh * 128: 512 + (hh + 1) * 128]], 1)
        maps.append(dict(
            x=np.ascontiguousarray(x_full[b]),
            cT=np.ascontiguousarray(inp["c"][b].reshape(8, 128).T),
            ng=np.ascontiguousarray(inp["norm_g"][l].reshape(8, 128).T),
            w_ada=inp["w_ada"][l], b_ada=inp["b_ada"][l].reshape(1, -1),
            wtm=np.ascontiguousarray(wtm), wgt=np.ascontiguousarray(wgt), wfm=np.ascontiguousarray(wfm),
            mgb=mgb, cw=np.ascontiguousarray(cw), cb=np.ascontiguousarray(cb),
            mng=np.ascontiguousarray(inp["m_norm_g"][l][c128].reshape(1, 128)),
            poolw=np.ascontiguousarray(inp["pool_w"][l][hh]),
            pscale=np.ascontiguousarray(inp["pool_scale"][l][c64].reshape(64, 1)),
            bands=_bands(POOL_WINDOWS[hh]), mtri=mtri, sbm=sbm, nui=nui, ident=ident))
    return maps


def hostA_gather(results):
    out = np.zeros((NB, D, SEQ), dtype=ml_dtypes.bfloat16)
    for core in range(8):
        b, hh = core // 4, core % 4
        y = results[core]["yTo"]
        out[b, hh * 128:(hh + 1) * 128] = y[0:128]
        out[b, 512 + hh * 64:512 + (hh + 1) * 64] = y[128:192]
        out[b, 768 + hh * 64:768 + (hh + 1) * 64] = y[192:256]
    return out


def hostB_inputs(inp, l, x_full, yT_full):
    maps = []
    ident = np.eye(128, dtype=np.float32)
    wbr = np.concatenate([inp["w_br_m"][l], inp["w_br_p"][l], inp["w_br_s"][l]], 0)
    for core in range(8):
        b, j = core // 4, core % 4
        sl = slice(j * NTB, (j + 1) * NTB)
        maps.append(dict(
            x=np.ascontiguousarray(x_full[b, sl, :]),
            yT=np.ascontiguousarray(yT_full[b][:, sl]),
            cT=np.ascontiguousarray(inp["c"][b].reshape(8, 128).T),
            ng=np.ascontiguousarray(inp["norm_g"][l].reshape(8, 128).T),
            w_ada=inp["w_ada"][l], b_ada=inp["b_ada"][l].reshape(1, -1),
            wg=np.ascontiguousarray(inp["w_in"][l][:, 4104:]),
            gb=np.ascontiguousarray(inp["gate_b"][l].reshape(24, 128).T),
            wbr=wbr, wout=inp["w_out"][l], fg=inp["final_g"].reshape(1, -1), ident=ident))
    return maps


_NC_CACHE = {}


def _get_nc(name):
    if name not in _NC_CACHE:
        _NC_CACHE[name] = {'A': lambda: build_A(), 'B0': lambda: build_B(False), 'B1': lambda: build_B(True)}[name]()
    return _NC_CACHE[name]


def kernel(**inputs):
    inp = {k: np.asarray(v) for k, v in inputs.items()}
    x = inp["x"]
    cores = list(range(8))
    for l in range(DEPTH):
        resA = run_bass_kernel_spmd(_get_nc('A'), hostA_inputs(inp, l, x), core_ids=cores)
        yT = hostA_gather(resA.results)
        resB = run_bass_kernel_spmd(_get_nc('B1' if l == DEPTH - 1 else 'B0'), hostB_inputs(inp, l, x, yT), core_ids=cores)
        x = np.stack([np.concatenate([resB.results[b * 4 + j]["xo"] for j in range(4)], 0) for b in range(NB)])
    return x
```

```python
import contextlib
import os
import numpy as np
import ml_dtypes
import concourse.bass as bass
import concourse.mybir as mybir
from concourse.bass_utils import run_bass_kernel_spmd

F32 = mybir.dt.float32
BF16 = mybir.dt.bfloat16
AF = mybir.ActivationFunctionType
ALU = mybir.AluOpType
AX = mybir.AxisListType

D = 1024
SEQ = 8192
NB = 2
DEPTH = 2
EPS = 1e-6
EPOCH = 12000
POOL_WINDOWS = (2, 4, 8, 16)


class _Rec:
    def __init__(self):
        self.calls = []

    def __getattr__(self, name):
        def f(*a, **kw):
            self.calls.append((name, a, kw))
        return f


class K:
    def __init__(self, nc, stack, n_dma_sems=12):
        self.nc = nc
        self.stack = stack
        self.engs = ['pe', 'act', 'dve', 'pool', 'sp']
        self.q = {e: [] for e in self.engs}
        self.cnt = {e: 0 for e in self.engs}
        self.epoch = {e: 0 for e in self.engs}
        self.sems = {}
        self.known = {e: {} for e in self.engs}
        self.last_w = {}
        self.readers = {}
        self.dma_sems = {q: [stack.enter_context(nc.semaphore(f"dma_{q}{i}")) for i in range(n)]
                         for q, n in (('sp', 10), ('pool', 6), ('act', 2))}
        self.dma_cnt = {q: [0] * len(v) for q, v in self.dma_sems.items()}
        self.dma_rr = {q: 0 for q in self.dma_sems}

    def _sem(self, e):
        key = (e, self.epoch[e])
        if key not in self.sems:
            self.sems[key] = self.stack.enter_context(self.nc.semaphore(f"s_{e}_{self.epoch[e]}"))
        return self.sems[key]

    def _need(self, e, tok):
        if tok is None:
            return
        sem, val, src = tok[:3]
        if src == e and e == 'pe':
            return
        kn = self.known[e]
        if kn.get(id(sem), 0) >= val:
            return
        kn[id(sem)] = val
        self.q[e].append(('wait', sem, val))

    def _deps(self, e, reads, writes):
        for k in reads:
            self._need(e, self.last_w.get(k))
            if k.startswith('bank') or k == 'tpb':
                for r in self.readers.get(k, ()):
                    if r[2] != e:
                        self._need(e, r)
        for k in writes:
            self._need(e, self.last_w.get(k))
            for r in self.readers.get(k, ()):
                if r[2] == e:
                    continue
                self._need(e, r)

    def _commit(self, tok, reads, writes):
        for k in writes:
            self.last_w[k] = tok
            self.readers[k] = []
        for k in reads:
            self.readers.setdefault(k, []).append(tok)

    def op(self, e, fn, reads=(), writes=()):
        self._deps(e, reads, writes)
        if self.cnt[e] >= EPOCH:
            self.epoch[e] += 1
            self.cnt[e] = 0
        sem = self._sem(e)
        self.cnt[e] += 1
        tok = (sem, self.cnt[e], e)
        rec = _Rec()
        fn(rec)
        assert len(rec.calls) == 1
        name, a, kw = rec.calls[0]
        self.q[e].append(('op', (lambda eng, name=name, a=a, kw=kw: getattr(eng, name)(*a, **kw)), sem, 1))
        self._commit(tok, reads, writes)
        return tok

    def dma(self, e, out, in_, reads=(), writes=(), **kw):
        self._deps(e, reads, writes)
        i = self.dma_rr[e]
        self.dma_rr[e] = (i + 1) % len(self.dma_sems[e])
        sem = self.dma_sems[e][i]
        cnts = self.dma_cnt[e]
        if cnts[i] > 0:
            self._need(e, (sem, cnts[i] * 16, 'dma'))
        cnts[i] += 1
        tok = (sem, cnts[i] * 16, 'dma')
        self.q[e].append(('op', lambda eng: eng.dma_start(
            out=(out() if callable(out) else out), in_=(in_() if callable(in_) else in_), **kw), sem, 16))
        self._commit(tok, reads, writes)
        return tok

    def coll(self, kind, groups, src, dst, reads=(), writes=()):
        e = 'pool'
        self._deps(e, reads, writes)
        if not hasattr(self, 'cc_sem'):
            self.cc_sem = self.stack.enter_context(self.nc.semaphore("cc_sem"))
            self.cc_cnt = 0
        if self.cc_cnt > 0:
            self._need(e, (self.cc_sem, self.cc_cnt, 'dma'))
        self.cc_cnt += 1
        tok = (self.cc_sem, self.cc_cnt, 'dma')
        self.q[e].append(('op', lambda eng: eng.collective_compute(
            kind, ALU.bypass, groups, ins=[src.opt()], outs=[dst.opt()]), self.cc_sem, 1))
        self._commit(tok, reads, writes)
        return tok

    def wait_all(self, e, keys):
        for k in keys:
            self._need(e, self.last_w.get(k))

    def barrier(self):
        toks = []
        for f in self.engs:
            if self.cnt[f] > 0:
                toks.append((self._sem(f), self.cnt[f], f))
        for q, sems in self.dma_sems.items():
            for i, sem in enumerate(sems):
                if self.dma_cnt[q][i] > 0:
                    toks.append((sem, self.dma_cnt[q][i] * 16, 'dma'))
        if getattr(self, 'cc_cnt', 0) > 0:
            toks.append((self.cc_sem, self.cc_cnt, 'dma'))
        for e in self.engs:
            for t in toks:
                if t[2] != e:
                    self._need(e, t)

    def emit(self):
        nc = self.nc
        with nc.Block() as block:
            def replay(name, eng):
                for it in self.q[name]:
                    if it[0] == 'wait':
                        eng.wait_ge(it[1], it[2])
                    else:
                        it[1](eng).then_inc(it[2], it[3])

            @block.sync
            def _(eng):
                replay('sp', eng)

            @block.scalar
            def _(eng):
                replay('act', eng)

            @block.vector
            def _(eng):
                replay('dve', eng)

            @block.gpsimd
            def _(eng):
                replay('pool', eng)

            @block.tensor
            def _(eng):
                replay('pe', eng)
        for e in self.engs:
            self.q[e] = []


class Ctx:
    def __init__(self, nc, st, k=None, prefix=""):
        self.nc = nc
        self.st = st
        self.k = k if k is not None else K(nc, st)
        self.n = 0
        self.outkeys = []
        self.prefix = prefix

    def sb(self, name, shape, dt):
        return self.st.enter_context(self.nc.sbuf_tensor("s_" + self.prefix + name, list(shape), dt))

    def ps(self, name, shape, dt):
        return self.st.enter_context(self.nc.psum_tensor("p_" + self.prefix + name, list(shape), dt))


def emit_mod(cx, w_ada, b_ada, cT_d, ng_d, banks, need_gate):
    k = cx.k
    ncol = 3 if need_gate else 2
    cT = cx.sb("cT", [128, 8], F32)
    ng = cx.sb("ng", [128, 8], F32)
    modrow = cx.sb("modrow", [1, 3072], F32)
    one11 = cx.sb("one11", [1, 128], F32)
    s1 = cx.sb("s1", [128, 8], F32)
    s2 = cx.sb("s2", [128, 8], F32)
    gate_bc = cx.sb("gate_bc", [128, 1024], F32) if need_gate else None
    NWA = 4
    wa = [cx.sb(f"wa{i}", [128, 512], F32) for i in range(NWA)]
    k.dma('sp', cT[:], cT_d, writes=['cT'])
    k.dma('sp', ng[:], ng_d, writes=['ng'])
    k.op('dve', lambda e: e.memset(one11[:], 1.0), writes=['one11'])
    ngrp = ncol * 2
    i = 0
    for kc in range(8):
        for cg in range(ngrp):
            buf = wa[i % NWA]
            bk = f"wa{i % NWA}"
            i += 1
            k.dma('sp', buf[:], w_ada[kc * 128:(kc + 1) * 128, cg * 512:(cg + 1) * 512], writes=[bk])
            k.op('pe', lambda e, buf=buf, cg=cg, kc=kc: e.matmul(
                banks[cg][0:1, :], lhsT=cT[:, kc:kc + 1], rhs=buf[:], start=(kc == 0), stop=(kc == 7)),
                reads=[bk, 'cT'], writes=[f"bank{cg}"])
    for cg in range(ngrp):
        buf = wa[i % NWA]
        bk = f"wa{i % NWA}"
        i += 1
        k.dma('sp', buf[0:1, :], b_ada[0:1, cg * 512:(cg + 1) * 512], writes=[bk])
        k.op('dve', lambda e, cg=cg, buf=buf: e.tensor_tensor(
            out=modrow[0:1, cg * 512:(cg + 1) * 512], in0=banks[cg][0:1, :],
            in1=buf[0:1, :], op=ALU.add),
            reads=[f"bank{cg}", bk], writes=['modrow'])
    colb = banks[6]
    for cc in range(8):
        k.op('pe', lambda e, cc=cc: e.matmul(
            colb[:, cc:cc + 1], lhsT=modrow[0:1, 1024 + cc * 128:1024 + (cc + 1) * 128],
            rhs=one11[0:1, 0:1], start=True, stop=True), reads=['modrow', 'one11'], writes=['bank6'])
        k.op('pe', lambda e, cc=cc: e.matmul(
            colb[:, 8 + cc:9 + cc], lhsT=modrow[0:1, cc * 128:(cc + 1) * 128],
            rhs=one11[0:1, 0:1], start=True, stop=True), reads=['modrow', 'one11'], writes=['bank6'])
    k.op('dve', lambda e: e.scalar_tensor_tensor(
        out=s1[:], in0=colb[:, 0:8], scalar=1.0, in1=ng[:], op0=ALU.add, op1=ALU.mult),
        reads=['bank6', 'ng'], writes=['s1'])
    k.op('dve', lambda e: e.tensor_copy(out=s2[:], in_=colb[:, 8:16]), reads=['bank6'], writes=['s2'])
    if need_gate:
        for hh in range(2):
            k.op('pe', lambda e, hh=hh: e.matmul(
                banks[hh][:, :], lhsT=one11[0:1, 0:128], rhs=modrow[0:1, 2048 + hh * 512:2048 + (hh + 1) * 512],
                start=True, stop=True), reads=['modrow', 'one11'], writes=[f"bank{hh}"])
            k.op('dve', lambda e, hh=hh: e.tensor_copy(out=gate_bc[:, hh * 512:(hh + 1) * 512], in_=banks[hh][:, :]),
                 reads=[f"bank{hh}"], writes=['gate_bc'])
    return s1, s2, gate_bc


def emit_norm_tile(cx, xt, xkey, tb, s1, s2, ident, tp, tpkey, hT, hkey, tagn):
    k = cx.k
    i = cx.n
    cx.n += 1
    r = i % 2
    if not hasattr(cx, 'nrm'):
        cx.nrm = dict(
            sq=[cx.sb("nsq", [128, 1024], BF16)] * 2,
            st=[cx.sb(f"nst{j}", [128, 4], F32) for j in range(2)],
            xn=[cx.sb(f"nxn{j}", [128, 1024], BF16) for j in range(2)],
        )
    sq, stt, xn = cx.nrm['sq'][r], cx.nrm['st'][r], cx.nrm['xn'][r]
    ksq, kst, kxn = "nsq", f"nst{r}", f"nxn{r}"
    k.op('pool', lambda e: e.memset(stt[:], 0.0), writes=[kst])
    k.op('act', lambda e: e.activation(out=sq[:], in_=xt, func=AF.Square, accum_out=stt[:, 0:1]),
         reads=[xkey, kst], writes=[ksq, kst])
    k.op('act', lambda e: e.activation(out=stt[:, 1:2], in_=stt[:, 0:1], func=AF.Ln, scale=1.0 / D, bias=EPS),
         reads=[kst], writes=[kst])
    k.op('act', lambda e: e.activation(out=stt[:, 2:3], in_=stt[:, 1:2], func=AF.Exp, scale=-0.5),
         reads=[kst], writes=[kst])
    k.op('dve', lambda e: e.tensor_scalar(out=xn[:], in0=xt, scalar1=stt[:, 2:3], scalar2=None, op0=ALU.mult),
         reads=[xkey, kst], writes=[kxn])
    for kc in range(8):
        k.op('pe', lambda e, kc=kc: e.transpose(tp[:, kc * 128:(kc + 1) * 128], xn[:, kc * 128:(kc + 1) * 128], ident[:]),
             reads=[kxn, 'ident'], writes=[tpkey])
    for kc in range(8):
        eng = 'dve' if kc % 2 == 0 else 'act'
        if eng == 'dve':
            k.op('dve', lambda e, kc=kc: e.tensor_scalar(
                out=hT[:, kc, tb * 128:(tb + 1) * 128], in0=tp[:, kc * 128:(kc + 1) * 128],
                scalar1=s1[:, kc:kc + 1], scalar2=s2[:, kc:kc + 1], op0=ALU.mult, op1=ALU.add),
                reads=[tpkey, 's1', 's2'], writes=[hkey])
        else:
            k.op('act', lambda e, kc=kc: e.activation(
                out=hT[:, kc, tb * 128:(tb + 1) * 128], in_=tp[:, kc * 128:(kc + 1) * 128],
                func=AF.Identity, scale=s1[:, kc:kc + 1], bias=s2[:, kc:kc + 1]),
                reads=[tpkey, 's1', 's2'], writes=[hkey])


NTB = 2048


def emit_B(cx, last, x_d, yT_d, cT_d, ng_d, wada_d, bada_d, wg_d, gb_d, wbr_d, wout_d, fg_d, id_d, xo_d, x_reads=(), y_reads=()):
    k = cx.k
    nc = cx.nc
    banks = [cx.ps(f"bank{i}", [128, 512], F32) for i in range(7)]
    tp = cx.ps("tpb", [128, 1024], BF16)
    ident = cx.sb("ident", [128, 128], BF16)
    k.dma('pool', ident[:], id_d, writes=['ident'])
    s1, s2, gate_bc = emit_mod(cx, wada_d, bada_d, cT_d, ng_d, banks, True)
    wg = cx.sb("wg", [128, 8, 3 * D], BF16)
    wbr = cx.sb("wbr", [128, 8, D], BF16)
    wout = cx.sb("wout", [128, 8, D], BF16)
    gb = cx.sb("gb", [128, 24], F32)
    k.dma('sp', gb[:], gb_d, writes=['gb'])
    for kc in range(8):
        k.dma('pool', wg[:, kc, :], wg_d[kc * 128:(kc + 1) * 128, :], writes=['wg'])
    for kc in range(8):
        k.dma('pool', wbr[:, kc, :], wbr_d[kc * 128:(kc + 1) * 128, :], writes=['wbr'])
    for kc in range(8):
        k.dma('pool', wout[:, kc, :], wout_d[kc * 128:(kc + 1) * 128, :], writes=['wout'])
    if last:
        fg_bc = cx.sb("fg_bc", [128, D], F32)
        k.dma('sp', fg_bc[:], fg_d.partition_broadcast(128), writes=['fg_bc'])
    xres = cx.sb("xres", [128, 4, D], F32)
    hT = [cx.sb(f"hT{i}", [128, 8, 512], BF16) for i in range(2)]
    yT = [cx.sb(f"yT{i}", [128, 8, 512], BF16) for i in range(2)]
    mT = cx.sb("mT", [128, 8, 512], BF16)
    sig = [cx.sb(f"sig{i}", [128, 512], F32) for i in range(3)]
    tmp = [cx.sb(f"tmp{i}", [128, 512], F32) for i in range(3)]
    xn_o = [cx.sb(f"xno{i}", [128, D], F32) for i in range(2)]
    fst = [cx.sb(f"fst{i}", [128, 4], F32) for i in range(2)]
    yo = [cx.sb(f"yo{i}", [128, D], F32) for i in range(2)]
    fsq = cx.sb("fsq", [128, D], BF16)
    nG = 0
    nP = 0
    nO = 0
    ntile = NTB // 512
    for tl in range(ntile):
        r = tl % 2
        for (kk, p0, pn, src) in yT_d(tl):
            k.dma('sp', yT[r][p0:p0 + pn, kk, :], src, reads=y_reads, writes=[f"yT{r}_{kk}_{p0}"])
        for tb in range(4):
            t0 = tl * 512 + tb * 128
            k.dma('sp', xres[:, tb, :], x_d(t0), reads=x_reads, writes=[f"xres{tb}"])
            emit_norm_tile(cx, xres[:, tb, :], f"xres{tb}", tb, s1, s2, ident, tp, 'tpb', hT[r], f"hT{r}", 'b')
        for dc in range(8):
            for gi in range(3):
                gbk = 0 + (nG % 2)
                nG += 1
                for kc in range(8):
                    k.op('pe', lambda e, gbk=gbk, gi=gi, kc=kc, dc=dc, r=r: e.matmul(
                        banks[gbk][:, :], lhsT=wg[:, kc, gi * D + dc * 128:gi * D + (dc + 1) * 128],
                        rhs=hT[r][:, kc, :], start=(kc == 0), stop=(kc == 7)),
                        reads=['wg', f"hT{r}"], writes=[f"bank{gbk}"])
                k.op('act', lambda e, gbk=gbk, gi=gi, dc=dc: e.activation(
                    out=sig[gi][:], in_=banks[gbk][:, :], func=AF.Sigmoid,
                    bias=gb[:, gi * 8 + dc:gi * 8 + dc + 1], scale=1.0),
                    reads=[f"bank{gbk}", 'gb'], writes=[f"sig{gi}"])
            for bi, (k0, k1) in enumerate(((0, 4), (4, 6), (6, 8))):
                pbk = 2 + (nP % 2)
                nP += 1
                for kc in range(k0, k1):
                    k.op('pe', lambda e, pbk=pbk, kc=kc, dc=dc, r=r, k0=k0, k1=k1: e.matmul(
                        banks[pbk][:, :], lhsT=wbr[:, kc, dc * 128:(dc + 1) * 128],
                        rhs=yT[r][:, kc, :], start=(kc == k0), stop=(kc == k1 - 1)),
                        reads=['wbr'] + [f"yT{r}_{kc}_{p0}" for p0 in (0, 64)], writes=[f"bank{pbk}"])
                k.op('dve', lambda e, pbk=pbk, bi=bi: e.tensor_tensor(
                    out=tmp[bi][:], in0=banks[pbk][:, :], in1=sig[bi][:], op=ALU.mult),
                    reads=[f"bank{pbk}", f"sig{bi}"], writes=[f"tmp{bi}"])
            k.op('pool', lambda e: e.tensor_tensor(out=tmp[0][:], in0=tmp[0][:], in1=tmp[1][:], op=ALU.add),
                 reads=['tmp0', 'tmp1'], writes=['tmp0'])
            k.op('pool', lambda e, dc=dc: e.tensor_tensor(out=mT[:, dc, :], in0=tmp[0][:], in1=tmp[2][:], op=ALU.add),
                 reads=['tmp0', 'tmp2'], writes=['mT'])
        for tb in range(4):
            t0 = tl * 512 + tb * 128
            ro = nO % 2
            nO += 1
            for ch in range(2):
                obk = 4 + ch
                for kc in range(8):
                    k.op('pe', lambda e, obk=obk, kc=kc, tb=tb, ch=ch: e.matmul(
                        banks[obk][:, :], lhsT=mT[:, kc, tb * 128:(tb + 1) * 128],
                        rhs=wout[:, kc, ch * 512:(ch + 1) * 512], start=(kc == 0), stop=(kc == 7)),
                        reads=['mT', 'wout'], writes=[f"bank{obk}"])
                k.op('dve', lambda e, obk=obk, ch=ch, ro=ro: e.tensor_tensor(
                    out=xn_o[ro][:, ch * 512:(ch + 1) * 512], in0=banks[obk][:, :],
                    in1=gate_bc[:, ch * 512:(ch + 1) * 512], op=ALU.mult),
                    reads=[f"bank{obk}", 'gate_bc'], writes=[f"xno{ro}"])
            k.op('pool', lambda e, ro=ro, tb=tb: e.tensor_tensor(
                out=xn_o[ro][:], in0=xn_o[ro][:], in1=xres[:, tb, :], op=ALU.add),
                reads=[f"xno{ro}", f"xres{tb}"], writes=[f"xno{ro}"])
            if not last:
                k.dma('sp', xo_d[t0:t0 + 128, :], xn_o[ro][:], reads=[f"xno{ro}"], writes=[f"xo{t0}"])
                cx.outkeys.append(f"xo{t0}")
            else:
                k.op('pool', lambda e, ro=ro: e.memset(fst[ro][:], 0.0), writes=[f"fst{ro}"])
                k.op('act', lambda e, ro=ro: e.activation(out=fsq[:], in_=xn_o[ro][:], func=AF.Square,
                                                           accum_out=fst[ro][:, 0:1]),
                     reads=[f"xno{ro}", f"fst{ro}"], writes=['fsq', f"fst{ro}"])
                k.op('act', lambda e, ro=ro: e.activation(out=fst[ro][:, 1:2], in_=fst[ro][:, 0:1], func=AF.Ln,
                                                           scale=1.0 / D, bias=EPS),
                     reads=[f"fst{ro}"], writes=[f"fst{ro}"])
                k.op('act', lambda e, ro=ro: e.activation(out=fst[ro][:, 2:3], in_=fst[ro][:, 1:2], func=AF.Exp, scale=-0.5),
                     reads=[f"fst{ro}"], writes=[f"fst{ro}"])
                k.op('dve', lambda e, ro=ro: e.scalar_tensor_tensor(
                    out=yo[ro][:], in0=xn_o[ro][:], scalar=fst[ro][:, 2:3], in1=fg_bc[:],
                    op0=ALU.mult, op1=ALU.mult), reads=[f"xno{ro}", f"fst{ro}", 'fg_bc'], writes=[f"yo{ro}"])
                k.dma('sp', xo_d[t0:t0 + 128, :], yo[ro][:], reads=[f"yo{ro}"], writes=[f"xo{t0}"])
                cx.outkeys.append(f"xo{t0}")


def emit_A(cx, a, yT_o, ntile):
    import os
    STAGE = int(os.environ.get('A_STAGE', '9'))
    SUB = int(os.environ.get('A_SUB', '9'))
    DIS = os.environ.get('A_DIS', '')
    k = cx.k
    banks = [cx.ps(f"bank{i}", [128, 512], F32) for i in range(7)]
    tp = cx.ps("tpb", [128, 1024], BF16)
    ZB = (0, 1)
    LB = (2, 3)
    OB = 4
    PB = 5
    MB = 6
    ident = cx.sb("ident", [128, 128], BF16)
    k.dma('pool', ident[:], a['ident'], writes=['ident'])
    s1, s2, _ = emit_mod(cx, a['w_ada'], a['b_ada'], a['cT'], a['ng'], banks, False)
    wtm = cx.sb("wtm", [128, 8, 512], BF16)
    wfm = cx.sb("wfm", [128, 8, 512], BF16)
    wgt = cx.sb("wgt", [128, 8, 16], BF16)
    for kc in range(8):
        k.dma('pool', wtm[:, kc, :], a['wtm'][kc * 128:(kc + 1) * 128, :], writes=['wtm'])
        k.dma('pool', wfm[:, kc, :], a['wfm'][kc * 128:(kc + 1) * 128, :], writes=['wfm'])
        k.dma('pool', wgt[:, kc, :], a['wgt'][kc * 128:(kc + 1) * 128, :], writes=['wgt'])
    mgb = cx.sb("mgb", [128, 2], F32)
    nmgb = cx.sb("nmgb", [128, 2], F32)
    cw = cx.sb("cw", [128, 8], F32)
    cb = cx.sb("cb", [128, 2], F32)
    mng = cx.sb("mng", [128, 128], F32)
    poolw = cx.sb("poolw", [64, 64], BF16)
    pscale = cx.sb("pscale", [64, 1], F32)
    bands = cx.sb("bands", [128, 3, 128], BF16)
    mtri = cx.sb("mtri", [128, 128], F32)
    onesf = cx.sb("onesf", [128, 128], F32)
    sbm = cx.sb("sbm", [128, 4, 512], BF16)
    nui = cx.sb("nui", [128, 128], BF16)
    nones = cx.sb("nones", [128, 128], BF16)
    k.dma('sp', mgb[:], a['mgb'], writes=['mgb'])
    k.dma('sp', cw[:], a['cw'], writes=['cw'])
    k.dma('sp', cb[:], a['cb'], writes=['cb'])
    k.dma('sp', mng[:], a['mng'].partition_broadcast(128), writes=['mng'])
    k.dma('pool', poolw[:], a['poolw'], writes=['poolw'])
    k.dma('sp', pscale[:], a['pscale'], writes=['pscale'])
    for i in range(3):
        k.dma('pool', bands[:, i, :], a['bands'][i], writes=['bands'])
    k.dma('sp', mtri[:], a['mtri'], writes=['mtri'])
    for i in range(4):
        k.dma('pool', sbm[:, i, :], a['sbm'][i], writes=['sbm'])
    k.dma('pool', nui[:], a['nui'], writes=['nui'])
    k.op('dve', lambda e: e.memset(onesf[:], 1.0), writes=['onesf'])
    k.op('dve', lambda e: e.memset(nones[:], -1.0), writes=['nones'])
    k.op('dve', lambda e: e.tensor_scalar(out=nmgb[:], in0=mgb[:], scalar1=-1.0, scalar2=None, op0=ALU.mult),
         reads=['mgb'], writes=['nmgb'])
    sqT = cx.sb("sqT", [64, SEQ], BF16)
    skT = cx.sb("skT", [64, SEQ], BF16)
    SV = cx.sb("SV", [128, SEQ // 128, 64], BF16)
    Cn32 = cx.sb("Cn32", [128, 132], F32)
    Cnb = cx.sb("Cnb", [128, 132], BF16)
    k.op('dve', lambda e: e.memset(Cn32[:], 0.0), writes=['Cn32'])
    k.op('dve', lambda e: e.memset(Cnb[:], 0.0), writes=['Cnb'])
    xt = [cx.sb(f"xt{i}", [128, D], F32) for i in range(2)]
    hT = [cx.sb(f"hT{i}", [128, 8, 512], BF16) for i in range(2)]
    qkr = [cx.sb(f"qkr{i}", [128, 516], F32) for i in range(2)]
    for g in range(2):
        k.op('pool', lambda e, g=g: e.memset(qkr[g][:], 0.0), writes=[f"qkr{g}"])
    cacc = [cx.sb(f"cacc{i}", [128, 512], F32) for i in range(2)]
    csg = [cx.sb(f"csg{i}", [128, 512], F32) for i in range(2)]
    qT = cx.sb("qT", [128, 512], BF16)
    kT = cx.sb("kT", [128, 512], BF16)
    spz = cx.sb("spz", [64, 512], F32)
    ssz = cx.sb("ssz", [64, 512], F32)
    sgz = cx.sb("sgz", [64, 512], F32)
    Ub = [cx.sb(f"Ub{i}", [128, 64], BF16) for i in range(3)]
    tmS = [cx.sb(f"tmS{i}", [128, 512], F32) for i in range(2)]
    gsb = [cx.sb(f"gsb{i}", [128, 16], F32) for i in range(2)]
    sgo = [cx.sb(f"sgo{i}", [128, 256], F32) for i in range(2)]
    gz = [cx.sb(f"gz{i}", [128, 128], F32) for i in range(2)]
    V2 = [cx.sb(f"V2{i}", [128, 132], BF16) for i in range(2)]
    Ktm = [cx.sb(f"Ktm{i}", [128, 128], BF16) for i in range(2)]
    for i in range(2):
        k.op('pool', lambda e, i=i: e.memset(V2[i][:], 0.0), writes=[f"V2{i}"])
    Sm = [cx.sb(f"Sm{i}", [128, 128], BF16) for i in range(2)]
    t1 = [cx.sb(f"t1{i}", [128, 128], F32) for i in range(2)]
    t1sq = cx.sb("t1sq", [128, 128], BF16)
    ymb = [cx.sb(f"ymb{i}", [128, 128], BF16) for i in range(2)]
    ymT = [cx.sb(f"ymT{i}", [128, 512], BF16) for i in range(2)]
    pTs = cx.sb("pTs", [64, 512], BF16)
    ypT = [cx.sb(f"ypT{i}", [64, 512], BF16) for i in range(2)]
    ysT = [cx.sb(f"ysT{i}", [64, 512], BF16) for i in range(2)]
    Eb = [cx.sb(f"Eb{i}", [128, 512], F32) for i in range(2)]
    L32 = cx.sb("L32", [128, 512], F32)
    Lb = [cx.sb(f"Lb{i}", [128, 512], BF16) for i in range(2)]
    S32 = cx.sb("S32", [128, 512], F32)
    Sb = [cx.sb(f"Sb{i}", [128, 512], BF16) for i in range(2)]
    At = [cx.sb(f"At{i}", [128, 512], BF16) for i in range(2)]
    Am = [cx.sb(f"Am{i}", [128, 512], BF16) for i in range(2)]
    mb = banks[MB]
    cnt = dict(z=0, l=0, u=0, g=0, p=0)
    PR = [PB, 0, 1, 2, 3]

    def nextpb():
        i = PR[cnt['p'] % len(PR)]
        cnt['p'] += 1
        return banks[i], f"bank{i}"
    KSCALE = 128.0 ** -0.5
    nx = 0
    for tl in range(ntile if STAGE >= 2 else 0):
        r = tl % 2
        for tb in range(4):
            t0 = tl * 512 + tb * 128
            xr = nx % 2
            nx += 1
            k.dma('sp', xt[xr][:], a['x'](t0), reads=a.get('xreads', ()), writes=[f"xt{xr}"])
            emit_norm_tile(cx, xt[xr][:], f"xt{xr}", tb, s1, s2, ident, tp, 'tpb', hT[r], f"hT{r}", 'a')
        hk = f"hT{r}"
        for g in range(2):
            pb, pk = nextpb()
            for kc in range(8):
                k.op('pe', lambda e, g=g, kc=kc, r=r: e.matmul(
                    pb[:, :], lhsT=wfm[:, kc, g * 128:(g + 1) * 128], rhs=hT[r][:, kc, :],
                    start=(kc == 0), stop=(kc == 7)), reads=['wfm', hk], writes=[pk])
            k.op('pool', lambda e, g=g: e.tensor_copy(out=qkr[g][:, 0:3], in_=qkr[g][:, 512:515]),
                 reads=[f"qkr{g}"], writes=[f"qkr{g}h"])
            k.op('act', lambda e, g=g: e.activation(out=qkr[g][:, 3:515], in_=pb[:, :], func=AF.Identity),
                 reads=[pk, f"qkr{g}h"], writes=[f"qkr{g}"])
            k.op('pool', lambda e, g=g: e.tensor_scalar(
                out=cacc[g][:], in0=qkr[g][:, 0:512], scalar1=cw[:, 4 * g:4 * g + 1], scalar2=cb[:, g:g + 1],
                op0=ALU.mult, op1=ALU.add), reads=[f"qkr{g}", f"qkr{g}h", 'cw', 'cb'], writes=[f"cacc{g}"])
            for j in range(1, 4):
                k.op('dve', lambda e, g=g, j=j: e.scalar_tensor_tensor(
                    out=cacc[g][:], in0=qkr[g][:, j:j + 512], scalar=cw[:, 4 * g + j:4 * g + j + 1],
                    in1=cacc[g][:], op0=ALU.mult, op1=ALU.add),
                    reads=[f"qkr{g}", f"qkr{g}h", f"cacc{g}", 'cw'], writes=[f"cacc{g}"])
            k.op('act', lambda e, g=g: e.activation(out=csg[g][:], in_=cacc[g][:], func=AF.Sigmoid),
                 reads=[f"cacc{g}"], writes=[f"csg{g}"])
            dst, dk, scl = (qT, 'qT', 1.0) if g == 0 else (kT, 'kT', KSCALE)
            k.op('dve', lambda e, g=g, dst=dst, scl=scl: e.scalar_tensor_tensor(
                out=dst[:], in0=cacc[g][:], scalar=scl, in1=csg[g][:], op0=ALU.mult, op1=ALU.mult),
                reads=[f"cacc{g}", f"csg{g}"], writes=[dk])
        for i4, nm in enumerate(('pz', 'sz', 'sq', 'sk')):
            c0 = 256 + i4 * 64
            pb, pk = nextpb()
            for kc in range(8):
                k.op('pe', lambda e, kc=kc, r=r, c0=c0: e.matmul(
                    pb[0:64, :], lhsT=wfm[:, kc, c0:c0 + 64], rhs=hT[r][:, kc, :],
                    start=(kc == 0), stop=(kc == 7)), reads=['wfm', hk], writes=[pk])
            if nm in ('pz', 'sz'):
                dst, dk = (spz, 'spz') if nm == 'pz' else (ssz, 'ssz')
                k.op('act', lambda e: e.activation(out=sgz[:], in_=pb[0:64, :], func=AF.Sigmoid),
                     reads=[pk], writes=['sgz'])
                k.op('dve', lambda e, dst=dst: e.tensor_tensor(out=dst[:], in0=pb[0:64, :], in1=sgz[:], op=ALU.mult),
                     reads=[pk, 'sgz'], writes=[dk])
            elif nm == 'sq':
                k.op('act', lambda e, tl=tl: e.activation(out=sqT[:, tl * 512:(tl + 1) * 512], in_=pb[0:64, :],
                                                           func=AF.Identity, scale=0.125),
                     reads=[pk], writes=[f"sqT{tl}"])
            else:
                k.op('dve', lambda e, tl=tl: e.tensor_copy(out=skT[:, tl * 512:(tl + 1) * 512], in_=pb[0:64, :]),
                     reads=[pk], writes=[f"skT{tl}"])
        if STAGE < 3:
            continue
        for tb in range(4):
            n = tl * 4 + tb
            tsl = slice(tb * 128, (tb + 1) * 128)
            pb, pk = nextpb()
            for kc in range(8):
                k.op('pe', lambda e, kc=kc, r=r, tsl=tsl: e.matmul(
                    pb[:, :], lhsT=hT[r][:, kc, tsl], rhs=wtm[:, kc, :], start=(kc == 0), stop=(kc == 7)),
                    reads=['wtm', hk], writes=[pk])
            for kc in range(8):
                k.op('pe', lambda e, kc=kc, r=r, tsl=tsl: e.matmul(
                    mb[:, 400:416], lhsT=hT[r][:, kc, tsl], rhs=wgt[:, kc, :], start=(kc == 0), stop=(kc == 7)),
                    reads=['wgt', hk], writes=['bank6'])
            gi = cnt['g'] % 2
            cnt['g'] += 1
            G = gsb[gi]
            gk = f"gsb{gi}"
            ts = tmS[n % 2]
            tk = f"tmS{n % 2}"
            k.op('act', lambda e, ts=ts, pb=pb: e.activation(out=ts[:], in_=pb[:, :], func=AF.Identity),
                 reads=[pk], writes=[tk])
            k.op('pool', lambda e, n=n, ts=ts: e.tensor_copy(out=SV[:, n, :], in_=ts[:, 448:512]),
                 reads=[tk], writes=[f"SV{n}"])
            ui = n % 3
            k.op('pool', lambda e, ui=ui, ts=ts: e.tensor_copy(out=Ub[ui][:], in_=ts[:, 384:448]),
                 reads=[tk], writes=[f"Ub{ui}"])
            k.op('act', lambda e, gi=gi, ts=ts: e.activation(out=sgo[gi][:], in_=ts[:, 128:384], func=AF.Sigmoid),
                 reads=[tk], writes=[f"sgo{gi}"])
            k.op('dve', lambda e, gi=gi, ts=ts: e.tensor_tensor(out=gz[gi][:], in0=ts[:, 256:384], in1=sgo[gi][:, 128:256], op=ALU.mult),
                 reads=[tk, f"sgo{gi}"], writes=[f"gz{gi}"])
            k.op('pool', lambda e, gi=gi: e.tensor_tensor(out=gz[gi][:], in0=gz[gi][:], in1=mng[:], op=ALU.mult),
                 reads=[f"gz{gi}", 'mng'], writes=[f"gz{gi}"])
            if SUB < 1:
                continue
            k.op('act', lambda e, G=G: e.activation(out=G[:, 0:2], in_=mb[:, 400:402], func=AF.Identity),
                 reads=['bank6'], writes=[gk])
            k.op('act', lambda e, G=G: e.activation(out=G[:, 2:3], in_=G[:, 1:2], func=AF.Exp, scale=-1.0, bias=nmgb[:, 1:2]),
                 reads=[gk, 'nmgb'], writes=[gk])
            k.op('act', lambda e, G=G: e.activation(out=G[:, 3:4], in_=G[:, 2:3], func=AF.Ln, scale=1.0, bias=1.0),
                 reads=[gk], writes=[gk])
            k.op('pe', lambda e, G=G: e.matmul(mb[:, 404:405], lhsT=mtri[:], rhs=G[:, 3:4], start=True, stop=True),
                 reads=[gk, 'mtri'], writes=['bank6'])
            k.op('pe', lambda e, G=G: e.matmul(mb[:, 405:406], lhsT=onesf[:], rhs=G[:, 3:4], start=True, stop=True),
                 reads=[gk, 'onesf'], writes=['bank6'])
            k.op('act', lambda e, G=G: e.activation(out=G[:, 4:6], in_=mb[:, 404:406], func=AF.Exp, scale=-1.0),
                 reads=['bank6'], writes=[gk])
            k.op('dve', lambda e, G=G: e.tensor_tensor(out=G[:, 6:7], in0=mb[:, 404:405], in1=G[:, 0:1], op=ALU.add),
                 reads=['bank6', gk], writes=[gk])
            k.op('act', lambda e, G=G: e.activation(out=G[:, 7:8], in_=G[:, 6:7], func=AF.Exp, scale=1.0, bias=mgb[:, 0:1]),
                 reads=[gk, 'mgb'], writes=[gk])
            if SUB < 2:
                continue
            vi = n % 2
            k.op('dve', lambda e, vi=vi, G=G, ts=ts: e.tensor_scalar(out=V2[vi][:, 0:128], in0=ts[:, 0:128], scalar1=G[:, 7:8],
                                                              scalar2=None, op0=ALU.mult),
                 reads=[tk, gk], writes=[f"V2{vi}"])
            k.op('dve', lambda e, vi=vi, G=G: e.tensor_copy(out=V2[vi][:, 128:129], in_=G[:, 7:8]),
                 reads=[gk], writes=[f"V2{vi}"])
            k.op('pe', lambda e, tsl=tsl: e.transpose(tp[:, 0:128], kT[:, tsl], ident[:]),
                 reads=['kT', 'ident'], writes=['tpb'])
            k.op('dve', lambda e, vi=vi: e.tensor_copy(out=Ktm[vi][:], in_=tp[:, 0:128]),
                 reads=['tpb'], writes=[f"Ktm{vi}"])
            k.op('pe', lambda e, tsl=tsl: e.matmul(mb[:, 0:128], lhsT=kT[:, tsl], rhs=qT[:, tsl], start=True, stop=True),
                 reads=['kT', 'qT'], writes=['bank6'])
            k.op('dve', lambda e, vi=vi: e.tensor_tensor(out=Sm[vi][:], in0=mb[:, 0:128], in1=mtri[:], op=ALU.mult),
                 reads=['bank6', 'mtri'], writes=[f"Sm{vi}"])
            if SUB < 3:
                continue
            k.op('pe', lambda e, tsl=tsl: e.matmul(mb[:, 128:258], lhsT=qT[:, tsl], rhs=Cnb[:, 0:130], start=True, stop=False),
                 reads=['qT', 'Cnb'], writes=['bank6'])
            k.op('pe', lambda e, vi=vi: e.matmul(mb[:, 128:258], lhsT=Sm[vi][:], rhs=V2[vi][:, 0:130], start=False, stop=True),
                 reads=[f"Sm{vi}", f"V2{vi}"], writes=['bank6'])
            k.op('pe', lambda e, vi=vi: e.matmul(mb[:, 260:390], lhsT=Ktm[vi][:], rhs=V2[vi][:, 0:130], start=True, stop=True),
                 reads=[f"Ktm{vi}", f"V2{vi}"], writes=['bank6'])
            k.op('dve', lambda e: e.tensor_tensor(out=Cn32[:, 0:130], in0=mb[:, 260:390], in1=Cn32[:, 0:130], op=ALU.add),
                 reads=['bank6', 'Cn32'], writes=['Cn32'])
            k.op('dve', lambda e, G=G: e.tensor_scalar(out=Cn32[:, 0:130], in0=Cn32[:, 0:130], scalar1=G[:, 5:6],
                                                       scalar2=None, op0=ALU.mult),
                 reads=['Cn32', gk], writes=['Cn32'])
            k.op('pool', lambda e: e.tensor_copy(out=Cnb[:, 0:130], in_=Cn32[:, 0:130]),
                 reads=['Cn32'], writes=['Cnb'])
            if SUB < 4:
                continue
            k.op('dve', lambda e, G=G: e.tensor_tensor(out=G[:, 8:9], in0=mb[:, 256:257], in1=G[:, 4:5], op=ALU.mult),
                 reads=['bank6', gk], writes=[gk])
            k.op('dve', lambda e, G=G: e.tensor_tensor(out=G[:, 8:9], in0=G[:, 8:9], in1=G[:, 8:9], op=ALU.mult),
                 reads=[gk], writes=[gk])
            k.op('dve', lambda e, G=G: e.tensor_scalar(out=G[:, 8:9], in0=G[:, 8:9], scalar1=1.0, scalar2=None, op0=ALU.max),
                 reads=[gk], writes=[gk])
            k.op('act', lambda e, G=G: e.activation(out=G[:, 14:15], in_=G[:, 8:9], func=AF.Ln),
                 reads=[gk], writes=[gk])
            k.op('act', lambda e, G=G: e.activation(out=G[:, 9:10], in_=G[:, 14:15], func=AF.Exp, scale=-0.5),
                 reads=[gk], writes=[gk])
            k.op('dve', lambda e, G=G: e.tensor_tensor(out=G[:, 10:11], in0=G[:, 9:10], in1=G[:, 4:5], op=ALU.mult),
                 reads=[gk], writes=[gk])
            k.op('dve', lambda e, G=G, gi=gi: e.scalar_tensor_tensor(
                out=t1[gi][:], in0=mb[:, 128:256], scalar=G[:, 10:11], in1=sgo[gi][:, 0:128], op0=ALU.mult, op1=ALU.mult),
                reads=['bank6', gk, f"sgo{gi}"], writes=[f"t1{gi}"])
            k.op('pool', lambda e, G=G: e.memset(G[:, 11:12], 0.0), reads=[], writes=[gk + 'a'])
            k.op('act', lambda e, G=G, gi=gi: e.activation(out=t1sq[:], in_=t1[gi][:], func=AF.Square, accum_out=G[:, 11:12]),
                 reads=[f"t1{gi}", gk + 'a'], writes=['t1sq', gk + 'a'])
            k.op('act', lambda e, G=G: e.activation(out=G[:, 12:13], in_=G[:, 11:12], func=AF.Ln, scale=1.0 / 128, bias=EPS),
                 reads=[gk + 'a'], writes=[gk + 'b'])
            k.op('act', lambda e, G=G: e.activation(out=G[:, 13:14], in_=G[:, 12:13], func=AF.Exp, scale=-0.5),
                 reads=[gk + 'b'], writes=[gk + 'c'])
            k.op('dve', lambda e, G=G, gi=gi: e.scalar_tensor_tensor(
                out=ymb[gi][:], in0=t1[gi][:], scalar=G[:, 13:14], in1=gz[gi][:], op0=ALU.mult, op1=ALU.mult),
                reads=[f"t1{gi}", gk + 'c', f"gz{gi}"], writes=[f"ymb{gi}"])
            k.op('pe', lambda e, gi=gi: e.transpose(tp[:, 128:256], ymb[gi][:], ident[:]),
                 reads=[f"ymb{gi}", 'ident'], writes=['tpb'])
            k.op('act', lambda e, r=r, tsl=tsl: e.activation(out=ymT[r][:, tsl], in_=tp[:, 128:256], func=AF.Identity),
                 reads=['tpb'], writes=[f"ymT{r}"])
            if SUB < 5:
                continue
            bi = 0 if n == 0 else 1
            k.op('pe', lambda e, ui=ui, bi=bi, tsl=tsl, n=n: e.matmul(
                banks[OB][0:64, 0:128], lhsT=Ub[ui][:], rhs=bands[:, bi, :], start=True, stop=(n == 0)),
                reads=[f"Ub{ui}", 'bands'], writes=['bank4'])
            if n > 0:
                up = (n - 1) % 3
                k.op('pe', lambda e, up=up: e.matmul(
                    banks[OB][0:64, 0:128], lhsT=Ub[up][:], rhs=bands[:, 2, :], start=False, stop=True),
                    reads=[f"Ub{up}", 'bands'], writes=['bank4'])
            k.op('dve', lambda e, tsl=tsl: e.tensor_copy(out=pTs[:, tsl], in_=banks[OB][0:64, 0:128]),
                 reads=['bank4'], writes=['pTs'])
        if os.environ.get('A_SKIPPOST'):
            continue
        k.dma('sp', yT_o('m', tl), ymT[r][:], reads=[f"ymT{r}"], writes=[f"oym{tl}"])
        cx.outkeys.append(f"oym{tl}")
        pb, pk = nextpb()
        k.op('pe', lambda e: e.matmul(pb[0:64, :], lhsT=poolw[:], rhs=pTs[:], start=True, stop=True),
             reads=['poolw', 'pTs'], writes=[pk])
        k.op('dve', lambda e, r=r: e.scalar_tensor_tensor(
            out=ypT[r][:], in0=pb[0:64, :], scalar=pscale[:, 0:1], in1=spz[:], op0=ALU.mult, op1=ALU.mult),
            reads=[pk, 'pscale', 'spz'], writes=[f"ypT{r}"])
        k.dma('sp', yT_o('p', tl), ypT[r][:], reads=[f"ypT{r}"], writes=[f"oyp{tl}"])
        cx.outkeys.append(f"oyp{tl}")
        if STAGE < 4:
            continue
        top = 4 * tl + 3
        qsl = slice(tl * 512, (tl + 1) * 512)
        sq_keys = [f"sqT{tl}"]
        U = top + 1

        def st_Z(u):
            kb = top - u
            ci = u % 2
            zi = ZB[ci]
            zb = banks[zi]
            ksl = slice(kb * 128, (kb + 1) * 128)
            kkey = f"skT{kb // 4}"
            diag = kb >= 4 * tl
            k.op('pe', lambda e: e.matmul(zb[:, :], lhsT=skT[:, ksl], rhs=sqT[:, qsl], start=True, stop=True),
                 reads=[kkey] + sq_keys, writes=[f"bank{zi}"])
            k.op('act', lambda e: e.activation(out=Eb[ci][:], in_=zb[:, :], func=AF.Exp),
                 reads=[f"bank{zi}"], writes=[f"Eb{ci}"])
            if diag:
                k.op('act', lambda e: e.activation(out=L32[:], in_=Eb[ci][:], func=AF.Ln, scale=1.0, bias=1.0),
                     reads=[f"Eb{ci}"], writes=['L32'])
                k.op('dve', lambda e: e.tensor_tensor(out=Lb[ci][:], in0=L32[:], in1=sbm[:, kb - 4 * tl, :], op=ALU.mult),
                     reads=['L32', 'sbm'], writes=[f"Lb{ci}"])
            else:
                k.op('act', lambda e: e.activation(out=Lb[ci][:], in_=Eb[ci][:], func=AF.Ln, scale=1.0, bias=1.0),
                     reads=[f"Eb{ci}"], writes=[f"Lb{ci}"])
        def st_S(u, tl=tl, top=top):
            kb = top - u
            ci = u % 2
            if kb > 0:
                if kb == top:
                    k.op('pool', lambda e: e.tensor_copy(out=S32[:], in_=Lb[ci][:]), reads=[f"Lb{ci}"], writes=['S32'])
                else:
                    k.op('pool', lambda e: e.tensor_tensor(out=S32[:], in0=S32[:], in1=Lb[ci][:], op=ALU.add),
                         reads=['S32', f"Lb{ci}"], writes=['S32'])
                k.op('pool', lambda e: e.tensor_copy(out=Sb[1 - ci][:], in_=S32[:]), reads=['S32'], writes=[f"Sb{1 - ci}"])

        def st_L(u):
            kb = top - u
            ci = u % 2
            li = LB[ci]
            lb = banks[li]
            ksl = slice(kb * 128, (kb + 1) * 128)
            kkey = f"skT{kb // 4}"
            diag = kb >= 4 * tl
            k.op('pe', lambda e: e.matmul(lb[:, :], lhsT=skT[:, ksl], rhs=sqT[:, qsl], start=True, stop=False),
                 reads=[kkey] + sq_keys, writes=[f"bank{li}"])
            k.op('pe', lambda e: e.matmul(lb[:, :], lhsT=nui[:], rhs=Lb[ci][:], start=False, stop=(kb == top)),
                 reads=['nui', f"Lb{ci}"], writes=[f"bank{li}"])
            if kb != top:
                k.op('pe', lambda e: e.matmul(lb[:, :], lhsT=nones[:], rhs=Sb[ci][:], start=False, stop=True),
                     reads=['nones', f"Sb{ci}"], writes=[f"bank{li}"])
            k.op('act', lambda e: e.activation(out=At[ci][:], in_=lb[:, :], func=AF.Exp),
                 reads=[f"bank{li}"], writes=[f"At{ci}"])
            if diag:
                k.op('dve', lambda e: e.tensor_tensor(out=Am[ci][:], in0=At[ci][:], in1=sbm[:, kb - 4 * tl, :], op=ALU.mult),
                     reads=[f"At{ci}", 'sbm'], writes=[f"Am{ci}"])

        def st_V(u):
            kb = top - u
            ci = u % 2
            diag = kb >= 4 * tl
            asrc, akey = (Am[ci], f"Am{ci}") if diag else (At[ci], f"At{ci}")
            k.op('pe', lambda e: e.matmul(banks[OB][0:64, :], lhsT=SV[:, kb, :], rhs=asrc[:],
                                          start=(kb == top), stop=(kb == 0)),
                 reads=[f"SV{kb}", akey], writes=['bank4'])

        for step in range(U + 2):
            if step < U:
                st_Z(step)
            if 1 <= step <= U:
                st_L(step - 1)
            if step < U:
                st_S(step)
            if step >= 2:
                st_V(step - 2)
        k.op('dve', lambda e, r=r: e.tensor_tensor(out=ysT[r][:], in0=banks[OB][0:64, :], in1=ssz[:], op=ALU.mult),
             reads=['bank4', 'ssz'], writes=[f"ysT{r}"])
        k.dma('sp', yT_o('s', tl), ysT[r][:], reads=[f"ysT{r}"], writes=[f"oys{tl}"])
        cx.outkeys.append(f"oys{tl}")


def _consts():
    s = np.arange(128)
    mtri = (s[:, None] <= s[None, :]).astype(np.float32)
    nui = -(s[:, None] >= s[None, :]).astype(np.float32)
    t = np.arange(512)
    sbm = np.stack([((i * 128 + s)[:, None] < t[None, :]).astype(np.float32) for i in range(4)])
    return mtri, nui, sbm


def _bands(w):
    s = np.arange(128)
    out = np.zeros((3, 128, 128), np.float32)
    for t in range(128):
        lo = max(t + 1 - w, 0)
        out[0, lo:t + 1, t] += 1.0 / (t + 1 - lo)
        out[0, t, t] -= 1.0
        lo = max(t + 1 - w, 0)
        out[1, lo:t + 1, t] += 1.0 / w
        out[1, t, t] -= 1.0
        nprev = w - (t + 1)
        if nprev > 0:
            out[2, 128 - nprev:, t] += 1.0 / w
    return out


def hostA_inputs(inp, l, x_full):
    mtri, nui, sbm = _consts()
    ident = np.eye(128, dtype=np.float32)
    w_in = inp["w_in"][l]
    maps = []
    for core in range(8):
        b, hh = core // 4, core % 4
        c128 = slice(hh * 128, (hh + 1) * 128)
        c64 = slice(hh * 64, (hh + 1) * 64)
        off = dict(mq=0, mk=512, mv=1024, mi=1536, mf=1540, mo=1544, mz=2056, pu=2568, pz=2824,
                   sq=3080, sk=3336, sv=3592, sz=3848)
        col = lambda nm, sl: w_in[:, off[nm] + sl.start: off[nm] + sl.stop]
        wtm = np.concatenate([col('mv', c128), col('mo', c128), col('mz', c128), col('pu', c64), col('sv', c64)], 1)
        wgt = np.zeros((D, 16), np.float32)
        wgt[:, 0] = w_in[:, off['mi'] + hh]
        wgt[:, 1] = w_in[:, off['mf'] + hh]
        wfm = np.concatenate([col('mq', c128), col('mk', c128), col('pz', c64), col('sz', c64),
                              col('sq', c64), col('sk', c64)], 1)
        mg = inp["m_gate_b"][l]
        mgb = np.tile(np.array([[mg[hh], mg[4 + hh]]], np.float32), (128, 1))
        cwl = inp["conv_w"][l]
        cw = np.concatenate([cwl[:, c128].T, cwl[:, 512 + hh * 128: 512 + (hh + 1) * 128].T], 1)
        cbl = inp["conv_b"][l]
        cb = np.stack([cbl[c128], cbl[512 + hh * 128: 512 + (hh + 1) * 128]], 1)
        maps.append(dict(
            cT=np.ascontiguousarray(inp["c"][b].reshape(8, 128).T),
            ng=np.ascontiguousarray(inp["norm_g"][l].reshape(8, 128).T),
            w_ada=inp["w_ada"][l], b_ada=inp["b_ada"][l].reshape(1, -1),
            wtm=np.ascontiguousarray(wtm), wgt=np.ascontiguousarray(wgt), wfm=np.ascontiguousarray(wfm),
            mgb=mgb, cw=np.ascontiguousarray(cw), cb=np.ascontiguousarray(cb),
            mng=np.ascontiguousarray(inp["m_norm_g"][l][c128].reshape(1, 128)),
            poolw=np.ascontiguousarray(inp["pool_w"][l][hh]),
            pscale=np.ascontiguousarray(inp["pool_scale"][l][c64].reshape(64, 1)),
            bands=_bands(POOL_WINDOWS[hh]), mtri=mtri, sbm=sbm, nui=nui, ident=ident))
    return maps


def hostA_gather(results):
    out = np.zeros((NB, D, SEQ), dtype=ml_dtypes.bfloat16)
    for core in range(8):
        b, hh = core // 4, core % 4
        y = results[core]["yTo"]
        out[b, hh * 128:(hh + 1) * 128] = y[0:128]
        out[b, 512 + hh * 64:512 + (hh + 1) * 64] = y[128:192]
        out[b, 768 + hh * 64:768 + (hh + 1) * 64] = y[192:256]
    return out


def hostB_inputs(inp, l, x_full, yT_full):
    maps = []
    ident = np.eye(128, dtype=np.float32)
    wbr = np.concatenate([inp["w_br_m"][l], inp["w_br_p"][l], inp["w_br_s"][l]], 0)
    for core in range(8):
        b, j = core // 4, core % 4
        sl = slice(j * NTB, (j + 1) * NTB)
        maps.append(dict(
            cT=np.ascontiguousarray(inp["c"][b].reshape(8, 128).T),
            ng=np.ascontiguousarray(inp["norm_g"][l].reshape(8, 128).T),
            w_ada=inp["w_ada"][l], b_ada=inp["b_ada"][l].reshape(1, -1),
            wg=np.ascontiguousarray(inp["w_in"][l][:, 4104:]),
            gb=np.ascontiguousarray(inp["gate_b"][l].reshape(24, 128).T),
            wbr=wbr, wout=inp["w_out"][l], fg=inp["final_g"].reshape(1, -1), ident=ident))
    return maps


RG4 = [[0, 1, 2, 3], [4, 5, 6, 7]]
A_NAMES = dict(cT=[128, 8], ng=[128, 8], w_ada=[D, 3 * D], b_ada=[1, 3 * D], wtm=[D, 512], wgt=[D, 16],
               wfm=[D, 512], mgb=[128, 2], cw=[128, 8], cb=[128, 2], mng=[1, 128], poolw=[64, 64],
               pscale=[64, 1], bands=[3, 128, 128])
B_NAMES = dict(wg=[D, 3 * D], gb=[128, 24], wbr=[D, D], wout=[D, D])
C_NAMES = dict(mtri=[128, 128], sbm=[4, 128, 512], nui=[128, 128], ident=[128, 128], fg=[1, D])


def build_fused(ntile=SEQ // 512):
    nc = bass.Bass("TRN2", target_bir_lowering=False)
    dt_in = lambda name, shape, dt=F32: nc.dram_tensor(name, list(shape), dt, kind="ExternalInput").ap()
    x_in = dt_in("x", [SEQ, D])
    cst = {n: dt_in(n, sh) for n, sh in C_NAMES.items()}
    lay = []
    for l in range(DEPTH):
        d = {n: dt_in(f"{n}_{l}", sh) for n, sh in A_NAMES.items()}
        d.update({n: dt_in(f"{n}_{l}", sh) for n, sh in B_NAMES.items()})
        lay.append(d)
    out_d = nc.dram_tensor("out", [NTB, D], F32, kind="ExternalOutput").ap()
    internal = lambda name, shape, dt: nc.dram_tensor(name, list(shape), dt).ap()
    ys = internal("ys", [4 * 256, NTB], BF16)
    yr = internal("yr", [4 * 1024, NTB], BF16)
    yown = internal("yown", [D, NTB], BF16)
    xown = internal("xown", [NTB, D], F32)
    xs1 = internal("xs1", [NTB, D], F32)
    xg1 = internal("xg1", [8 * 1024, D], F32)
    ROW = dict(m=(0, 128), p=(128, 64), s=(192, 64))

    def ys_out(nm, tl):
        r0, nr = ROW[nm]
        q = tl // 4
        return ys[q * 256 + r0:q * 256 + r0 + nr, (tl % 4) * 512:(tl % 4 + 1) * 512]

    with contextlib.ExitStack() as outer:
        k = K(nc, outer)
        PID = nc.partition_id()

        def quarter():
            return PID % 4

        for l in range(DEPTH):
            last = (l == DEPTH - 1)
            with contextlib.ExitStack() as st:
                cx = Ctx(nc, st, k, prefix=f"A{l}_")
                a = dict(lay[l])
                a.update(cst)
                if l == 0:
                    a['x'] = lambda t0: x_in[t0:t0 + 128, :]
                else:
                    def xg_tile(t0):
                        rank, c, r0 = t0 // NTB, (t0 % NTB) // 256, t0 % 256
                        return xg1[c * 1024 + rank * 256 + r0:c * 1024 + rank * 256 + r0 + 128, :]
                    a['x'] = xg_tile
                    a['xreads'] = ['xg1']
                emit_A(cx, a, ys_out, ntile)
                akeys = list(cx.outkeys)
                k.barrier()
                k.emit()
            for q in range(4):
                k.coll("AllGather", RG4, ys[q * 256:(q + 1) * 256, :], yr[q * 1024:(q + 1) * 1024, :],
                       reads=akeys, writes=['yrecv'])
            for i in range(2):
                k.dma('sp', yown[i * 512:(i + 1) * 512, :],
                      (lambda i=i: yr.rearrange("(q r) t -> q r t", q=4)[
                          bass.ds(quarter(), 1), i * 512:(i + 1) * 512, :].squeeze(0)),
                      reads=['yrecv'], writes=[f'yown_{i}'])
            ykeys = ['yown_0', 'yown_1']
            if l == 0:
                for i in range(4):
                    k.dma('sp', xown[i * 512:(i + 1) * 512, :],
                          (lambda i=i: x_in.rearrange("(q r) d -> q r d", q=4)[
                              bass.ds(quarter(), 1), i * 512:(i + 1) * 512, :].squeeze(0)), writes=[f'xown{i}'])
            with contextlib.ExitStack() as st:
                cx = Ctx(nc, st, k, prefix=f"B{l}_")

                def yT_d(tl):
                    cs = slice(tl * 512, (tl + 1) * 512)
                    res = []
                    for kk in range(4):
                        res.append((kk, 0, 128, yown[kk * 256:kk * 256 + 128, cs]))
                    for j, r0 in ((0, 128), (1, 192)):
                        for half in range(2):
                            kk = 4 + 2 * j + half
                            for hh2 in range(2):
                                rank = 2 * half + hh2
                                res.append((kk, hh2 * 64, 64, yown[rank * 256 + r0:rank * 256 + r0 + 64, cs]))
                    return res

                if l == 0:
                    x_d = lambda t0: xown[t0:t0 + 128, :]
                    x_reads = [f'xown{i}' for i in range(4)]
                else:
                    x_d = lambda t0: xs1[t0:t0 + 128, :]
                    x_reads = ['xs1']
                d = lay[l]
                emit_B(cx, last, x_d, yT_d, d['cT'], d['ng'], d['w_ada'], d['b_ada'], d['wg'], d['gb'], d['wbr'],
                       d['wout'], cst['fg'], cst['ident'], out_d if last else xs1, x_reads=x_reads, y_reads=ykeys)
                bkeys = list(cx.outkeys)
                if last:
                    k.wait_all('sp', bkeys)
                else:
                    k.barrier()
                k.emit()
            if not last:
                for c in range(8):
                    k.coll("AllGather", RG4, xs1[c * 256:(c + 1) * 256, :], xg1[c * 1024:(c + 1) * 1024, :],
                           reads=bkeys, writes=['xg1'])
                k.last_w['xs1'] = k.last_w[bkeys[-1]]
    return nc


def host_inputs(inp):
    mtri, nui, sbm = _consts()
    ident = np.eye(128, dtype=np.float32)
    maps = [dict(mtri=mtri, nui=nui, sbm=sbm, ident=ident, fg=inp["final_g"].reshape(1, -1).astype(np.float32))
            for _ in range(8)]
    for core in range(8):
        maps[core]["x"] = np.ascontiguousarray(inp["x"][core // 4])
    dummy_x = None
    for l in range(DEPTH):
        ma = hostA_inputs(inp, l, None)
        mb = hostB_inputs(inp, l, None, None)
        for core in range(8):
            for n in A_NAMES:
                maps[core][f"{n}_{l}"] = ma[core][n]
            for n in B_NAMES:
                maps[core][f"{n}_{l}"] = mb[core][n]
    return maps


_NC = {}


def kernel(**inputs):
    inp = {k: np.asarray(v) for k, v in inputs.items()}
    if 'nc' not in _NC:
        _NC['nc'] = build_fused()
    res = run_bass_kernel_spmd(_NC['nc'], host_inputs(inp), core_ids=list(range(8)))
    out = np.stack([np.concatenate([res.results[b * 4 + j]["out"] for j in range(4)], 0) for b in range(NB)])
    return out.astype(np.float32)
```

```python
import contextlib
import os
import numpy as np
import ml_dtypes
import concourse.bass as bass
import concourse.mybir as mybir
from concourse.bass_utils import run_bass_kernel_spmd

F32 = mybir.dt.float32
BF16 = mybir.dt.bfloat16
AF = mybir.ActivationFunctionType
ALU = mybir.AluOpType
AX = mybir.AxisListType

D = 1024
SEQ = 8192
NB = 2
DEPTH = 2
EPS = 1e-6
EPOCH = 12000
POOL_WINDOWS = (2, 4, 8, 16)


class _Rec:
    def __init__(self):
        self.calls = []

    def __getattr__(self, name):
        def f(*a, **kw):
            self.calls.append((name, a, kw))
        return f


class K:
    def __init__(self, nc, stack, n_dma_sems=12):
        self.nc = nc
        self.stack = stack
        self.engs = ['pe', 'act', 'dve', 'pool', 'sp']
        self.q = {e: [] for e in self.engs}
        self.cnt = {e: 0 for e in self.engs}
        self.epoch = {e: 0 for e in self.engs}
        self.sems = {}
        self.known = {e: {} for e in self.engs}
        self.last_w = {}
        self.readers = {}
        self.dma_sems = {q: [stack.enter_context(nc.semaphore(f"dma_{q}{i}")) for i in range(n)]
                         for q, n in (('sp', 10), ('pool', 6), ('act', 2))}
        self.dma_cnt = {q: [0] * len(v) for q, v in self.dma_sems.items()}
        self.dma_rr = {q: 0 for q in self.dma_sems}

    def _sem(self, e):
        key = (e, self.epoch[e])
        if key not in self.sems:
            self.sems[key] = self.stack.enter_context(self.nc.semaphore(f"s_{e}_{self.epoch[e]}"))
        return self.sems[key]

    def _need(self, e, tok):
        if tok is None:
            return
        sem, val, src = tok[:3]
        if src == e and e == 'pe':
            return
        kn = self.known[e]
        if kn.get(id(sem), 0) >= val:
            return
        kn[id(sem)] = val
        self.q[e].append(('wait', sem, val))

    def _deps(self, e, reads, writes):
        for k in reads:
            self._need(e, self.last_w.get(k))
            if k.startswith('bank') or k == 'tpb':
                for r in self.readers.get(k, ()):
                    if r[2] != e:
                        self._need(e, r)
        for k in writes:
            self._need(e, self.last_w.get(k))
            for r in self.readers.get(k, ()):
                if r[2] == e:
                    continue
                self._need(e, r)

    def _commit(self, tok, reads, writes):
        for k in writes:
            self.last_w[k] = tok
            self.readers[k] = []
        for k in reads:
            self.readers.setdefault(k, []).append(tok)

    def op(self, e, fn, reads=(), writes=()):
        self._deps(e, reads, writes)
        if self.cnt[e] >= EPOCH:
            self.epoch[e] += 1
            self.cnt[e] = 0
        sem = self._sem(e)
        self.cnt[e] += 1
        tok = (sem, self.cnt[e], e)
        rec = _Rec()
        fn(rec)
        assert len(rec.calls) == 1
        name, a, kw = rec.calls[0]
        self.q[e].append(('op', (lambda eng, name=name, a=a, kw=kw: getattr(eng, name)(*a, **kw)), sem, 1))
        self._commit(tok, reads, writes)
        return tok

    def dma(self, e, out, in_, reads=(), writes=(), **kw):
        self._deps(e, reads, writes)
        i = self.dma_rr[e]
        self.dma_rr[e] = (i + 1) % len(self.dma_sems[e])
        sem = self.dma_sems[e][i]
        cnts = self.dma_cnt[e]
        if cnts[i] > 0:
            self._need(e, (sem, cnts[i] * 16, 'dma'))
        cnts[i] += 1
        tok = (sem, cnts[i] * 16, 'dma')
        self.q[e].append(('op', lambda eng: eng.dma_start(
            out=(out() if callable(out) else out), in_=(in_() if callable(in_) else in_), **kw), sem, 16))
        self._commit(tok, reads, writes)
        return tok

    def coll(self, kind, groups, src, dst, reads=(), writes=()):
        e = 'pool'
        self._deps(e, reads, writes)
        if not hasattr(self, 'cc_sem'):
            self.cc_sem = self.stack.enter_context(self.nc.semaphore("cc_sem"))
            self.cc_cnt = 0
        if self.cc_cnt > 0:
            self._need(e, (self.cc_sem, self.cc_cnt, 'dma'))
        self.cc_cnt += 1
        tok = (self.cc_sem, self.cc_cnt, 'dma')
        self.q[e].append(('op', lambda eng: eng.collective_compute(
            kind, ALU.bypass, groups, ins=[src.opt()], outs=[dst.opt()]), self.cc_sem, 1))
        self._commit(tok, reads, writes)
        return tok

    def wait_all(self, e, keys):
        for k in keys:
            self._need(e, self.last_w.get(k))

    def barrier(self):
        toks = []
        for f in self.engs:
            if self.cnt[f] > 0:
                toks.append((self._sem(f), self.cnt[f], f))
        for q, sems in self.dma_sems.items():
            for i, sem in enumerate(sems):
                if self.dma_cnt[q][i] > 0:
                    toks.append((sem, self.dma_cnt[q][i] * 16, 'dma'))
        if getattr(self, 'cc_cnt', 0) > 0:
            toks.append((self.cc_sem, self.cc_cnt, 'dma'))
        for e in self.engs:
            for t in toks:
                if t[2] != e:
                    self._need(e, t)

    def emit(self):
        nc = self.nc
        with nc.Block() as block:
            def replay(name, eng):
                for it in self.q[name]:
                    if it[0] == 'wait':
                        eng.wait_ge(it[1], it[2])
                    else:
                        it[1](eng).then_inc(it[2], it[3])

            @block.sync
            def _(eng):
                replay('sp', eng)

            @block.scalar
            def _(eng):
                replay('act', eng)

            @block.vector
            def _(eng):
                replay('dve', eng)

            @block.gpsimd
            def _(eng):
                replay('pool', eng)

            @block.tensor
            def _(eng):
                replay('pe', eng)
        for e in self.engs:
            self.q[e] = []


class Ctx:
    def __init__(self, nc, st, k=None, prefix=""):
        self.nc = nc
        self.st = st
        self.k = k if k is not None else K(nc, st)
        self.n = 0
        self.outkeys = []
        self.prefix = prefix

    def sb(self, name, shape, dt):
        return self.st.enter_context(self.nc.sbuf_tensor("s_" + self.prefix + name, list(shape), dt))

    def ps(self, name, shape, dt):
        return self.st.enter_context(self.nc.psum_tensor("p_" + self.prefix + name, list(shape), dt))


def emit_mod(cx, w_ada, b_ada, cT_d, ng_d, banks, need_gate):
    k = cx.k
    ncol = 3 if need_gate else 2
    cT = cx.sb("cT", [128, 8], F32)
    ng = cx.sb("ng", [128, 8], F32)
    modrow = cx.sb("modrow", [1, 3072], F32)
    one11 = cx.sb("one11", [1, 128], F32)
    s1 = cx.sb("s1", [128, 8], F32)
    s2 = cx.sb("s2", [128, 8], F32)
    gate_bc = cx.sb("gate_bc", [128, 1024], F32) if need_gate else None
    NWA = 4
    wa = [cx.sb(f"wa{i}", [128, 512], F32) for i in range(NWA)]
    k.dma('sp', cT[:], cT_d, writes=['cT'])
    k.dma('sp', ng[:], ng_d, writes=['ng'])
    k.op('dve', lambda e: e.memset(one11[:], 1.0), writes=['one11'])
    ngrp = ncol * 2
    i = 0
    for kc in range(8):
        for cg in range(ngrp):
            buf = wa[i % NWA]
            bk = f"wa{i % NWA}"
            i += 1
            k.dma('sp', buf[:], w_ada[kc * 128:(kc + 1) * 128, cg * 512:(cg + 1) * 512], writes=[bk])
            k.op('pe', lambda e, buf=buf, cg=cg, kc=kc: e.matmul(
                banks[cg][0:1, :], lhsT=cT[:, kc:kc + 1], rhs=buf[:], start=(kc == 0), stop=(kc == 7)),
                reads=[bk, 'cT'], writes=[f"bank{cg}"])
    for cg in range(ngrp):
        buf = wa[i % NWA]
        bk = f"wa{i % NWA}"
        i += 1
        k.dma('sp', buf[0:1, :], b_ada[0:1, cg * 512:(cg + 1) * 512], writes=[bk])
        k.op('dve', lambda e, cg=cg, buf=buf: e.tensor_tensor(
            out=modrow[0:1, cg * 512:(cg + 1) * 512], in0=banks[cg][0:1, :],
            in1=buf[0:1, :], op=ALU.add),
            reads=[f"bank{cg}", bk], writes=['modrow'])
    colb = banks[6]
    for cc in range(8):
        k.op('pe', lambda e, cc=cc: e.matmul(
            colb[:, cc:cc + 1], lhsT=modrow[0:1, 1024 + cc * 128:1024 + (cc + 1) * 128],
            rhs=one11[0:1, 0:1], start=True, stop=True), reads=['modrow', 'one11'], writes=['bank6'])
        k.op('pe', lambda e, cc=cc: e.matmul(
            colb[:, 8 + cc:9 + cc], lhsT=modrow[0:1, cc * 128:(cc + 1) * 128],
            rhs=one11[0:1, 0:1], start=True, stop=True), reads=['modrow', 'one11'], writes=['bank6'])
    k.op('dve', lambda e: e.scalar_tensor_tensor(
        out=s1[:], in0=colb[:, 0:8], scalar=1.0, in1=ng[:], op0=ALU.add, op1=ALU.mult),
        reads=['bank6', 'ng'], writes=['s1'])
    k.op('dve', lambda e: e.tensor_copy(out=s2[:], in_=colb[:, 8:16]), reads=['bank6'], writes=['s2'])
    if need_gate:
        for hh in range(2):
            k.op('pe', lambda e, hh=hh: e.matmul(
                banks[hh][:, :], lhsT=one11[0:1, 0:128], rhs=modrow[0:1, 2048 + hh * 512:2048 + (hh + 1) * 512],
                start=True, stop=True), reads=['modrow', 'one11'], writes=[f"bank{hh}"])
            k.op('dve', lambda e, hh=hh: e.tensor_copy(out=gate_bc[:, hh * 512:(hh + 1) * 512], in_=banks[hh][:, :]),
                 reads=[f"bank{hh}"], writes=['gate_bc'])
    return s1, s2, gate_bc


def emit_norm_tile(cx, xt, xkey, tb, s1, s2, ident, tp, tpkey, hT, hkey, tagn):
    k = cx.k
    i = cx.n
    cx.n += 1
    r = i % 2
    if not hasattr(cx, 'nrm'):
        cx.nrm = dict(
            sq=[cx.sb("nsq", [128, 1024], BF16)] * 2,
            st=[cx.sb(f"nst{j}", [128, 4], F32) for j in range(2)],
            xn=[cx.sb(f"nxn{j}", [128, 1024], BF16) for j in range(2)],
            nf=cx.sb("nrm_nf", [128, 8, 128], F32),
        )
    sq, stt, xn = cx.nrm['sq'][r], cx.nrm['st'][r], cx.nrm['xn'][r]
    ksq, kst, kxn = "nsq", f"nst{r}", f"nxn{r}"
    k.op('pool', lambda e: e.memset(stt[:], 0.0), writes=[kst])
    k.op('act', lambda e: e.activation(out=sq[:], in_=xt, func=AF.Square, accum_out=stt[:, 0:1]),
         reads=[xkey, kst], writes=[ksq, kst])
    k.op('act', lambda e: e.activation(out=stt[:, 1:2], in_=stt[:, 0:1], func=AF.Ln, scale=1.0 / D, bias=EPS),
         reads=[kst], writes=[kst])
    k.op('act', lambda e: e.activation(out=stt[:, 2:3], in_=stt[:, 1:2], func=AF.Exp, scale=-0.5),
         reads=[kst], writes=[kst])
    k.op('dve', lambda e: e.tensor_scalar(out=xn[:], in0=xt, scalar1=stt[:, 2:3], scalar2=None, op0=ALU.mult),
         reads=[xkey, kst], writes=[kxn])
    for kc in range(8):
        k.op('pe', lambda e, kc=kc: e.transpose(tp[:, kc * 128:(kc + 1) * 128], xn[:, kc * 128:(kc + 1) * 128], ident[:]),
             reads=[kxn, 'ident'], writes=[tpkey])
    hv = hT[:, :, tb * 128:(tb + 1) * 128]
    tpv = tp[:, :].rearrange("p (k t) -> p k t", k=8)
    nf = cx.nrm['nf']
    k.op('dve', lambda e: e.tensor_tensor(out=nf[:, :, :], in0=tpv, in1=s1[:, :].unsqueeze(2).broadcast_to([128, 8, 128]),
                                          op=ALU.mult), reads=[tpkey, 's1'], writes=['nrm_nf'])
    k.op('dve', lambda e: e.tensor_tensor(out=hv, in0=nf[:, :, :], in1=s2[:, :].unsqueeze(2).broadcast_to([128, 8, 128]),
                                          op=ALU.add), reads=['nrm_nf', 's2'], writes=[hkey])


NTB = 2048


def emit_B(cx, last, x_d, yT_d, cT_d, ng_d, wada_d, bada_d, wg_d, gb_d, wbr_d, wout_d, fg_d, id_d, xo_d, x_reads=(), y_reads=()):
    k = cx.k
    nc = cx.nc
    banks = [cx.ps(f"bank{i}", [128, 512], F32) for i in range(7)]
    tp = cx.ps("tpb", [128, 1024], BF16)
    ident = cx.sb("ident", [128, 128], BF16)
    k.dma('pool', ident[:], id_d, writes=['ident'])
    s1, s2, gate_bc = emit_mod(cx, wada_d, bada_d, cT_d, ng_d, banks, True)
    wg = cx.sb("wg", [128, 8, 3 * D], BF16)
    wbr = cx.sb("wbr", [128, 8, D], BF16)
    wout = cx.sb("wout", [128, 8, D], BF16)
    gb = cx.sb("gb", [128, 24], F32)
    k.dma('sp', gb[:], gb_d, writes=['gb'])
    for kc in range(8):
        k.dma('pool', wg[:, kc, :], wg_d[kc * 128:(kc + 1) * 128, :], writes=['wg'])
    for kc in range(8):
        k.dma('pool', wbr[:, kc, :], wbr_d[kc * 128:(kc + 1) * 128, :], writes=['wbr'])
    for kc in range(8):
        k.dma('pool', wout[:, kc, :], wout_d[kc * 128:(kc + 1) * 128, :], writes=['wout'])
    if last:
        fg_bc = cx.sb("fg_bc", [128, D], F32)
        k.dma('sp', fg_bc[:], fg_d.partition_broadcast(128), writes=['fg_bc'])
    xres = cx.sb("xres", [128, 4, D], F32)
    hT = [cx.sb(f"hT{i}", [128, 8, 512], BF16) for i in range(2)]
    yT = [cx.sb(f"yT{i}", [128, 8, 512], BF16) for i in range(2)]
    mT = cx.sb("mT", [128, 8, 512], BF16)
    sig = [cx.sb(f"sig{i}", [128, 512], F32) for i in range(3)]
    tmp = [cx.sb(f"tmp{i}", [128, 512], F32) for i in range(3)]
    xn_o = [cx.sb(f"xno{i}", [128, D], F32) for i in range(2)]
    fst = [cx.sb(f"fst{i}", [128, 4], F32) for i in range(2)]
    yo = [cx.sb(f"yo{i}", [128, D], F32) for i in range(2)]
    fsq = cx.sb("fsq", [128, D], BF16)
    nG = 0
    nP = 0
    nO = 0
    ntile = NTB // 512
    for tl in range(ntile):
        r = tl % 2
        for (kk, p0, pn, src) in yT_d(tl):
            k.dma('sp', yT[r][p0:p0 + pn, kk, :], src, reads=y_reads, writes=[f"yT{r}_{kk}_{p0}"])
        for tb in range(4):
            t0 = tl * 512 + tb * 128
            k.dma('sp', xres[:, tb, :], x_d(t0), reads=x_reads, writes=[f"xres{tb}"])
            emit_norm_tile(cx, xres[:, tb, :], f"xres{tb}", tb, s1, s2, ident, tp, 'tpb', hT[r], f"hT{r}", 'b')
        for dc in range(8):
            for gi in range(3):
                gbk = 0 + (nG % 2)
                nG += 1
                for kc in range(8):
                    k.op('pe', lambda e, gbk=gbk, gi=gi, kc=kc, dc=dc, r=r: e.matmul(
                        banks[gbk][:, :], lhsT=wg[:, kc, gi * D + dc * 128:gi * D + (dc + 1) * 128],
                        rhs=hT[r][:, kc, :], start=(kc == 0), stop=(kc == 7)),
                        reads=['wg', f"hT{r}"], writes=[f"bank{gbk}"])
                k.op('act', lambda e, gbk=gbk, gi=gi, dc=dc: e.activation(
                    out=sig[gi][:], in_=banks[gbk][:, :], func=AF.Sigmoid,
                    bias=gb[:, gi * 8 + dc:gi * 8 + dc + 1], scale=1.0),
                    reads=[f"bank{gbk}", 'gb'], writes=[f"sig{gi}"])
            for bi, (k0, k1) in enumerate(((0, 4), (4, 6), (6, 8))):
                pbk = 2 + (nP % 2)
                nP += 1
                for kc in range(k0, k1):
                    k.op('pe', lambda e, pbk=pbk, kc=kc, dc=dc, r=r, k0=k0, k1=k1: e.matmul(
                        banks[pbk][:, :], lhsT=wbr[:, kc, dc * 128:(dc + 1) * 128],
                        rhs=yT[r][:, kc, :], start=(kc == k0), stop=(kc == k1 - 1)),
                        reads=['wbr'] + [f"yT{r}_{kc}_{p0}" for p0 in (0, 64)], writes=[f"bank{pbk}"])
                k.op('dve', lambda e, pbk=pbk, bi=bi: e.tensor_tensor(
                    out=tmp[bi][:], in0=banks[pbk][:, :], in1=sig[bi][:], op=ALU.mult),
                    reads=[f"bank{pbk}", f"sig{bi}"], writes=[f"tmp{bi}"])
            k.op('pool', lambda e: e.tensor_tensor(out=tmp[0][:], in0=tmp[0][:], in1=tmp[1][:], op=ALU.add),
                 reads=['tmp0', 'tmp1'], writes=['tmp0'])
            k.op('pool', lambda e, dc=dc: e.tensor_tensor(out=mT[:, dc, :], in0=tmp[0][:], in1=tmp[2][:], op=ALU.add),
                 reads=['tmp0', 'tmp2'], writes=['mT'])
        for tb in range(4):
            t0 = tl * 512 + tb * 128
            ro = nO % 2
            nO += 1
            for ch in range(2):
                obk = 4 + ch
                for kc in range(8):
                    k.op('pe', lambda e, obk=obk, kc=kc, tb=tb, ch=ch: e.matmul(
                        banks[obk][:, :], lhsT=mT[:, kc, tb * 128:(tb + 1) * 128],
                        rhs=wout[:, kc, ch * 512:(ch + 1) * 512], start=(kc == 0), stop=(kc == 7)),
                        reads=['mT', 'wout'], writes=[f"bank{obk}"])
                k.op('dve', lambda e, obk=obk, ch=ch, ro=ro: e.tensor_tensor(
                    out=xn_o[ro][:, ch * 512:(ch + 1) * 512], in0=banks[obk][:, :],
                    in1=gate_bc[:, ch * 512:(ch + 1) * 512], op=ALU.mult),
                    reads=[f"bank{obk}", 'gate_bc'], writes=[f"xno{ro}"])
            k.op('pool', lambda e, ro=ro, tb=tb: e.tensor_tensor(
                out=xn_o[ro][:], in0=xn_o[ro][:], in1=xres[:, tb, :], op=ALU.add),
                reads=[f"xno{ro}", f"xres{tb}"], writes=[f"xno{ro}"])
            if not last:
                k.dma('sp', xo_d[t0:t0 + 128, :], xn_o[ro][:], reads=[f"xno{ro}"], writes=[f"xo{t0}"])
                cx.outkeys.append(f"xo{t0}")
            else:
                k.op('pool', lambda e, ro=ro: e.memset(fst[ro][:], 0.0), writes=[f"fst{ro}"])
                k.op('act', lambda e, ro=ro: e.activation(out=fsq[:], in_=xn_o[ro][:], func=AF.Square,
                                                           accum_out=fst[ro][:, 0:1]),
                     reads=[f"xno{ro}", f"fst{ro}"], writes=['fsq', f"fst{ro}"])
                k.op('act', lambda e, ro=ro: e.activation(out=fst[ro][:, 1:2], in_=fst[ro][:, 0:1], func=AF.Ln,
                                                           scale=1.0 / D, bias=EPS),
                     reads=[f"fst{ro}"], writes=[f"fst{ro}"])
                k.op('act', lambda e, ro=ro: e.activation(out=fst[ro][:, 2:3], in_=fst[ro][:, 1:2], func=AF.Exp, scale=-0.5),
                     reads=[f"fst{ro}"], writes=[f"fst{ro}"])
                k.op('dve', lambda e, ro=ro: e.scalar_tensor_tensor(
                    out=yo[ro][:], in0=xn_o[ro][:], scalar=fst[ro][:, 2:3], in1=fg_bc[:],
                    op0=ALU.mult, op1=ALU.mult), reads=[f"xno{ro}", f"fst{ro}", 'fg_bc'], writes=[f"yo{ro}"])
                k.dma('sp', xo_d[t0:t0 + 128, :], yo[ro][:], reads=[f"yo{ro}"], writes=[f"xo{t0}"])
                cx.outkeys.append(f"xo{t0}")


def emit_A(cx, a, yT_o, ntile):
    import os
    STAGE = int(os.environ.get('A_STAGE', '9'))
    SUB = int(os.environ.get('A_SUB', '9'))
    DIS = os.environ.get('A_DIS', '')
    k = cx.k
    banks = [cx.ps(f"bank{i}", [128, 512], F32) for i in range(7)]
    tp = cx.ps("tpb", [128, 1024], BF16)
    ZB = (0, 1)
    LB = (2, 3)
    OB = 4
    PB = 5
    MB = 6
    ident = cx.sb("ident", [128, 128], BF16)
    k.dma('pool', ident[:], a['ident'], writes=['ident'])
    s1, s2, _ = emit_mod(cx, a['w_ada'], a['b_ada'], a['cT'], a['ng'], banks, False)
    wtm = cx.sb("wtm", [128, 8, 512], BF16)
    wfm = cx.sb("wfm", [128, 8, 512], BF16)
    wgt = cx.sb("wgt", [128, 8, 16], BF16)
    for kc in range(8):
        k.dma('pool', wtm[:, kc, :], a['wtm'][kc * 128:(kc + 1) * 128, :], writes=['wtm'])
        k.dma('pool', wfm[:, kc, :], a['wfm'][kc * 128:(kc + 1) * 128, :], writes=['wfm'])
        k.dma('pool', wgt[:, kc, :], a['wgt'][kc * 128:(kc + 1) * 128, :], writes=['wgt'])
    mgb = cx.sb("mgb", [128, 2], F32)
    nmgb = cx.sb("nmgb", [128, 2], F32)
    cw = cx.sb("cw", [128, 8], F32)
    cb = cx.sb("cb", [128, 2], F32)
    mng = cx.sb("mng", [128, 128], F32)
    poolw = cx.sb("poolw", [64, 64], BF16)
    pscale = cx.sb("pscale", [64, 1], F32)
    bands = cx.sb("bands", [128, 3, 128], BF16)
    mtri = cx.sb("mtri", [128, 128], F32)
    onesf = cx.sb("onesf", [128, 128], F32)
    sbm = cx.sb("sbm", [128, 4, 512], BF16)
    nui = cx.sb("nui", [128, 128], BF16)
    nones = cx.sb("nones", [128, 128], BF16)
    k.dma('sp', mgb[:], a['mgb'], writes=['mgb'])
    k.dma('sp', cw[:], a['cw'], writes=['cw'])
    k.dma('sp', cb[:], a['cb'], writes=['cb'])
    k.dma('sp', mng[:], a['mng'].partition_broadcast(128), writes=['mng'])
    k.dma('pool', poolw[:], a['poolw'], writes=['poolw'])
    k.dma('sp', pscale[:], a['pscale'], writes=['pscale'])
    for i in range(3):
        k.dma('pool', bands[:, i, :], a['bands'][i], writes=['bands'])
    k.dma('sp', mtri[:], a['mtri'], writes=['mtri'])
    for i in range(4):
        k.dma('pool', sbm[:, i, :], a['sbm'][i], writes=['sbm'])
    k.dma('pool', nui[:], a['nui'], writes=['nui'])
    k.op('dve', lambda e: e.memset(onesf[:], 1.0), writes=['onesf'])
    k.op('dve', lambda e: e.memset(nones[:], -1.0), writes=['nones'])
    k.op('dve', lambda e: e.tensor_scalar(out=nmgb[:], in0=mgb[:], scalar1=-1.0, scalar2=None, op0=ALU.mult),
         reads=['mgb'], writes=['nmgb'])
    sqT = cx.sb("sqT", [64, SEQ], BF16)
    skT = cx.sb("skT", [64, SEQ], BF16)
    SV = cx.sb("SV", [128, SEQ // 128, 64], BF16)
    Cn32 = cx.sb("Cn32", [128, 132], F32)
    Cnb = cx.sb("Cnb", [128, 132], BF16)
    k.op('dve', lambda e: e.memset(Cn32[:], 0.0), writes=['Cn32'])
    k.op('dve', lambda e: e.memset(Cnb[:], 0.0), writes=['Cnb'])
    xt = [cx.sb(f"xt{i}", [128, D], F32) for i in range(2)]
    hT = [cx.sb(f"hT{i}", [128, 8, 512], BF16) for i in range(2)]
    qkr = [cx.sb(f"qkr{i}", [128, 516], F32) for i in range(2)]
    for g in range(2):
        k.op('pool', lambda e, g=g: e.memset(qkr[g][:], 0.0), writes=[f"qkr{g}"])
    cacc = [cx.sb(f"cacc{i}", [128, 512], F32) for i in range(2)]
    csg = [cx.sb(f"csg{i}", [128, 512], F32) for i in range(2)]
    qT = cx.sb("qT", [128, 512], BF16)
    kT = cx.sb("kT", [128, 512], BF16)
    spz = cx.sb("spz", [64, 512], F32)
    ssz = cx.sb("ssz", [64, 512], F32)
    sgz = cx.sb("sgz", [64, 512], F32)
    Ub = [cx.sb(f"Ub{i}", [128, 64], BF16) for i in range(3)]
    tmS = [cx.sb(f"tmS{i}", [128, 512], F32) for i in range(2)]
    gsb = [cx.sb(f"gsb{i}", [128, 16], F32) for i in range(2)]
    sgo = [cx.sb(f"sgo{i}", [128, 256], F32) for i in range(2)]
    gz = [cx.sb(f"gz{i}", [128, 128], F32) for i in range(2)]
    V2 = [cx.sb(f"V2{i}", [128, 132], BF16) for i in range(2)]
    Ktm = [cx.sb(f"Ktm{i}", [128, 128], BF16) for i in range(2)]
    for i in range(2):
        k.op('pool', lambda e, i=i: e.memset(V2[i][:], 0.0), writes=[f"V2{i}"])
    Sm = [cx.sb(f"Sm{i}", [128, 128], BF16) for i in range(2)]
    t1 = [cx.sb(f"t1{i}", [128, 128], F32) for i in range(2)]
    t1sq = cx.sb("t1sq", [128, 128], BF16)
    ymb = [cx.sb(f"ymb{i}", [128, 128], BF16) for i in range(2)]
    ymT = [cx.sb(f"ymT{i}", [128, 512], BF16) for i in range(2)]
    pTs = cx.sb("pTs", [64, 512], BF16)
    ypT = [cx.sb(f"ypT{i}", [64, 512], BF16) for i in range(2)]
    ysT = [cx.sb(f"ysT{i}", [64, 512], BF16) for i in range(2)]
    Eb = [cx.sb(f"Eb{i}", [128, 512], F32) for i in range(2)]
    L32 = cx.sb("L32", [128, 512], F32)
    Lb = [cx.sb(f"Lb{i}", [128, 512], BF16) for i in range(2)]
    S32 = cx.sb("S32", [128, 512], F32)
    Sb = [cx.sb(f"Sb{i}", [128, 512], BF16) for i in range(2)]
    At = [cx.sb(f"At{i}", [128, 512], BF16) for i in range(2)]
    Am = [cx.sb(f"Am{i}", [128, 512], BF16) for i in range(2)]
    mb = banks[MB]
    cnt = dict(z=0, l=0, u=0, g=0, p=0)
    PR = [PB, 0, 1, 2, 3]

    def nextpb():
        i = PR[cnt['p'] % len(PR)]
        cnt['p'] += 1
        return banks[i], f"bank{i}"
    KSCALE = 128.0 ** -0.5
    nx = 0
    for tl in range(ntile if STAGE >= 2 else 0):
        r = tl % 2
        for tb in range(4):
            t0 = tl * 512 + tb * 128
            xr = nx % 2
            nx += 1
            k.dma('sp', xt[xr][:], a['x'](t0), reads=a.get('xreads', ()), writes=[f"xt{xr}"])
            emit_norm_tile(cx, xt[xr][:], f"xt{xr}", tb, s1, s2, ident, tp, 'tpb', hT[r], f"hT{r}", 'a')
        hk = f"hT{r}"
        for g in range(2):
            pb, pk = nextpb()
            for kc in range(8):
                k.op('pe', lambda e, g=g, kc=kc, r=r: e.matmul(
                    pb[:, :], lhsT=wfm[:, kc, g * 128:(g + 1) * 128], rhs=hT[r][:, kc, :],
                    start=(kc == 0), stop=(kc == 7)), reads=['wfm', hk], writes=[pk])
            k.op('pool', lambda e, g=g: e.tensor_copy(out=qkr[g][:, 0:3], in_=qkr[g][:, 512:515]),
                 reads=[f"qkr{g}"], writes=[f"qkr{g}h"])
            k.op('act', lambda e, g=g: e.activation(out=qkr[g][:, 3:515], in_=pb[:, :], func=AF.Identity),
                 reads=[pk, f"qkr{g}h"], writes=[f"qkr{g}"])
            k.op('pool', lambda e, g=g: e.tensor_scalar(
                out=cacc[g][:], in0=qkr[g][:, 0:512], scalar1=cw[:, 4 * g:4 * g + 1], scalar2=cb[:, g:g + 1],
                op0=ALU.mult, op1=ALU.add), reads=[f"qkr{g}", f"qkr{g}h", 'cw', 'cb'], writes=[f"cacc{g}"])
            for j in range(1, 4):
                k.op('dve', lambda e, g=g, j=j: e.scalar_tensor_tensor(
                    out=cacc[g][:], in0=qkr[g][:, j:j + 512], scalar=cw[:, 4 * g + j:4 * g + j + 1],
                    in1=cacc[g][:], op0=ALU.mult, op1=ALU.add),
                    reads=[f"qkr{g}", f"qkr{g}h", f"cacc{g}", 'cw'], writes=[f"cacc{g}"])
            k.op('act', lambda e, g=g: e.activation(out=csg[g][:], in_=cacc[g][:], func=AF.Sigmoid),
                 reads=[f"cacc{g}"], writes=[f"csg{g}"])
            dst, dk, scl = (qT, 'qT', 1.0) if g == 0 else (kT, 'kT', KSCALE)
            k.op('dve', lambda e, g=g, dst=dst, scl=scl: e.scalar_tensor_tensor(
                out=dst[:], in0=cacc[g][:], scalar=scl, in1=csg[g][:], op0=ALU.mult, op1=ALU.mult),
                reads=[f"cacc{g}", f"csg{g}"], writes=[dk])
        for i4, nm in enumerate(('pz', 'sz', 'sq', 'sk')):
            c0 = 256 + i4 * 64
            pb, pk = nextpb()
            for kc in range(8):
                k.op('pe', lambda e, kc=kc, r=r, c0=c0: e.matmul(
                    pb[0:64, :], lhsT=wfm[:, kc, c0:c0 + 64], rhs=hT[r][:, kc, :],
                    start=(kc == 0), stop=(kc == 7)), reads=['wfm', hk], writes=[pk])
            if nm in ('pz', 'sz'):
                dst, dk = (spz, 'spz') if nm == 'pz' else (ssz, 'ssz')
                k.op('act', lambda e: e.activation(out=sgz[:], in_=pb[0:64, :], func=AF.Sigmoid),
                     reads=[pk], writes=['sgz'])
                k.op('dve', lambda e, dst=dst: e.tensor_tensor(out=dst[:], in0=pb[0:64, :], in1=sgz[:], op=ALU.mult),
                     reads=[pk, 'sgz'], writes=[dk])
            elif nm == 'sq':
                k.op('act', lambda e, tl=tl: e.activation(out=sqT[:, tl * 512:(tl + 1) * 512], in_=pb[0:64, :],
                                                           func=AF.Identity, scale=0.125),
                     reads=[pk], writes=[f"sqT{tl}"])
            else:
                k.op('dve', lambda e, tl=tl: e.tensor_copy(out=skT[:, tl * 512:(tl + 1) * 512], in_=pb[0:64, :]),
                     reads=[pk], writes=[f"skT{tl}"])
        if STAGE < 3:
            continue
        for tb in range(4):
            n = tl * 4 + tb
            tsl = slice(tb * 128, (tb + 1) * 128)
            pb, pk = nextpb()
            for kc in range(8):
                k.op('pe', lambda e, kc=kc, r=r, tsl=tsl: e.matmul(
                    pb[:, :], lhsT=hT[r][:, kc, tsl], rhs=wtm[:, kc, :], start=(kc == 0), stop=(kc == 7)),
                    reads=['wtm', hk], writes=[pk])
            for kc in range(8):
                k.op('pe', lambda e, kc=kc, r=r, tsl=tsl: e.matmul(
                    mb[:, 400:416], lhsT=hT[r][:, kc, tsl], rhs=wgt[:, kc, :], start=(kc == 0), stop=(kc == 7)),
                    reads=['wgt', hk], writes=['bank6'])
            gi = cnt['g'] % 2
            cnt['g'] += 1
            G = gsb[gi]
            gk = f"gsb{gi}"
            ts = tmS[n % 2]
            tk = f"tmS{n % 2}"
            k.op('act', lambda e, ts=ts, pb=pb: e.activation(out=ts[:], in_=pb[:, :], func=AF.Identity),
                 reads=[pk], writes=[tk])
            k.op('pool', lambda e, n=n, ts=ts: e.tensor_copy(out=SV[:, n, :], in_=ts[:, 448:512]),
                 reads=[tk], writes=[f"SV{n}"])
            ui = n % 3
            k.op('pool', lambda e, ui=ui, ts=ts: e.tensor_copy(out=Ub[ui][:], in_=ts[:, 384:448]),
                 reads=[tk], writes=[f"Ub{ui}"])
            k.op('act', lambda e, gi=gi, ts=ts: e.activation(out=sgo[gi][:], in_=ts[:, 128:384], func=AF.Sigmoid),
                 reads=[tk], writes=[f"sgo{gi}"])
            k.op('dve', lambda e, gi=gi, ts=ts: e.tensor_tensor(out=gz[gi][:], in0=ts[:, 256:384], in1=sgo[gi][:, 128:256], op=ALU.mult),
                 reads=[tk, f"sgo{gi}"], writes=[f"gz{gi}"])
            k.op('pool', lambda e, gi=gi: e.tensor_tensor(out=gz[gi][:], in0=gz[gi][:], in1=mng[:], op=ALU.mult),
                 reads=[f"gz{gi}", 'mng'], writes=[f"gz{gi}"])
            if SUB < 1:
                continue
            k.op('act', lambda e, G=G: e.activation(out=G[:, 0:2], in_=mb[:, 400:402], func=AF.Identity),
                 reads=['bank6'], writes=[gk])
            k.op('act', lambda e, G=G: e.activation(out=G[:, 2:3], in_=G[:, 1:2], func=AF.Exp, scale=-1.0, bias=nmgb[:, 1:2]),
                 reads=[gk, 'nmgb'], writes=[gk])
            k.op('act', lambda e, G=G: e.activation(out=G[:, 3:4], in_=G[:, 2:3], func=AF.Ln, scale=1.0, bias=1.0),
                 reads=[gk], writes=[gk])
            k.op('pe', lambda e, G=G: e.matmul(mb[:, 404:405], lhsT=mtri[:], rhs=G[:, 3:4], start=True, stop=True),
                 reads=[gk, 'mtri'], writes=['bank6'])
            k.op('pe', lambda e, G=G: e.matmul(mb[:, 405:406], lhsT=onesf[:], rhs=G[:, 3:4], start=True, stop=True),
                 reads=[gk, 'onesf'], writes=['bank6'])
            k.op('act', lambda e, G=G: e.activation(out=G[:, 4:6], in_=mb[:, 404:406], func=AF.Exp, scale=-1.0),
                 reads=['bank6'], writes=[gk])
            k.op('dve', lambda e, G=G: e.tensor_tensor(out=G[:, 6:7], in0=mb[:, 404:405], in1=G[:, 0:1], op=ALU.add),
                 reads=['bank6', gk], writes=[gk])
            k.op('act', lambda e, G=G: e.activation(out=G[:, 7:8], in_=G[:, 6:7], func=AF.Exp, scale=1.0, bias=mgb[:, 0:1]),
                 reads=[gk, 'mgb'], writes=[gk])
            if SUB < 2:
                continue
            vi = n % 2
            k.op('dve', lambda e, vi=vi, G=G, ts=ts: e.tensor_scalar(out=V2[vi][:, 0:128], in0=ts[:, 0:128], scalar1=G[:, 7:8],
                                                              scalar2=None, op0=ALU.mult),
                 reads=[tk, gk], writes=[f"V2{vi}"])
            k.op('dve', lambda e, vi=vi, G=G: e.tensor_copy(out=V2[vi][:, 128:129], in_=G[:, 7:8]),
                 reads=[gk], writes=[f"V2{vi}"])
            k.op('pe', lambda e, tsl=tsl: e.transpose(tp[:, 0:128], kT[:, tsl], ident[:]),
                 reads=['kT', 'ident'], writes=['tpb'])
            k.op('dve', lambda e, vi=vi: e.tensor_copy(out=Ktm[vi][:], in_=tp[:, 0:128]),
                 reads=['tpb'], writes=[f"Ktm{vi}"])
            k.op('pe', lambda e, tsl=tsl: e.matmul(mb[:, 0:128], lhsT=kT[:, tsl], rhs=qT[:, tsl], start=True, stop=True),
                 reads=['kT', 'qT'], writes=['bank6'])
            k.op('dve', lambda e, vi=vi: e.tensor_tensor(out=Sm[vi][:], in0=mb[:, 0:128], in1=mtri[:], op=ALU.mult),
                 reads=['bank6', 'mtri'], writes=[f"Sm{vi}"])
            if SUB < 3:
                continue
            k.op('pe', lambda e, tsl=tsl: e.matmul(mb[:, 128:258], lhsT=qT[:, tsl], rhs=Cnb[:, 0:130], start=True, stop=False),
                 reads=['qT', 'Cnb'], writes=['bank6'])
            k.op('pe', lambda e, vi=vi: e.matmul(mb[:, 128:258], lhsT=Sm[vi][:], rhs=V2[vi][:, 0:130], start=False, stop=True),
                 reads=[f"Sm{vi}", f"V2{vi}"], writes=['bank6'])
            k.op('pe', lambda e, vi=vi: e.matmul(mb[:, 260:390], lhsT=Ktm[vi][:], rhs=V2[vi][:, 0:130], start=True, stop=True),
                 reads=[f"Ktm{vi}", f"V2{vi}"], writes=['bank6'])
            k.op('dve', lambda e: e.tensor_tensor(out=Cn32[:, 0:130], in0=mb[:, 260:390], in1=Cn32[:, 0:130], op=ALU.add),
                 reads=['bank6', 'Cn32'], writes=['Cn32'])
            k.op('dve', lambda e, G=G: e.tensor_scalar(out=Cn32[:, 0:130], in0=Cn32[:, 0:130], scalar1=G[:, 5:6],
                                                       scalar2=None, op0=ALU.mult),
                 reads=['Cn32', gk], writes=['Cn32'])
            k.op('pool', lambda e: e.tensor_copy(out=Cnb[:, 0:130], in_=Cn32[:, 0:130]),
                 reads=['Cn32'], writes=['Cnb'])
            if SUB < 4:
                continue
            k.op('dve', lambda e, G=G: e.tensor_tensor(out=G[:, 8:9], in0=mb[:, 256:257], in1=G[:, 4:5], op=ALU.mult),
                 reads=['bank6', gk], writes=[gk])
            k.op('dve', lambda e, G=G: e.tensor_tensor(out=G[:, 8:9], in0=G[:, 8:9], in1=G[:, 8:9], op=ALU.mult),
                 reads=[gk], writes=[gk])
            k.op('dve', lambda e, G=G: e.tensor_scalar(out=G[:, 8:9], in0=G[:, 8:9], scalar1=1.0, scalar2=None, op0=ALU.max),
                 reads=[gk], writes=[gk])
            k.op('act', lambda e, G=G: e.activation(out=G[:, 14:15], in_=G[:, 8:9], func=AF.Ln),
                 reads=[gk], writes=[gk])
            k.op('act', lambda e, G=G: e.activation(out=G[:, 9:10], in_=G[:, 14:15], func=AF.Exp, scale=-0.5),
                 reads=[gk], writes=[gk])
            k.op('dve', lambda e, G=G: e.tensor_tensor(out=G[:, 10:11], in0=G[:, 9:10], in1=G[:, 4:5], op=ALU.mult),
                 reads=[gk], writes=[gk])
            k.op('dve', lambda e, G=G, gi=gi: e.scalar_tensor_tensor(
                out=t1[gi][:], in0=mb[:, 128:256], scalar=G[:, 10:11], in1=sgo[gi][:, 0:128], op0=ALU.mult, op1=ALU.mult),
                reads=['bank6', gk, f"sgo{gi}"], writes=[f"t1{gi}"])
            k.op('pool', lambda e, G=G: e.memset(G[:, 11:12], 0.0), reads=[], writes=[gk + 'a'])
            k.op('act', lambda e, G=G, gi=gi: e.activation(out=t1sq[:], in_=t1[gi][:], func=AF.Square, accum_out=G[:, 11:12]),
                 reads=[f"t1{gi}", gk + 'a'], writes=['t1sq', gk + 'a'])
            k.op('act', lambda e, G=G: e.activation(out=G[:, 12:13], in_=G[:, 11:12], func=AF.Ln, scale=1.0 / 128, bias=EPS),
                 reads=[gk + 'a'], writes=[gk + 'b'])
            k.op('act', lambda e, G=G: e.activation(out=G[:, 13:14], in_=G[:, 12:13], func=AF.Exp, scale=-0.5),
                 reads=[gk + 'b'], writes=[gk + 'c'])
            k.op('dve', lambda e, G=G, gi=gi: e.scalar_tensor_tensor(
                out=ymb[gi][:], in0=t1[gi][:], scalar=G[:, 13:14], in1=gz[gi][:], op0=ALU.mult, op1=ALU.mult),
                reads=[f"t1{gi}", gk + 'c', f"gz{gi}"], writes=[f"ymb{gi}"])
            k.op('pe', lambda e, gi=gi: e.transpose(tp[:, 128:256], ymb[gi][:], ident[:]),
                 reads=[f"ymb{gi}", 'ident'], writes=['tpb'])
            k.op('act', lambda e, r=r, tsl=tsl: e.activation(out=ymT[r][:, tsl], in_=tp[:, 128:256], func=AF.Identity),
                 reads=['tpb'], writes=[f"ymT{r}"])
            if SUB < 5:
                continue
            bi = 0 if n == 0 else 1
            k.op('pe', lambda e, ui=ui, bi=bi, tsl=tsl, n=n: e.matmul(
                banks[OB][0:64, 0:128], lhsT=Ub[ui][:], rhs=bands[:, bi, :], start=True, stop=(n == 0)),
                reads=[f"Ub{ui}", 'bands'], writes=['bank4'])
            if n > 0:
                up = (n - 1) % 3
                k.op('pe', lambda e, up=up: e.matmul(
                    banks[OB][0:64, 0:128], lhsT=Ub[up][:], rhs=bands[:, 2, :], start=False, stop=True),
                    reads=[f"Ub{up}", 'bands'], writes=['bank4'])
            k.op('dve', lambda e, tsl=tsl: e.tensor_copy(out=pTs[:, tsl], in_=banks[OB][0:64, 0:128]),
                 reads=['bank4'], writes=['pTs'])
        if os.environ.get('A_SKIPPOST'):
            continue
        k.dma('sp', yT_o('m', tl), ymT[r][:], reads=[f"ymT{r}"], writes=[f"oym{tl}"])
        cx.outkeys.append(f"oym{tl}")
        pb, pk = nextpb()
        k.op('pe', lambda e: e.matmul(pb[0:64, :], lhsT=poolw[:], rhs=pTs[:], start=True, stop=True),
             reads=['poolw', 'pTs'], writes=[pk])
        k.op('dve', lambda e, r=r: e.scalar_tensor_tensor(
            out=ypT[r][:], in0=pb[0:64, :], scalar=pscale[:, 0:1], in1=spz[:], op0=ALU.mult, op1=ALU.mult),
            reads=[pk, 'pscale', 'spz'], writes=[f"ypT{r}"])
        k.dma('sp', yT_o('p', tl), ypT[r][:], reads=[f"ypT{r}"], writes=[f"oyp{tl}"])
        cx.outkeys.append(f"oyp{tl}")
        if STAGE < 4:
            continue
        top = 4 * tl + 3
        qsl = slice(tl * 512, (tl + 1) * 512)
        sq_keys = [f"sqT{tl}"]
        U = top + 1

        def st_Z(u):
            kb = top - u
            ci = u % 2
            zi = ZB[ci]
            zb = banks[zi]
            ksl = slice(kb * 128, (kb + 1) * 128)
            kkey = f"skT{kb // 4}"
            diag = kb >= 4 * tl
            k.op('pe', lambda e: e.matmul(zb[:, :], lhsT=skT[:, ksl], rhs=sqT[:, qsl], start=True, stop=True),
                 reads=[kkey] + sq_keys, writes=[f"bank{zi}"])
            k.op('act', lambda e: e.activation(out=Eb[ci][:], in_=zb[:, :], func=AF.Exp),
                 reads=[f"bank{zi}"], writes=[f"Eb{ci}"])
            if diag:
                k.op('act', lambda e: e.activation(out=L32[:], in_=Eb[ci][:], func=AF.Ln, scale=1.0, bias=1.0),
                     reads=[f"Eb{ci}"], writes=['L32'])
                k.op('dve', lambda e: e.tensor_tensor(out=Lb[ci][:], in0=L32[:], in1=sbm[:, kb - 4 * tl, :], op=ALU.mult),
                     reads=['L32', 'sbm'], writes=[f"Lb{ci}"])
            else:
                k.op('act', lambda e: e.activation(out=Lb[ci][:], in_=Eb[ci][:], func=AF.Ln, scale=1.0, bias=1.0),
                     reads=[f"Eb{ci}"], writes=[f"Lb{ci}"])
        def st_S(u, tl=tl, top=top):
            kb = top - u
            ci = u % 2
            if kb > 0:
                if kb == top:
                    k.op('dve', lambda e: e.tensor_copy(out=S32[:], in_=Lb[ci][:]), reads=[f"Lb{ci}"], writes=['S32'])
                else:
                    k.op('dve', lambda e: e.tensor_tensor(out=S32[:], in0=S32[:], in1=Lb[ci][:], op=ALU.add),
                         reads=['S32', f"Lb{ci}"], writes=['S32'])
                k.op('dve', lambda e: e.tensor_copy(out=Sb[1 - ci][:], in_=S32[:]), reads=['S32'], writes=[f"Sb{1 - ci}"])

        def st_L(u):
            kb = top - u
            ci = u % 2
            li = LB[ci]
            lb = banks[li]
            ksl = slice(kb * 128, (kb + 1) * 128)
            kkey = f"skT{kb // 4}"
            diag = kb >= 4 * tl
            k.op('pe', lambda e: e.matmul(lb[:, :], lhsT=skT[:, ksl], rhs=sqT[:, qsl], start=True, stop=False),
                 reads=[kkey] + sq_keys, writes=[f"bank{li}"])
            k.op('pe', lambda e: e.matmul(lb[:, :], lhsT=nui[:], rhs=Lb[ci][:], start=False, stop=(kb == top)),
                 reads=['nui', f"Lb{ci}"], writes=[f"bank{li}"])
            if kb != top:
                k.op('pe', lambda e: e.matmul(lb[:, :], lhsT=nones[:], rhs=Sb[ci][:], start=False, stop=True),
                     reads=['nones', f"Sb{ci}"], writes=[f"bank{li}"])
            k.op('act', lambda e: e.activation(out=At[ci][:], in_=lb[:, :], func=AF.Exp),
                 reads=[f"bank{li}"], writes=[f"At{ci}"])
            if diag:
                k.op('dve', lambda e: e.tensor_tensor(out=Am[ci][:], in0=At[ci][:], in1=sbm[:, kb - 4 * tl, :], op=ALU.mult),
                     reads=[f"At{ci}", 'sbm'], writes=[f"Am{ci}"])

        def st_V(u):
            kb = top - u
            ci = u % 2
            diag = kb >= 4 * tl
            asrc, akey = (Am[ci], f"Am{ci}") if diag else (At[ci], f"At{ci}")
            k.op('pe', lambda e: e.matmul(banks[OB][0:64, :], lhsT=SV[:, kb, :], rhs=asrc[:],
                                          start=(kb == top), stop=(kb == 0)),
                 reads=[f"SV{kb}", akey], writes=['bank4'])

        for step in range(U + 2):
            if step < U:
                st_Z(step)
            if 1 <= step <= U:
                st_L(step - 1)
            if step < U:
                st_S(step)
            if step >= 2:
                st_V(step - 2)
        k.op('dve', lambda e, r=r: e.tensor_tensor(out=ysT[r][:], in0=banks[OB][0:64, :], in1=ssz[:], op=ALU.mult),
             reads=['bank4', 'ssz'], writes=[f"ysT{r}"])
        k.dma('sp', yT_o('s', tl), ysT[r][:], reads=[f"ysT{r}"], writes=[f"oys{tl}"])
        cx.outkeys.append(f"oys{tl}")


def _consts():
    s = np.arange(128)
    mtri = (s[:, None] <= s[None, :]).astype(np.float32)
    nui = -(s[:, None] >= s[None, :]).astype(np.float32)
    t = np.arange(512)
    sbm = np.stack([((i * 128 + s)[:, None] < t[None, :]).astype(np.float32) for i in range(4)])
    return mtri, nui, sbm


def _bands(w):
    s = np.arange(128)
    out = np.zeros((3, 128, 128), np.float32)
    for t in range(128):
        lo = max(t + 1 - w, 0)
        out[0, lo:t + 1, t] += 1.0 / (t + 1 - lo)
        out[0, t, t] -= 1.0
        lo = max(t + 1 - w, 0)
        out[1, lo:t + 1, t] += 1.0 / w
        out[1, t, t] -= 1.0
        nprev = w - (t + 1)
        if nprev > 0:
            out[2, 128 - nprev:, t] += 1.0 / w
    return out


def hostA_inputs(inp, l, x_full):
    mtri, nui, sbm = _consts()
    ident = np.eye(128, dtype=np.float32)
    w_in = inp["w_in"][l]
    maps = []
    for core in range(8):
        b, hh = core // 4, core % 4
        c128 = slice(hh * 128, (hh + 1) * 128)
        c64 = slice(hh * 64, (hh + 1) * 64)
        off = dict(mq=0, mk=512, mv=1024, mi=1536, mf=1540, mo=1544, mz=2056, pu=2568, pz=2824,
                   sq=3080, sk=3336, sv=3592, sz=3848)
        col = lambda nm, sl: w_in[:, off[nm] + sl.start: off[nm] + sl.stop]
        wtm = np.concatenate([col('mv', c128), col('mo', c128), col('mz', c128), col('pu', c64), col('sv', c64)], 1)
        wgt = np.zeros((D, 16), np.float32)
        wgt[:, 0] = w_in[:, off['mi'] + hh]
        wgt[:, 1] = w_in[:, off['mf'] + hh]
        wfm = np.concatenate([col('mq', c128), col('mk', c128), col('pz', c64), col('sz', c64),
                              col('sq', c64), col('sk', c64)], 1)
        mg = inp["m_gate_b"][l]
        mgb = np.tile(np.array([[mg[hh], mg[4 + hh]]], np.float32), (128, 1))
        cwl = inp["conv_w"][l]
        cw = np.concatenate([cwl[:, c128].T, cwl[:, 512 + hh * 128: 512 + (hh + 1) * 128].T], 1)
        cbl = inp["conv_b"][l]
        cb = np.stack([cbl[c128], cbl[512 + hh * 128: 512 + (hh + 1) * 128]], 1)
        maps.append(dict(
            cT=np.ascontiguousarray(inp["c"][b].reshape(8, 128).T),
            ng=np.ascontiguousarray(inp["norm_g"][l].reshape(8, 128).T),
            w_ada=inp["w_ada"][l], b_ada=inp["b_ada"][l].reshape(1, -1),
            wtm=np.ascontiguousarray(wtm), wgt=np.ascontiguousarray(wgt), wfm=np.ascontiguousarray(wfm),
            mgb=mgb, cw=np.ascontiguousarray(cw), cb=np.ascontiguousarray(cb),
            mng=np.ascontiguousarray(inp["m_norm_g"][l][c128].reshape(1, 128)),
            poolw=np.ascontiguousarray(inp["pool_w"][l][hh]),
            pscale=np.ascontiguousarray(inp["pool_scale"][l][c64].reshape(64, 1)),
            bands=_bands(POOL_WINDOWS[hh]), mtri=mtri, sbm=sbm, nui=nui, ident=ident))
    return maps


def hostA_gather(results):
    out = np.zeros((NB, D, SEQ), dtype=ml_dtypes.bfloat16)
    for core in range(8):
        b, hh = core // 4, core % 4
        y = results[core]["yTo"]
        out[b, hh * 128:(hh + 1) * 128] = y[0:128]
        out[b, 512 + hh * 64:512 + (hh + 1) * 64] = y[128:192]
        out[b, 768 + hh * 64:768 + (hh + 1) * 64] = y[192:256]
    return out


def hostB_inputs(inp, l, x_full, yT_full):
    maps = []
    ident = np.eye(128, dtype=np.float32)
    wbr = np.concatenate([inp["w_br_m"][l], inp["w_br_p"][l], inp["w_br_s"][l]], 0)
    for core in range(8):
        b, j = core // 4, core % 4
        sl = slice(j * NTB, (j + 1) * NTB)
        maps.append(dict(
            cT=np.ascontiguousarray(inp["c"][b].reshape(8, 128).T),
            ng=np.ascontiguousarray(inp["norm_g"][l].reshape(8, 128).T),
            w_ada=inp["w_ada"][l], b_ada=inp["b_ada"][l].reshape(1, -1),
            wg=np.ascontiguousarray(inp["w_in"][l][:, 4104:]),
            gb=np.ascontiguousarray(inp["gate_b"][l].reshape(24, 128).T),
            wbr=wbr, wout=inp["w_out"][l], fg=inp["final_g"].reshape(1, -1), ident=ident))
    return maps


RG4 = [[0, 1, 2, 3], [4, 5, 6, 7]]
A_NAMES = dict(cT=[128, 8], ng=[128, 8], w_ada=[D, 3 * D], b_ada=[1, 3 * D], wtm=[D, 512], wgt=[D, 16],
               wfm=[D, 512], mgb=[128, 2], cw=[128, 8], cb=[128, 2], mng=[1, 128], poolw=[64, 64],
               pscale=[64, 1], bands=[3, 128, 128])
B_NAMES = dict(wg=[D, 3 * D], gb=[128, 24], wbr=[D, D], wout=[D, D])
C_NAMES = dict(mtri=[128, 128], sbm=[4, 128, 512], nui=[128, 128], ident=[128, 128], fg=[1, D])


def build_fused(ntile=SEQ // 512):
    nc = bass.Bass("TRN2", target_bir_lowering=False)
    dt_in = lambda name, shape, dt=F32: nc.dram_tensor(name, list(shape), dt, kind="ExternalInput").ap()
    x_in = dt_in("x", [SEQ, D])
    cst = {n: dt_in(n, sh) for n, sh in C_NAMES.items()}
    lay = []
    for l in range(DEPTH):
        d = {n: dt_in(f"{n}_{l}", sh) for n, sh in A_NAMES.items()}
        d.update({n: dt_in(f"{n}_{l}", sh) for n, sh in B_NAMES.items()})
        lay.append(d)
    out_d = nc.dram_tensor("out", [NTB, D], F32, kind="ExternalOutput").ap()
    internal = lambda name, shape, dt: nc.dram_tensor(name, list(shape), dt).ap()
    ys = internal("ys", [4 * 256, NTB], BF16)
    yr = internal("yr", [4 * 1024, NTB], BF16)
    yown = internal("yown", [D, NTB], BF16)
    xown = internal("xown", [NTB, D], F32)
    xs1 = internal("xs1", [NTB, D], F32)
    xg1 = internal("xg1", [8 * 1024, D], F32)
    ROW = dict(m=(0, 128), p=(128, 64), s=(192, 64))

    def ys_out(nm, tl):
        r0, nr = ROW[nm]
        q = tl // 4
        return ys[q * 256 + r0:q * 256 + r0 + nr, (tl % 4) * 512:(tl % 4 + 1) * 512]

    with contextlib.ExitStack() as outer:
        k = K(nc, outer)
        PID = nc.partition_id()

        def quarter():
            return PID % 4

        for l in range(DEPTH):
            last = (l == DEPTH - 1)
            with contextlib.ExitStack() as st:
                cx = Ctx(nc, st, k, prefix=f"A{l}_")
                a = dict(lay[l])
                a.update(cst)
                if l == 0:
                    a['x'] = lambda t0: x_in[t0:t0 + 128, :]
                else:
                    def xg_tile(t0):
                        rank, c, r0 = t0 // NTB, (t0 % NTB) // 256, t0 % 256
                        return xg1[c * 1024 + rank * 256 + r0:c * 1024 + rank * 256 + r0 + 128, :]
                    a['x'] = xg_tile
                    a['xreads'] = ['xg1']
                emit_A(cx, a, ys_out, ntile)
                akeys = list(cx.outkeys)
                k.barrier()
                k.emit()
            for q in range(4):
                k.coll("AllGather", RG4, ys[q * 256:(q + 1) * 256, :], yr[q * 1024:(q + 1) * 1024, :],
                       reads=akeys, writes=['yrecv'])
            for i in range(2):
                k.dma('sp', yown[i * 512:(i + 1) * 512, :],
                      (lambda i=i: yr.rearrange("(q r) t -> q r t", q=4)[
                          bass.ds(quarter(), 1), i * 512:(i + 1) * 512, :].squeeze(0)),
                      reads=['yrecv'], writes=[f'yown_{i}'])
            ykeys = ['yown_0', 'yown_1']
            if l == 0:
                for i in range(4):
                    k.dma('sp', xown[i * 512:(i + 1) * 512, :],
                          (lambda i=i: x_in.rearrange("(q r) d -> q r d", q=4)[
                              bass.ds(quarter(), 1), i * 512:(i + 1) * 512, :].squeeze(0)), writes=[f'xown{i}'])
            with contextlib.ExitStack() as st:
                cx = Ctx(nc, st, k, prefix=f"B{l}_")

                def yT_d(tl):
                    cs = slice(tl * 512, (tl + 1) * 512)
                    res = []
                    for kk in range(4):
                        res.append((kk, 0, 128, yown[kk * 256:kk * 256 + 128, cs]))
                    for j, r0 in ((0, 128), (1, 192)):
                        for half in range(2):
                            kk = 4 + 2 * j + half
                            for hh2 in range(2):
                                rank = 2 * half + hh2
                                res.append((kk, hh2 * 64, 64, yown[rank * 256 + r0:rank * 256 + r0 + 64, cs]))
                    return res

                if l == 0:
                    x_d = lambda t0: xown[t0:t0 + 128, :]
                    x_reads = [f'xown{i}' for i in range(4)]
                else:
                    x_d = lambda t0: xs1[t0:t0 + 128, :]
                    x_reads = ['xs1']
                d = lay[l]
                emit_B(cx, last, x_d, yT_d, d['cT'], d['ng'], d['w_ada'], d['b_ada'], d['wg'], d['gb'], d['wbr'],
                       d['wout'], cst['fg'], cst['ident'], out_d if last else xs1, x_reads=x_reads, y_reads=ykeys)
                bkeys = list(cx.outkeys)
                if last:
                    k.wait_all('sp', bkeys)
                else:
                    k.barrier()
                k.emit()
            if not last:
                for c in range(8):
                    k.coll("AllGather", RG4, xs1[c * 256:(c + 1) * 256, :], xg1[c * 1024:(c + 1) * 1024, :],
                           reads=bkeys, writes=['xg1'])
                k.last_w['xs1'] = k.last_w[bkeys[-1]]
    return nc


def host_inputs(inp):
    mtri, nui, sbm = _consts()
    ident = np.eye(128, dtype=np.float32)
    maps = [dict(mtri=mtri, nui=nui, sbm=sbm, ident=ident, fg=inp["final_g"].reshape(1, -1).astype(np.float32))
            for _ in range(8)]
    for core in range(8):
        maps[core]["x"] = np.ascontiguousarray(inp["x"][core // 4])
    dummy_x = None
    for l in range(DEPTH):
        ma = hostA_inputs(inp, l, None)
        mb = hostB_inputs(inp, l, None, None)
        for core in range(8):
            for n in A_NAMES:
                maps[core][f"{n}_{l}"] = ma[core][n]
            for n in B_NAMES:
                maps[core][f"{n}_{l}"] = mb[core][n]
    return maps


_NC = {}


def kernel(**inputs):
    inp = {k: np.asarray(v) for k, v in inputs.items()}
    if 'nc' not in _NC:
        _NC['nc'] = build_fused()
    res = run_bass_kernel_spmd(_NC['nc'], host_inputs(inp), core_ids=list(range(8)))
    out = np.stack([np.concatenate([res.results[b * 4 + j]["out"] for j in range(4)], 0) for b in range(NB)])
    return out.astype(np.float32)
```

```python
import contextlib
import os
import numpy as np
import ml_dtypes
import concourse.bass as bass
import concourse.mybir as mybir
from concourse.bass_utils import run_bass_kernel_spmd

F32 = mybir.dt.float32
BF16 = mybir.dt.bfloat16
AF = mybir.ActivationFunctionType
ALU = mybir.AluOpType
AX = mybir.AxisListType

D = 1024
SEQ = 8192
NB = 2
DEPTH = 2
EPS = 1e-6
EPOCH = 12000
POOL_WINDOWS = (2, 4, 8, 16)


class _Rec:
    def __init__(self):
        self.calls = []

    def __getattr__(self, name):
        def f(*a, **kw):
            self.calls.append((name, a, kw))
        return f


class K:
    def __init__(self, nc, stack, n_dma_sems=12):
        self.nc = nc
        self.stack = stack
        self.engs = ['pe', 'act', 'dve', 'pool', 'sp']
        self.q = {e: [] for e in self.engs}
        self.cnt = {e: 0 for e in self.engs}
        self.epoch = {e: 0 for e in self.engs}
        self.sems = {}
        self.known = {e: {} for e in self.engs}
        self.last_w = {}
        self.readers = {}
        self.dma_sems = {q: [stack.enter_context(nc.semaphore(f"dma_{q}{i}")) for i in range(n)]
                         for q, n in (('sp', 10), ('pool', 6), ('act', 2))}
        self.dma_cnt = {q: [0] * len(v) for q, v in self.dma_sems.items()}
        self.dma_rr = {q: 0 for q in self.dma_sems}

    def _sem(self, e):
        key = (e, self.epoch[e])
        if key not in self.sems:
            self.sems[key] = self.stack.enter_context(self.nc.semaphore(f"s_{e}_{self.epoch[e]}"))
        return self.sems[key]

    def _need(self, e, tok):
        if tok is None:
            return
        sem, val, src = tok[:3]
        if src == e and e == 'pe':
            return
        kn = self.known[e]
        if kn.get(id(sem), 0) >= val:
            return
        kn[id(sem)] = val
        self.q[e].append(('wait', sem, val))

    def _deps(self, e, reads, writes):
        for k in reads:
            self._need(e, self.last_w.get(k))
            if k.startswith('bank') or k == 'tpb':
                for r in self.readers.get(k, ()):
                    if r[2] != e:
                        self._need(e, r)
        for k in writes:
            self._need(e, self.last_w.get(k))
            for r in self.readers.get(k, ()):
                if r[2] == e:
                    continue
                self._need(e, r)

    def _commit(self, tok, reads, writes):
        for k in writes:
            self.last_w[k] = tok
            self.readers[k] = []
        for k in reads:
            self.readers.setdefault(k, []).append(tok)

    def op(self, e, fn, reads=(), writes=()):
        self._deps(e, reads, writes)
        if self.cnt[e] >= EPOCH:
            self.epoch[e] += 1
            self.cnt[e] = 0
        sem = self._sem(e)
        self.cnt[e] += 1
        tok = (sem, self.cnt[e], e)
        rec = _Rec()
        fn(rec)
        assert len(rec.calls) == 1
        name, a, kw = rec.calls[0]
        self.q[e].append(('op', (lambda eng, name=name, a=a, kw=kw: getattr(eng, name)(*a, **kw)), sem, 1))
        self._commit(tok, reads, writes)
        return tok

    def dma(self, e, out, in_, reads=(), writes=(), **kw):
        self._deps(e, reads, writes)
        i = self.dma_rr[e]
        self.dma_rr[e] = (i + 1) % len(self.dma_sems[e])
        sem = self.dma_sems[e][i]
        cnts = self.dma_cnt[e]
        if cnts[i] > 0:
            self._need(e, (sem, cnts[i] * 16, 'dma'))
        cnts[i] += 1
        tok = (sem, cnts[i] * 16, 'dma')
        self.q[e].append(('op', lambda eng: eng.dma_start(
            out=(out() if callable(out) else out), in_=(in_() if callable(in_) else in_), **kw), sem, 16))
        self._commit(tok, reads, writes)
        return tok

    def coll(self, kind, groups, src, dst, reads=(), writes=()):
        e = 'pool'
        self._deps(e, reads, writes)
        if not hasattr(self, 'cc_sem'):
            self.cc_sem = self.stack.enter_context(self.nc.semaphore("cc_sem"))
            self.cc_cnt = 0
        if self.cc_cnt > 0:
            self._need(e, (self.cc_sem, self.cc_cnt, 'dma'))
        self.cc_cnt += 1
        tok = (self.cc_sem, self.cc_cnt, 'dma')
        self.q[e].append(('op', lambda eng: eng.collective_compute(
            kind, ALU.bypass, groups, ins=[src.opt()], outs=[dst.opt()]), self.cc_sem, 1))
        self._commit(tok, reads, writes)
        return tok

    def wait_all(self, e, keys):
        for k in keys:
            self._need(e, self.last_w.get(k))

    def barrier(self):
        toks = []
        for f in self.engs:
            if self.cnt[f] > 0:
                toks.append((self._sem(f), self.cnt[f], f))
        for q, sems in self.dma_sems.items():
            for i, sem in enumerate(sems):
                if self.dma_cnt[q][i] > 0:
                    toks.append((sem, self.dma_cnt[q][i] * 16, 'dma'))
        if getattr(self, 'cc_cnt', 0) > 0:
            toks.append((self.cc_sem, self.cc_cnt, 'dma'))
        for e in self.engs:
            for t in toks:
                if t[2] != e:
                    self._need(e, t)

    def emit(self):
        nc = self.nc
        with nc.Block() as block:
            def replay(name, eng):
                for it in self.q[name]:
                    if it[0] == 'wait':
                        eng.wait_ge(it[1], it[2])
                    else:
                        it[1](eng).then_inc(it[2], it[3])

            @block.sync
            def _(eng):
                replay('sp', eng)

            @block.scalar
            def _(eng):
                replay('act', eng)

            @block.vector
            def _(eng):
                replay('dve', eng)

            @block.gpsimd
            def _(eng):
                replay('pool', eng)

            @block.tensor
            def _(eng):
                replay('pe', eng)
        for e in self.engs:
            self.q[e] = []


class Ctx:
    def __init__(self, nc, st, k=None, prefix=""):
        self.nc = nc
        self.st = st
        self.k = k if k is not None else K(nc, st)
        self.n = 0
        self.outkeys = []
        self.prefix = prefix

    def sb(self, name, shape, dt):
        return self.st.enter_context(self.nc.sbuf_tensor("s_" + self.prefix + name, list(shape), dt))

    def ps(self, name, shape, dt):
        return self.st.enter_context(self.nc.psum_tensor("p_" + self.prefix + name, list(shape), dt))


def emit_mod(cx, w_ada, b_ada, cT_d, ng_d, banks, need_gate):
    k = cx.k
    ncol = 3 if need_gate else 2
    cT = cx.sb("cT", [128, 8], F32)
    ng = cx.sb("ng", [128, 8], F32)
    modrow = cx.sb("modrow", [1, 3072], F32)
    one11 = cx.sb("one11", [1, 128], F32)
    s1 = cx.sb("s1", [128, 8], F32)
    s2 = cx.sb("s2", [128, 8], F32)
    gate_bc = cx.sb("gate_bc", [128, 1024], F32) if need_gate else None
    NWA = 4
    wa = [cx.sb(f"wa{i}", [128, 512], F32) for i in range(NWA)]
    k.dma('sp', cT[:], cT_d, writes=['cT'])
    k.dma('sp', ng[:], ng_d, writes=['ng'])
    k.op('dve', lambda e: e.memset(one11[:], 1.0), writes=['one11'])
    ngrp = ncol * 2
    i = 0
    for kc in range(8):
        for cg in range(ngrp):
            buf = wa[i % NWA]
            bk = f"wa{i % NWA}"
            i += 1
            k.dma('sp', buf[:], w_ada[kc * 128:(kc + 1) * 128, cg * 512:(cg + 1) * 512], writes=[bk])
            k.op('pe', lambda e, buf=buf, cg=cg, kc=kc: e.matmul(
                banks[cg][0:1, :], lhsT=cT[:, kc:kc + 1], rhs=buf[:], start=(kc == 0), stop=(kc == 7)),
                reads=[bk, 'cT'], writes=[f"bank{cg}"])
    for cg in range(ngrp):
        buf = wa[i % NWA]
        bk = f"wa{i % NWA}"
        i += 1
        k.dma('sp', buf[0:1, :], b_ada[0:1, cg * 512:(cg + 1) * 512], writes=[bk])
        k.op('dve', lambda e, cg=cg, buf=buf: e.tensor_tensor(
            out=modrow[0:1, cg * 512:(cg + 1) * 512], in0=banks[cg][0:1, :],
            in1=buf[0:1, :], op=ALU.add),
            reads=[f"bank{cg}", bk], writes=['modrow'])
    colb = banks[6]
    for cc in range(8):
        k.op('pe', lambda e, cc=cc: e.matmul(
            colb[:, cc:cc + 1], lhsT=modrow[0:1, 1024 + cc * 128:1024 + (cc + 1) * 128],
            rhs=one11[0:1, 0:1], start=True, stop=True), reads=['modrow', 'one11'], writes=['bank6'])
        k.op('pe', lambda e, cc=cc: e.matmul(
            colb[:, 8 + cc:9 + cc], lhsT=modrow[0:1, cc * 128:(cc + 1) * 128],
            rhs=one11[0:1, 0:1], start=True, stop=True), reads=['modrow', 'one11'], writes=['bank6'])
    k.op('dve', lambda e: e.scalar_tensor_tensor(
        out=s1[:], in0=colb[:, 0:8], scalar=1.0, in1=ng[:], op0=ALU.add, op1=ALU.mult),
        reads=['bank6', 'ng'], writes=['s1'])
    k.op('dve', lambda e: e.tensor_copy(out=s2[:], in_=colb[:, 8:16]), reads=['bank6'], writes=['s2'])
    if need_gate:
        for hh in range(2):
            k.op('pe', lambda e, hh=hh: e.matmul(
                banks[hh][:, :], lhsT=one11[0:1, 0:128], rhs=modrow[0:1, 2048 + hh * 512:2048 + (hh + 1) * 512],
                start=True, stop=True), reads=['modrow', 'one11'], writes=[f"bank{hh}"])
            k.op('dve', lambda e, hh=hh: e.tensor_copy(out=gate_bc[:, hh * 512:(hh + 1) * 512], in_=banks[hh][:, :]),
                 reads=[f"bank{hh}"], writes=['gate_bc'])
    return s1, s2, gate_bc


def emit_norm_tile(cx, xt, xkey, tb, s1, s2, ident, tp, tpkey, hT, hkey, tagn):
    k = cx.k
    i = cx.n
    cx.n += 1
    r = i % 2
    if not hasattr(cx, 'nrm'):
        cx.nrm = dict(
            sq=[cx.sb("nsq", [128, 1024], BF16)] * 2,
            st=[cx.sb(f"nst{j}", [128, 4], F32) for j in range(2)],
            xn=[cx.sb(f"nxn{j}", [128, 1024], BF16) for j in range(2)],
            nf=cx.sb("nrm_nf", [128, 8, 128], F32),
        )
    sq, stt, xn = cx.nrm['sq'][r], cx.nrm['st'][r], cx.nrm['xn'][r]
    ksq, kst, kxn = "nsq", f"nst{r}", f"nxn{r}"
    k.op('dve', lambda e: e.memset(stt[:], 0.0), writes=[kst])
    k.op('act', lambda e: e.activation(out=sq[:], in_=xt, func=AF.Square, accum_out=stt[:, 0:1]),
         reads=[xkey, kst], writes=[ksq, kst])
    k.op('act', lambda e: e.activation(out=stt[:, 1:2], in_=stt[:, 0:1], func=AF.Ln, scale=1.0 / D, bias=EPS),
         reads=[kst], writes=[kst])
    k.op('act', lambda e: e.activation(out=stt[:, 2:3], in_=stt[:, 1:2], func=AF.Exp, scale=-0.5),
         reads=[kst], writes=[kst])
    k.op('dve', lambda e: e.tensor_scalar(out=xn[:], in0=xt, scalar1=stt[:, 2:3], scalar2=None, op0=ALU.mult),
         reads=[xkey, kst], writes=[kxn])
    for kc in range(8):
        k.op('pe', lambda e, kc=kc: e.transpose(tp[:, kc * 128:(kc + 1) * 128], xn[:, kc * 128:(kc + 1) * 128], ident[:]),
             reads=[kxn, 'ident'], writes=[tpkey])
    hv = hT[:, :, tb * 128:(tb + 1) * 128]
    tpv = tp[:, :].rearrange("p (k t) -> p k t", k=8)
    nf = cx.nrm['nf']
    k.op('dve', lambda e: e.tensor_tensor(out=nf[:, :, :], in0=tpv, in1=s1[:, :].unsqueeze(2).broadcast_to([128, 8, 128]),
                                          op=ALU.mult), reads=[tpkey, 's1'], writes=['nrm_nf'])
    k.op('dve', lambda e: e.tensor_tensor(out=hv, in0=nf[:, :, :], in1=s2[:, :].unsqueeze(2).broadcast_to([128, 8, 128]),
                                          op=ALU.add), reads=['nrm_nf', 's2'], writes=[hkey])


NTB = 2048


def emit_B(cx, last, x_d, yT_d, cT_d, ng_d, wada_d, bada_d, wg_d, gb_d, wbr_d, wout_d, fg_d, id_d, xo_d, x_reads=(), y_reads=(), after_block=None):
    k = cx.k
    nc = cx.nc
    banks = [cx.ps(f"bank{i}", [128, 512], F32) for i in range(7)]
    tp = cx.ps("tpb", [128, 1024], BF16)
    ident = cx.sb("ident", [128, 128], BF16)
    k.dma('pool', ident[:], id_d, writes=['ident'])
    s1, s2, gate_bc = emit_mod(cx, wada_d, bada_d, cT_d, ng_d, banks, True)
    wg = cx.sb("wg", [128, 8, 3 * D], BF16)
    wbr = cx.sb("wbr", [128, 8, D], BF16)
    wout = cx.sb("wout", [128, 8, D], BF16)
    gb = cx.sb("gb", [128, 24], F32)
    k.dma('sp', gb[:], gb_d, writes=['gb'])
    for kc in range(8):
        k.dma('pool', wg[:, kc, :], wg_d[kc * 128:(kc + 1) * 128, :], writes=['wg'])
    for kc in range(8):
        k.dma('pool', wbr[:, kc, :], wbr_d[kc * 128:(kc + 1) * 128, :], writes=['wbr'])
    for kc in range(8):
        k.dma('pool', wout[:, kc, :], wout_d[kc * 128:(kc + 1) * 128, :], writes=['wout'])
    if last:
        fg_bc = cx.sb("fg_bc", [128, D], F32)
        k.dma('sp', fg_bc[:], fg_d.partition_broadcast(128), writes=['fg_bc'])
    xres = cx.sb("xres", [128, 4, D], F32)
    hT = [cx.sb(f"hT{i}", [128, 8, 512], BF16) for i in range(2)]
    yT = [cx.sb(f"yT{i}", [128, 8, 512], BF16) for i in range(2)]
    mT = cx.sb("mT", [128, 8, 512], BF16)
    sig = [cx.sb(f"sig{i}", [128, 512], F32) for i in range(3)]
    tmp = [cx.sb(f"tmp{i}", [128, 512], F32) for i in range(3)]
    xn_o = [cx.sb(f"xno{i}", [128, D], F32) for i in range(2)]
    fst = [cx.sb(f"fst{i}", [128, 4], F32) for i in range(2)]
    yo = [cx.sb(f"yo{i}", [128, D], F32) for i in range(2)]
    fsq = cx.sb("fsq", [128, D], BF16)
    nG = 0
    nP = 0
    nO = 0
    ntile = NTB // 512
    for tl in range(ntile):
        r = tl % 2
        for (kk, p0, pn, src) in yT_d(tl):
            k.dma('sp', yT[r][p0:p0 + pn, kk, :], src, reads=y_reads, writes=[f"yT{r}_{kk}_{p0}"])
        for tb in range(4):
            t0 = tl * 512 + tb * 128
            k.dma('sp', xres[:, tb, :], x_d(t0), reads=x_reads, writes=[f"xres{tb}"])
            emit_norm_tile(cx, xres[:, tb, :], f"xres{tb}", tb, s1, s2, ident, tp, 'tpb', hT[r], f"hT{r}", 'b')
        for dc in range(8):
            for gi in range(3):
                gbk = 0 + (nG % 2)
                nG += 1
                for kc in range(8):
                    k.op('pe', lambda e, gbk=gbk, gi=gi, kc=kc, dc=dc, r=r: e.matmul(
                        banks[gbk][:, :], lhsT=wg[:, kc, gi * D + dc * 128:gi * D + (dc + 1) * 128],
                        rhs=hT[r][:, kc, :], start=(kc == 0), stop=(kc == 7)),
                        reads=['wg', f"hT{r}"], writes=[f"bank{gbk}"])
                k.op('act', lambda e, gbk=gbk, gi=gi, dc=dc: e.activation(
                    out=sig[gi][:], in_=banks[gbk][:, :], func=AF.Sigmoid,
                    bias=gb[:, gi * 8 + dc:gi * 8 + dc + 1], scale=1.0),
                    reads=[f"bank{gbk}", 'gb'], writes=[f"sig{gi}"])
            for bi, (k0, k1) in enumerate(((0, 4), (4, 6), (6, 8))):
                pbk = 2 + (nP % 2)
                nP += 1
                for kc in range(k0, k1):
                    k.op('pe', lambda e, pbk=pbk, kc=kc, dc=dc, r=r, k0=k0, k1=k1: e.matmul(
                        banks[pbk][:, :], lhsT=wbr[:, kc, dc * 128:(dc + 1) * 128],
                        rhs=yT[r][:, kc, :], start=(kc == k0), stop=(kc == k1 - 1)),
                        reads=['wbr'] + [f"yT{r}_{kc}_{p0}" for p0 in (0, 64)], writes=[f"bank{pbk}"])
                k.op('dve', lambda e, pbk=pbk, bi=bi: e.tensor_tensor(
                    out=tmp[bi][:], in0=banks[pbk][:, :], in1=sig[bi][:], op=ALU.mult),
                    reads=[f"bank{pbk}", f"sig{bi}"], writes=[f"tmp{bi}"])
            k.op('dve', lambda e: e.tensor_tensor(out=tmp[0][:], in0=tmp[0][:], in1=tmp[1][:], op=ALU.add),
                 reads=['tmp0', 'tmp1'], writes=['tmp0'])
            k.op('dve', lambda e, dc=dc: e.tensor_tensor(out=mT[:, dc, :], in0=tmp[0][:], in1=tmp[2][:], op=ALU.add),
                 reads=['tmp0', 'tmp2'], writes=['mT'])
        for tb in range(4):
            t0 = tl * 512 + tb * 128
            ro = nO % 2
            nO += 1
            for ch in range(2):
                obk = 4 + ch
                for kc in range(8):
                    k.op('pe', lambda e, obk=obk, kc=kc, tb=tb, ch=ch: e.matmul(
                        banks[obk][:, :], lhsT=mT[:, kc, tb * 128:(tb + 1) * 128],
                        rhs=wout[:, kc, ch * 512:(ch + 1) * 512], start=(kc == 0), stop=(kc == 7)),
                        reads=['mT', 'wout'], writes=[f"bank{obk}"])
                k.op('dve', lambda e, obk=obk, ch=ch, ro=ro: e.tensor_tensor(
                    out=xn_o[ro][:, ch * 512:(ch + 1) * 512], in0=banks[obk][:, :],
                    in1=gate_bc[:, ch * 512:(ch + 1) * 512], op=ALU.mult),
                    reads=[f"bank{obk}", 'gate_bc'], writes=[f"xno{ro}"])
            k.op('dve', lambda e, ro=ro, tb=tb: e.tensor_tensor(
                out=xn_o[ro][:], in0=xn_o[ro][:], in1=xres[:, tb, :], op=ALU.add),
                reads=[f"xno{ro}", f"xres{tb}"], writes=[f"xno{ro}"])
            if not last:
                k.dma('sp', xo_d[t0:t0 + 128, :], xn_o[ro][:], reads=[f"xno{ro}"], writes=[f"xo{t0}"])
                cx.outkeys.append(f"xo{t0}")
                if after_block is not None:
                    after_block(t0)
            else:
                k.op('dve', lambda e, ro=ro: e.memset(fst[ro][:], 0.0), writes=[f"fst{ro}"])
                k.op('act', lambda e, ro=ro: e.activation(out=fsq[:], in_=xn_o[ro][:], func=AF.Square,
                                                           accum_out=fst[ro][:, 0:1]),
                     reads=[f"xno{ro}", f"fst{ro}"], writes=['fsq', f"fst{ro}"])
                k.op('act', lambda e, ro=ro: e.activation(out=fst[ro][:, 1:2], in_=fst[ro][:, 0:1], func=AF.Ln,
                                                           scale=1.0 / D, bias=EPS),
                     reads=[f"fst{ro}"], writes=[f"fst{ro}"])
                k.op('act', lambda e, ro=ro: e.activation(out=fst[ro][:, 2:3], in_=fst[ro][:, 1:2], func=AF.Exp, scale=-0.5),
                     reads=[f"fst{ro}"], writes=[f"fst{ro}"])
                k.op('dve', lambda e, ro=ro: e.scalar_tensor_tensor(
                    out=yo[ro][:], in0=xn_o[ro][:], scalar=fst[ro][:, 2:3], in1=fg_bc[:],
                    op0=ALU.mult, op1=ALU.mult), reads=[f"xno{ro}", f"fst{ro}", 'fg_bc'], writes=[f"yo{ro}"])
                k.dma('sp', xo_d[t0:t0 + 128, :], yo[ro][:], reads=[f"yo{ro}"], writes=[f"xo{t0}"])
                cx.outkeys.append(f"xo{t0}")


def emit_A(cx, a, yT_o, ntile):
    import os
    STAGE = int(os.environ.get('A_STAGE', '9'))
    SUB = int(os.environ.get('A_SUB', '9'))
    DIS = os.environ.get('A_DIS', '')
    k = cx.k
    banks = [cx.ps(f"bank{i}", [128, 512], F32) for i in range(7)]
    tp = cx.ps("tpb", [128, 1024], BF16)
    ZB = (0, 1)
    LB = (2, 3)
    OB = 4
    PB = 5
    MB = 6
    ident = cx.sb("ident", [128, 128], BF16)
    k.dma('pool', ident[:], a['ident'], writes=['ident'])
    s1, s2, _ = emit_mod(cx, a['w_ada'], a['b_ada'], a['cT'], a['ng'], banks, False)
    wtm = cx.sb("wtm", [128, 8, 512], BF16)
    wfm = cx.sb("wfm", [128, 8, 512], BF16)
    wgt = cx.sb("wgt", [128, 8, 16], BF16)
    for kc in range(8):
        k.dma('pool', wtm[:, kc, :], a['wtm'][kc * 128:(kc + 1) * 128, :], writes=['wtm'])
        k.dma('pool', wfm[:, kc, :], a['wfm'][kc * 128:(kc + 1) * 128, :], writes=['wfm'])
        k.dma('pool', wgt[:, kc, :], a['wgt'][kc * 128:(kc + 1) * 128, :], writes=['wgt'])
    mgb = cx.sb("mgb", [128, 2], F32)
    nmgb = cx.sb("nmgb", [128, 2], F32)
    cw = cx.sb("cw", [128, 8], F32)
    cb = cx.sb("cb", [128, 2], F32)
    mng = cx.sb("mng", [128, 128], F32)
    poolw = cx.sb("poolw", [64, 64], BF16)
    pscale = cx.sb("pscale", [64, 1], F32)
    bands = cx.sb("bands", [128, 3, 128], BF16)
    mtri = cx.sb("mtri", [128, 128], F32)
    onesf = cx.sb("onesf", [128, 128], F32)
    sbm = cx.sb("sbm", [128, 4, 512], BF16)
    nui = cx.sb("nui", [128, 128], BF16)
    nones = cx.sb("nones", [128, 128], BF16)
    k.dma('sp', mgb[:], a['mgb'], writes=['mgb'])
    k.dma('sp', cw[:], a['cw'], writes=['cw'])
    k.dma('sp', cb[:], a['cb'], writes=['cb'])
    k.dma('sp', mng[:], a['mng'].partition_broadcast(128), writes=['mng'])
    k.dma('pool', poolw[:], a['poolw'], writes=['poolw'])
    k.dma('sp', pscale[:], a['pscale'], writes=['pscale'])
    for i in range(3):
        k.dma('pool', bands[:, i, :], a['bands'][i], writes=['bands'])
    k.dma('sp', mtri[:], a['mtri'], writes=['mtri'])
    for i in range(4):
        k.dma('pool', sbm[:, i, :], a['sbm'][i], writes=['sbm'])
    k.dma('pool', nui[:], a['nui'], writes=['nui'])
    k.op('dve', lambda e: e.memset(onesf[:], 1.0), writes=['onesf'])
    k.op('dve', lambda e: e.memset(nones[:], -1.0), writes=['nones'])
    k.op('dve', lambda e: e.tensor_scalar(out=nmgb[:], in0=mgb[:], scalar1=-1.0, scalar2=None, op0=ALU.mult),
         reads=['mgb'], writes=['nmgb'])
    sqT = cx.sb("sqT", [64, SEQ], BF16)
    skT = cx.sb("skT", [64, SEQ], BF16)
    SV = cx.sb("SV", [128, SEQ // 128, 64], BF16)
    Cn32 = cx.sb("Cn32", [128, 132], F32)
    Cnb = cx.sb("Cnb", [128, 132], BF16)
    k.op('dve', lambda e: e.memset(Cn32[:], 0.0), writes=['Cn32'])
    k.op('dve', lambda e: e.memset(Cnb[:], 0.0), writes=['Cnb'])
    xt = [cx.sb(f"xt{i}", [128, D], F32) for i in range(2)]
    hT = [cx.sb(f"hT{i}", [128, 8, 512], BF16) for i in range(2)]
    qkr = [cx.sb(f"qkr{i}", [128, 516], F32) for i in range(2)]
    for g in range(2):
        k.op('pool', lambda e, g=g: e.memset(qkr[g][:], 0.0), writes=[f"qkr{g}"])
    cacc = [cx.sb(f"cacc{i}", [128, 512], F32) for i in range(2)]
    csg = [cx.sb(f"csg{i}", [128, 512], F32) for i in range(2)]
    qT = cx.sb("qT", [128, 512], BF16)
    kT = cx.sb("kT", [128, 512], BF16)
    spz = cx.sb("spz", [64, 512], F32)
    ssz = cx.sb("ssz", [64, 512], F32)
    sgz = cx.sb("sgz", [64, 512], F32)
    Ub = [cx.sb(f"Ub{i}", [128, 64], BF16) for i in range(3)]
    tmS = [cx.sb(f"tmS{i}", [128, 512], F32) for i in range(2)]
    gsb = [cx.sb(f"gsb{i}", [128, 16], F32) for i in range(2)]
    sgo = [cx.sb(f"sgo{i}", [128, 256], F32) for i in range(2)]
    gz = [cx.sb(f"gz{i}", [128, 128], F32) for i in range(2)]
    V2 = [cx.sb(f"V2{i}", [128, 132], BF16) for i in range(2)]
    Ktm = [cx.sb(f"Ktm{i}", [128, 128], BF16) for i in range(2)]
    for i in range(2):
        k.op('pool', lambda e, i=i: e.memset(V2[i][:], 0.0), writes=[f"V2{i}"])
    Sm = [cx.sb(f"Sm{i}", [128, 128], BF16) for i in range(2)]
    t1 = [cx.sb(f"t1{i}", [128, 128], F32) for i in range(2)]
    t1sq = cx.sb("t1sq", [128, 128], BF16)
    ymb = [cx.sb(f"ymb{i}", [128, 128], BF16) for i in range(2)]
    ymT = [cx.sb(f"ymT{i}", [128, 512], BF16) for i in range(2)]
    pTs = cx.sb("pTs", [64, 512], BF16)
    ypT = [cx.sb(f"ypT{i}", [64, 512], BF16) for i in range(2)]
    ysT = [cx.sb(f"ysT{i}", [64, 512], BF16) for i in range(2)]
    Eb = [cx.sb(f"Eb{i}", [128, 512], F32) for i in range(2)]
    L32 = cx.sb("L32", [128, 512], F32)
    Lb = [cx.sb(f"Lb{i}", [128, 512], BF16) for i in range(2)]
    S32 = cx.sb("S32", [128, 512], F32)
    Sb = [cx.sb(f"Sb{i}", [128, 512], BF16) for i in range(2)]
    At = [cx.sb(f"At{i}", [128, 512], BF16) for i in range(2)]
    Am = [cx.sb(f"Am{i}", [128, 512], BF16) for i in range(2)]
    mb = banks[MB]
    cnt = dict(z=0, l=0, u=0, g=0, p=0)
    PR = [PB, 0, 1, 2, 3]

    def nextpb():
        i = PR[cnt['p'] % len(PR)]
        cnt['p'] += 1
        return banks[i], f"bank{i}"
    KSCALE = 128.0 ** -0.5
    nx = 0
    for tl in range(ntile if STAGE >= 2 else 0):
        r = tl % 2
        for tb in range(4):
            t0 = tl * 512 + tb * 128
            xr = nx % 2
            nx += 1
            k.dma('sp', xt[xr][:], a['x'](t0), reads=(a['xreads'](t0) if 'xreads' in a else ()), writes=[f"xt{xr}"])
            emit_norm_tile(cx, xt[xr][:], f"xt{xr}", tb, s1, s2, ident, tp, 'tpb', hT[r], f"hT{r}", 'a')
        hk = f"hT{r}"
        for g in range(2):
            pb, pk = nextpb()
            for kc in range(8):
                k.op('pe', lambda e, g=g, kc=kc, r=r: e.matmul(
                    pb[:, :], lhsT=wfm[:, kc, g * 128:(g + 1) * 128], rhs=hT[r][:, kc, :],
                    start=(kc == 0), stop=(kc == 7)), reads=['wfm', hk], writes=[pk])
            k.op('pool', lambda e, g=g: e.tensor_copy(out=qkr[g][:, 0:3], in_=qkr[g][:, 512:515]),
                 reads=[f"qkr{g}"], writes=[f"qkr{g}h"])
            k.op('act', lambda e, g=g: e.activation(out=qkr[g][:, 3:515], in_=pb[:, :], func=AF.Identity),
                 reads=[pk, f"qkr{g}h"], writes=[f"qkr{g}"])
            k.op('pool', lambda e, g=g: e.tensor_scalar(
                out=cacc[g][:], in0=qkr[g][:, 0:512], scalar1=cw[:, 4 * g:4 * g + 1], scalar2=cb[:, g:g + 1],
                op0=ALU.mult, op1=ALU.add), reads=[f"qkr{g}", f"qkr{g}h", 'cw', 'cb'], writes=[f"cacc{g}"])
            for j in range(1, 4):
                k.op('dve', lambda e, g=g, j=j: e.scalar_tensor_tensor(
                    out=cacc[g][:], in0=qkr[g][:, j:j + 512], scalar=cw[:, 4 * g + j:4 * g + j + 1],
                    in1=cacc[g][:], op0=ALU.mult, op1=ALU.add),
                    reads=[f"qkr{g}", f"qkr{g}h", f"cacc{g}", 'cw'], writes=[f"cacc{g}"])
            k.op('act', lambda e, g=g: e.activation(out=csg[g][:], in_=cacc[g][:], func=AF.Sigmoid),
                 reads=[f"cacc{g}"], writes=[f"csg{g}"])
            dst, dk, scl = (qT, 'qT', 1.0) if g == 0 else (kT, 'kT', KSCALE)
            k.op('dve', lambda e, g=g, dst=dst, scl=scl: e.scalar_tensor_tensor(
                out=dst[:], in0=cacc[g][:], scalar=scl, in1=csg[g][:], op0=ALU.mult, op1=ALU.mult),
                reads=[f"cacc{g}", f"csg{g}"], writes=[dk])
        for i4, nm in enumerate(('pz', 'sz', 'sq', 'sk')):
            c0 = 256 + i4 * 64
            pb, pk = nextpb()
            for kc in range(8):
                k.op('pe', lambda e, kc=kc, r=r, c0=c0: e.matmul(
                    pb[0:64, :], lhsT=wfm[:, kc, c0:c0 + 64], rhs=hT[r][:, kc, :],
                    start=(kc == 0), stop=(kc == 7)), reads=['wfm', hk], writes=[pk])
            if nm in ('pz', 'sz'):
                dst, dk = (spz, 'spz') if nm == 'pz' else (ssz, 'ssz')
                k.op('act', lambda e: e.activation(out=sgz[:], in_=pb[0:64, :], func=AF.Sigmoid),
                     reads=[pk], writes=['sgz'])
                k.op('dve', lambda e, dst=dst: e.tensor_tensor(out=dst[:], in0=pb[0:64, :], in1=sgz[:], op=ALU.mult),
                     reads=[pk, 'sgz'], writes=[dk])
            elif nm == 'sq':
                k.op('act', lambda e, tl=tl: e.activation(out=sqT[:, tl * 512:(tl + 1) * 512], in_=pb[0:64, :],
                                                           func=AF.Identity, scale=0.125),
                     reads=[pk], writes=[f"sqT{tl}"])
            else:
                k.op('dve', lambda e, tl=tl: e.tensor_copy(out=skT[:, tl * 512:(tl + 1) * 512], in_=pb[0:64, :]),
                     reads=[pk], writes=[f"skT{tl}"])
        if STAGE < 3:
            continue
        for tb in range(4):
            n = tl * 4 + tb
            tsl = slice(tb * 128, (tb + 1) * 128)
            pb, pk = nextpb()
            for kc in range(8):
                k.op('pe', lambda e, kc=kc, r=r, tsl=tsl: e.matmul(
                    pb[:, :], lhsT=hT[r][:, kc, tsl], rhs=wtm[:, kc, :], start=(kc == 0), stop=(kc == 7)),
                    reads=['wtm', hk], writes=[pk])
            for kc in range(8):
                k.op('pe', lambda e, kc=kc, r=r, tsl=tsl: e.matmul(
                    mb[:, 400:416], lhsT=hT[r][:, kc, tsl], rhs=wgt[:, kc, :], start=(kc == 0), stop=(kc == 7)),
                    reads=['wgt', hk], writes=['bank6'])
            gi = cnt['g'] % 2
            cnt['g'] += 1
            G = gsb[gi]
            gk = f"gsb{gi}"
            ts = tmS[n % 2]
            tk = f"tmS{n % 2}"
            k.op('act', lambda e, ts=ts, pb=pb: e.activation(out=ts[:], in_=pb[:, :], func=AF.Identity),
                 reads=[pk], writes=[tk])
            k.op('pool', lambda e, n=n, ts=ts: e.tensor_copy(out=SV[:, n, :], in_=ts[:, 448:512]),
                 reads=[tk], writes=[f"SV{n}"])
            ui = n % 3
            k.op('pool', lambda e, ui=ui, ts=ts: e.tensor_copy(out=Ub[ui][:], in_=ts[:, 384:448]),
                 reads=[tk], writes=[f"Ub{ui}"])
            k.op('act', lambda e, gi=gi, ts=ts: e.activation(out=sgo[gi][:], in_=ts[:, 128:384], func=AF.Sigmoid),
                 reads=[tk], writes=[f"sgo{gi}"])
            k.op('dve', lambda e, gi=gi, ts=ts: e.tensor_tensor(out=gz[gi][:], in0=ts[:, 256:384], in1=sgo[gi][:, 128:256], op=ALU.mult),
                 reads=[tk, f"sgo{gi}"], writes=[f"gz{gi}"])
            k.op('pool', lambda e, gi=gi: e.tensor_tensor(out=gz[gi][:], in0=gz[gi][:], in1=mng[:], op=ALU.mult),
                 reads=[f"gz{gi}", 'mng'], writes=[f"gz{gi}"])
            if SUB < 1:
                continue
            k.op('act', lambda e, G=G: e.activation(out=G[:, 0:2], in_=mb[:, 400:402], func=AF.Identity),
                 reads=['bank6'], writes=[gk])
            k.op('act', lambda e, G=G: e.activation(out=G[:, 2:3], in_=G[:, 1:2], func=AF.Exp, scale=-1.0, bias=nmgb[:, 1:2]),
                 reads=[gk, 'nmgb'], writes=[gk])
            k.op('act', lambda e, G=G: e.activation(out=G[:, 3:4], in_=G[:, 2:3], func=AF.Ln, scale=1.0, bias=1.0),
                 reads=[gk], writes=[gk])
            k.op('pe', lambda e, G=G: e.matmul(mb[:, 404:405], lhsT=mtri[:], rhs=G[:, 3:4], start=True, stop=True),
                 reads=[gk, 'mtri'], writes=['bank6'])
            k.op('pe', lambda e, G=G: e.matmul(mb[:, 405:406], lhsT=onesf[:], rhs=G[:, 3:4], start=True, stop=True),
                 reads=[gk, 'onesf'], writes=['bank6'])
            k.op('act', lambda e, G=G: e.activation(out=G[:, 4:6], in_=mb[:, 404:406], func=AF.Exp, scale=-1.0),
                 reads=['bank6'], writes=[gk])
            k.op('dve', lambda e, G=G: e.tensor_tensor(out=G[:, 6:7], in0=mb[:, 404:405], in1=G[:, 0:1], op=ALU.add),
                 reads=['bank6', gk], writes=[gk])
            k.op('act', lambda e, G=G: e.activation(out=G[:, 7:8], in_=G[:, 6:7], func=AF.Exp, scale=1.0, bias=mgb[:, 0:1]),
                 reads=[gk, 'mgb'], writes=[gk])
            if SUB < 2:
                continue
            vi = n % 2
            k.op('dve', lambda e, vi=vi, G=G, ts=ts: e.tensor_scalar(out=V2[vi][:, 0:128], in0=ts[:, 0:128], scalar1=G[:, 7:8],
                                                              scalar2=None, op0=ALU.mult),
                 reads=[tk, gk], writes=[f"V2{vi}"])
            k.op('dve', lambda e, vi=vi, G=G: e.tensor_copy(out=V2[vi][:, 128:129], in_=G[:, 7:8]),
                 reads=[gk], writes=[f"V2{vi}"])
            k.op('pe', lambda e, tsl=tsl: e.transpose(tp[:, 0:128], kT[:, tsl], ident[:]),
                 reads=['kT', 'ident'], writes=['tpb'])
            k.op('dve', lambda e, vi=vi: e.tensor_copy(out=Ktm[vi][:], in_=tp[:, 0:128]),
                 reads=['tpb'], writes=[f"Ktm{vi}"])
            k.op('pe', lambda e, tsl=tsl: e.matmul(mb[:, 0:128], lhsT=kT[:, tsl], rhs=qT[:, tsl], start=True, stop=True),
                 reads=['kT', 'qT'], writes=['bank6'])
            k.op('dve', lambda e, vi=vi: e.tensor_tensor(out=Sm[vi][:], in0=mb[:, 0:128], in1=mtri[:], op=ALU.mult),
                 reads=['bank6', 'mtri'], writes=[f"Sm{vi}"])
            if SUB < 3:
                continue
            k.op('pe', lambda e, tsl=tsl: e.matmul(mb[:, 128:258], lhsT=qT[:, tsl], rhs=Cnb[:, 0:130], start=True, stop=False),
                 reads=['qT', 'Cnb'], writes=['bank6'])
            k.op('pe', lambda e, vi=vi: e.matmul(mb[:, 128:258], lhsT=Sm[vi][:], rhs=V2[vi][:, 0:130], start=False, stop=True),
                 reads=[f"Sm{vi}", f"V2{vi}"], writes=['bank6'])
            k.op('pe', lambda e, vi=vi: e.matmul(mb[:, 260:390], lhsT=Ktm[vi][:], rhs=V2[vi][:, 0:130], start=True, stop=True),
                 reads=[f"Ktm{vi}", f"V2{vi}"], writes=['bank6'])
            k.op('dve', lambda e: e.tensor_tensor(out=Cn32[:, 0:130], in0=mb[:, 260:390], in1=Cn32[:, 0:130], op=ALU.add),
                 reads=['bank6', 'Cn32'], writes=['Cn32'])
            k.op('dve', lambda e, G=G: e.tensor_scalar(out=Cn32[:, 0:130], in0=Cn32[:, 0:130], scalar1=G[:, 5:6],
                                                       scalar2=None, op0=ALU.mult),
                 reads=['Cn32', gk], writes=['Cn32'])
            k.op('pool', lambda e: e.tensor_copy(out=Cnb[:, 0:130], in_=Cn32[:, 0:130]),
                 reads=['Cn32'], writes=['Cnb'])
            if SUB < 4:
                continue
            k.op('dve', lambda e, G=G: e.tensor_tensor(out=G[:, 8:9], in0=mb[:, 256:257], in1=G[:, 4:5], op=ALU.mult),
                 reads=['bank6', gk], writes=[gk])
            k.op('dve', lambda e, G=G: e.tensor_tensor(out=G[:, 8:9], in0=G[:, 8:9], in1=G[:, 8:9], op=ALU.mult),
                 reads=[gk], writes=[gk])
            k.op('dve', lambda e, G=G: e.tensor_scalar(out=G[:, 8:9], in0=G[:, 8:9], scalar1=1.0, scalar2=None, op0=ALU.max),
                 reads=[gk], writes=[gk])
            k.op('act', lambda e, G=G: e.activation(out=G[:, 14:15], in_=G[:, 8:9], func=AF.Ln),
                 reads=[gk], writes=[gk])
            k.op('act', lambda e, G=G: e.activation(out=G[:, 9:10], in_=G[:, 14:15], func=AF.Exp, scale=-0.5),
                 reads=[gk], writes=[gk])
            k.op('dve', lambda e, G=G: e.tensor_tensor(out=G[:, 10:11], in0=G[:, 9:10], in1=G[:, 4:5], op=ALU.mult),
                 reads=[gk], writes=[gk])
            k.op('dve', lambda e, G=G, gi=gi: e.scalar_tensor_tensor(
                out=t1[gi][:], in0=mb[:, 128:256], scalar=G[:, 10:11], in1=sgo[gi][:, 0:128], op0=ALU.mult, op1=ALU.mult),
                reads=['bank6', gk, f"sgo{gi}"], writes=[f"t1{gi}"])
            k.op('pool', lambda e, G=G: e.memset(G[:, 11:12], 0.0), reads=[], writes=[gk + 'a'])
            k.op('act', lambda e, G=G, gi=gi: e.activation(out=t1sq[:], in_=t1[gi][:], func=AF.Square, accum_out=G[:, 11:12]),
                 reads=[f"t1{gi}", gk + 'a'], writes=['t1sq', gk + 'a'])
            k.op('act', lambda e, G=G: e.activation(out=G[:, 12:13], in_=G[:, 11:12], func=AF.Ln, scale=1.0 / 128, bias=EPS),
                 reads=[gk + 'a'], writes=[gk + 'b'])
            k.op('act', lambda e, G=G: e.activation(out=G[:, 13:14], in_=G[:, 12:13], func=AF.Exp, scale=-0.5),
                 reads=[gk + 'b'], writes=[gk + 'c'])
            k.op('dve', lambda e, G=G, gi=gi: e.scalar_tensor_tensor(
                out=ymb[gi][:], in0=t1[gi][:], scalar=G[:, 13:14], in1=gz[gi][:], op0=ALU.mult, op1=ALU.mult),
                reads=[f"t1{gi}", gk + 'c', f"gz{gi}"], writes=[f"ymb{gi}"])
            k.op('pe', lambda e, gi=gi: e.transpose(tp[:, 128:256], ymb[gi][:], ident[:]),
                 reads=[f"ymb{gi}", 'ident'], writes=['tpb'])
            k.op('act', lambda e, r=r, tsl=tsl: e.activation(out=ymT[r][:, tsl], in_=tp[:, 128:256], func=AF.Identity),
                 reads=['tpb'], writes=[f"ymT{r}"])
            if SUB < 5:
                continue
            bi = 0 if n == 0 else 1
            k.op('pe', lambda e, ui=ui, bi=bi, tsl=tsl, n=n: e.matmul(
                banks[OB][0:64, 0:128], lhsT=Ub[ui][:], rhs=bands[:, bi, :], start=True, stop=(n == 0)),
                reads=[f"Ub{ui}", 'bands'], writes=['bank4'])
            if n > 0:
                up = (n - 1) % 3
                k.op('pe', lambda e, up=up: e.matmul(
                    banks[OB][0:64, 0:128], lhsT=Ub[up][:], rhs=bands[:, 2, :], start=False, stop=True),
                    reads=[f"Ub{up}", 'bands'], writes=['bank4'])
            k.op('dve', lambda e, tsl=tsl: e.tensor_copy(out=pTs[:, tsl], in_=banks[OB][0:64, 0:128]),
                 reads=['bank4'], writes=['pTs'])
        if os.environ.get('A_SKIPPOST'):
            continue
        k.dma('sp', yT_o('m', tl), ymT[r][:], reads=[f"ymT{r}"], writes=[f"oym{tl}"])
        cx.outkeys.append(f"oym{tl}")
        pb, pk = nextpb()
        k.op('pe', lambda e: e.matmul(pb[0:64, :], lhsT=poolw[:], rhs=pTs[:], start=True, stop=True),
             reads=['poolw', 'pTs'], writes=[pk])
        k.op('dve', lambda e, r=r: e.scalar_tensor_tensor(
            out=ypT[r][:], in0=pb[0:64, :], scalar=pscale[:, 0:1], in1=spz[:], op0=ALU.mult, op1=ALU.mult),
            reads=[pk, 'pscale', 'spz'], writes=[f"ypT{r}"])
        k.dma('sp', yT_o('p', tl), ypT[r][:], reads=[f"ypT{r}"], writes=[f"oyp{tl}"])
        cx.outkeys.append(f"oyp{tl}")
        if STAGE < 4:
            continue
        top = 4 * tl + 3
        qsl = slice(tl * 512, (tl + 1) * 512)
        sq_keys = [f"sqT{tl}"]
        U = top + 1

        def st_Z(u):
            kb = top - u
            ci = u % 2
            zi = ZB[ci]
            zb = banks[zi]
            ksl = slice(kb * 128, (kb + 1) * 128)
            kkey = f"skT{kb // 4}"
            diag = kb >= 4 * tl
            k.op('pe', lambda e: e.matmul(zb[:, :], lhsT=skT[:, ksl], rhs=sqT[:, qsl], start=True, stop=True),
                 reads=[kkey] + sq_keys, writes=[f"bank{zi}"])
            k.op('act', lambda e: e.activation(out=Eb[ci][:], in_=zb[:, :], func=AF.Exp),
                 reads=[f"bank{zi}"], writes=[f"Eb{ci}"])
            if diag:
                k.op('act', lambda e: e.activation(out=L32[:], in_=Eb[ci][:], func=AF.Ln, scale=1.0, bias=1.0),
                     reads=[f"Eb{ci}"], writes=['L32'])
                k.op('dve', lambda e: e.tensor_tensor(out=Lb[ci][:], in0=L32[:], in1=sbm[:, kb - 4 * tl, :], op=ALU.mult),
                     reads=['L32', 'sbm'], writes=[f"Lb{ci}"])
            else:
                k.op('act', lambda e: e.activation(out=Lb[ci][:], in_=Eb[ci][:], func=AF.Ln, scale=1.0, bias=1.0),
                     reads=[f"Eb{ci}"], writes=[f"Lb{ci}"])
        def st_S(u, tl=tl, top=top):
            kb = top - u
            ci = u % 2
            if kb > 0:
                if kb == top:
                    k.op('dve', lambda e: e.tensor_copy(out=S32[:], in_=Lb[ci][:]), reads=[f"Lb{ci}"], writes=['S32'])
                else:
                    k.op('dve', lambda e: e.tensor_tensor(out=S32[:], in0=S32[:], in1=Lb[ci][:], op=ALU.add),
                         reads=['S32', f"Lb{ci}"], writes=['S32'])
                k.op('dve', lambda e: e.tensor_copy(out=Sb[1 - ci][:], in_=S32[:]), reads=['S32'], writes=[f"Sb{1 - ci}"])

        def st_L(u):
            kb = top - u
            ci = u % 2
            li = LB[ci]
            lb = banks[li]
            ksl = slice(kb * 128, (kb + 1) * 128)
            kkey = f"skT{kb // 4}"
            diag = kb >= 4 * tl
            k.op('pe', lambda e: e.matmul(lb[:, :], lhsT=skT[:, ksl], rhs=sqT[:, qsl], start=True, stop=False),
                 reads=[kkey] + sq_keys, writes=[f"bank{li}"])
            k.op('pe', lambda e: e.matmul(lb[:, :], lhsT=nui[:], rhs=Lb[ci][:], start=False, stop=(kb == top)),
                 reads=['nui', f"Lb{ci}"], writes=[f"bank{li}"])
            if kb != top:
                k.op('pe', lambda e: e.matmul(lb[:, :], lhsT=nones[:], rhs=Sb[ci][:], start=False, stop=True),
                     reads=['nones', f"Sb{ci}"], writes=[f"bank{li}"])
            k.op('act', lambda e: e.activation(out=At[ci][:], in_=lb[:, :], func=AF.Exp),
                 reads=[f"bank{li}"], writes=[f"At{ci}"])
            if diag:
                k.op('dve', lambda e: e.tensor_tensor(out=Am[ci][:], in0=At[ci][:], in1=sbm[:, kb - 4 * tl, :], op=ALU.mult),
                     reads=[f"At{ci}", 'sbm'], writes=[f"Am{ci}"])

        def st_V(u):
            kb = top - u
            ci = u % 2
            diag = kb >= 4 * tl
            asrc, akey = (Am[ci], f"Am{ci}") if diag else (At[ci], f"At{ci}")
            k.op('pe', lambda e: e.matmul(banks[OB][0:64, :], lhsT=SV[:, kb, :], rhs=asrc[:],
                                          start=(kb == top), stop=(kb == 0)),
                 reads=[f"SV{kb}", akey], writes=['bank4'])

        for step in range(U + 2):
            if step < U:
                st_Z(step)
            if 1 <= step <= U:
                st_L(step - 1)
            if step < U:
                st_S(step)
            if step >= 2:
                st_V(step - 2)
        k.op('dve', lambda e, r=r: e.tensor_tensor(out=ysT[r][:], in0=banks[OB][0:64, :], in1=ssz[:], op=ALU.mult),
             reads=['bank4', 'ssz'], writes=[f"ysT{r}"])
        k.dma('sp', yT_o('s', tl), ysT[r][:], reads=[f"ysT{r}"], writes=[f"oys{tl}"])
        cx.outkeys.append(f"oys{tl}")


def _consts():
    s = np.arange(128)
    mtri = (s[:, None] <= s[None, :]).astype(np.float32)
    nui = -(s[:, None] >= s[None, :]).astype(np.float32)
    t = np.arange(512)
    sbm = np.stack([((i * 128 + s)[:, None] < t[None, :]).astype(np.float32) for i in range(4)])
    return mtri, nui, sbm


def _bands(w):
    s = np.arange(128)
    out = np.zeros((3, 128, 128), np.float32)
    for t in range(128):
        lo = max(t + 1 - w, 0)
        out[0, lo:t + 1, t] += 1.0 / (t + 1 - lo)
        out[0, t, t] -= 1.0
        lo = max(t + 1 - w, 0)
        out[1, lo:t + 1, t] += 1.0 / w
        out[1, t, t] -= 1.0
        nprev = w - (t + 1)
        if nprev > 0:
            out[2, 128 - nprev:, t] += 1.0 / w
    return out


def hostA_inputs(inp, l, x_full):
    mtri, nui, sbm = _consts()
    ident = np.eye(128, dtype=np.float32)
    w_in = inp["w_in"][l]
    maps = []
    for core in range(8):
        b, hh = core // 4, core % 4
        c128 = slice(hh * 128, (hh + 1) * 128)
        c64 = slice(hh * 64, (hh + 1) * 64)
        off = dict(mq=0, mk=512, mv=1024, mi=1536, mf=1540, mo=1544, mz=2056, pu=2568, pz=2824,
                   sq=3080, sk=3336, sv=3592, sz=3848)
        col = lambda nm, sl: w_in[:, off[nm] + sl.start: off[nm] + sl.stop]
        wtm = np.concatenate([col('mv', c128), col('mo', c128), col('mz', c128), col('pu', c64), col('sv', c64)], 1)
        wgt = np.zeros((D, 16), np.float32)
        wgt[:, 0] = w_in[:, off['mi'] + hh]
        wgt[:, 1] = w_in[:, off['mf'] + hh]
        wfm = np.concatenate([col('mq', c128), col('mk', c128), col('pz', c64), col('sz', c64),
                              col('sq', c64), col('sk', c64)], 1)
        mg = inp["m_gate_b"][l]
        mgb = np.tile(np.array([[mg[hh], mg[4 + hh]]], np.float32), (128, 1))
        cwl = inp["conv_w"][l]
        cw = np.concatenate([cwl[:, c128].T, cwl[:, 512 + hh * 128: 512 + (hh + 1) * 128].T], 1)
        cbl = inp["conv_b"][l]
        cb = np.stack([cbl[c128], cbl[512 + hh * 128: 512 + (hh + 1) * 128]], 1)
        maps.append(dict(
            cT=np.ascontiguousarray(inp["c"][b].reshape(8, 128).T),
            ng=np.ascontiguousarray(inp["norm_g"][l].reshape(8, 128).T),
            w_ada=inp["w_ada"][l], b_ada=inp["b_ada"][l].reshape(1, -1),
            wtm=np.ascontiguousarray(wtm), wgt=np.ascontiguousarray(wgt), wfm=np.ascontiguousarray(wfm),
            mgb=mgb, cw=np.ascontiguousarray(cw), cb=np.ascontiguousarray(cb),
            mng=np.ascontiguousarray(inp["m_norm_g"][l][c128].reshape(1, 128)),
            poolw=np.ascontiguousarray(inp["pool_w"][l][hh]),
            pscale=np.ascontiguousarray(inp["pool_scale"][l][c64].reshape(64, 1)),
            bands=_bands(POOL_WINDOWS[hh]), mtri=mtri, sbm=sbm, nui=nui, ident=ident))
    return maps


def hostA_gather(results):
    out = np.zeros((NB, D, SEQ), dtype=ml_dtypes.bfloat16)
    for core in range(8):
        b, hh = core // 4, core % 4
        y = results[core]["yTo"]
        out[b, hh * 128:(hh + 1) * 128] = y[0:128]
        out[b, 512 + hh * 64:512 + (hh + 1) * 64] = y[128:192]
        out[b, 768 + hh * 64:768 + (hh + 1) * 64] = y[192:256]
    return out


def hostB_inputs(inp, l, x_full, yT_full):
    maps = []
    ident = np.eye(128, dtype=np.float32)
    wbr = np.concatenate([inp["w_br_m"][l], inp["w_br_p"][l], inp["w_br_s"][l]], 0)
    for core in range(8):
        b, j = core // 4, core % 4
        sl = slice(j * NTB, (j + 1) * NTB)
        maps.append(dict(
            cT=np.ascontiguousarray(inp["c"][b].reshape(8, 128).T),
            ng=np.ascontiguousarray(inp["norm_g"][l].reshape(8, 128).T),
            w_ada=inp["w_ada"][l], b_ada=inp["b_ada"][l].reshape(1, -1),
            wg=np.ascontiguousarray(inp["w_in"][l][:, 4104:]),
            gb=np.ascontiguousarray(inp["gate_b"][l].reshape(24, 128).T),
            wbr=wbr, wout=inp["w_out"][l], fg=inp["final_g"].reshape(1, -1), ident=ident))
    return maps


RG4 = [[0, 1, 2, 3], [4, 5, 6, 7]]
A_NAMES = dict(cT=[128, 8], ng=[128, 8], w_ada=[D, 3 * D], b_ada=[1, 3 * D], wtm=[D, 512], wgt=[D, 16],
               wfm=[D, 512], mgb=[128, 2], cw=[128, 8], cb=[128, 2], mng=[1, 128], poolw=[64, 64],
               pscale=[64, 1], bands=[3, 128, 128])
B_NAMES = dict(wg=[D, 3 * D], gb=[128, 24], wbr=[D, D], wout=[D, D])
C_NAMES = dict(mtri=[128, 128], sbm=[4, 128, 512], nui=[128, 128], ident=[128, 128], fg=[1, D])


def build_fused(ntile=SEQ // 512):
    nc = bass.Bass("TRN2", target_bir_lowering=False)
    dt_in = lambda name, shape, dt=F32: nc.dram_tensor(name, list(shape), dt, kind="ExternalInput").ap()
    x_in = dt_in("x", [SEQ, D])
    cst = {n: dt_in(n, sh) for n, sh in C_NAMES.items()}
    lay = []
    for l in range(DEPTH):
        d = {n: dt_in(f"{n}_{l}", sh) for n, sh in A_NAMES.items()}
        d.update({n: dt_in(f"{n}_{l}", sh) for n, sh in B_NAMES.items()})
        lay.append(d)
    out_d = nc.dram_tensor("out", [NTB, D], F32, kind="ExternalOutput").ap()
    internal = lambda name, shape, dt: nc.dram_tensor(name, list(shape), dt).ap()
    ys = internal("ys", [4 * 256, NTB], BF16)
    yr = internal("yr", [4 * 1024, NTB], BF16)
    yown = internal("yown", [D, NTB], BF16)
    xown = internal("xown", [NTB, D], F32)
    xs1 = internal("xs1", [NTB, D], F32)
    xg1 = internal("xg1", [8 * 1024, D], F32)
    ROW = dict(m=(0, 128), p=(128, 64), s=(192, 64))

    def ys_out(nm, tl):
        r0, nr = ROW[nm]
        q = tl // 4
        return ys[q * 256 + r0:q * 256 + r0 + nr, (tl % 4) * 512:(tl % 4 + 1) * 512]

    with contextlib.ExitStack() as outer:
        k = K(nc, outer)
        PID = nc.partition_id()

        def quarter():
            return PID % 4

        for l in range(DEPTH):
            last = (l == DEPTH - 1)
            with contextlib.ExitStack() as st:
                cx = Ctx(nc, st, k, prefix=f"A{l}_")
                a = dict(lay[l])
                a.update(cst)
                if l == 0:
                    a['x'] = lambda t0: x_in[t0:t0 + 128, :]
                else:
                    def xg_tile(t0):
                        rank, c, r0 = t0 // NTB, (t0 % NTB) // 256, t0 % 256
                        return xg1[c * 1024 + rank * 256 + r0:c * 1024 + rank * 256 + r0 + 128, :]
                    a['x'] = xg_tile
                    a['xreads'] = lambda t0: [f'xg1_{(t0 % NTB) // 256}']
                emit_A(cx, a, ys_out, ntile)
                akeys = list(cx.outkeys)
                k.barrier()
                k.emit()
            for q in range(4):
                k.coll("AllGather", RG4, ys[q * 256:(q + 1) * 256, :], yr[q * 1024:(q + 1) * 1024, :],
                       reads=akeys, writes=['yrecv'])
            for i in range(2):
                k.dma('sp', yown[i * 512:(i + 1) * 512, :],
                      (lambda i=i: yr.rearrange("(q r) t -> q r t", q=4)[
                          bass.ds(quarter(), 1), i * 512:(i + 1) * 512, :].squeeze(0)),
                      reads=['yrecv'], writes=[f'yown_{i}'])
            ykeys = ['yown_0', 'yown_1']
            if l == 0:
                for i in range(4):
                    k.dma('sp', xown[i * 512:(i + 1) * 512, :],
                          (lambda i=i: x_in.rearrange("(q r) d -> q r d", q=4)[
                              bass.ds(quarter(), 1), i * 512:(i + 1) * 512, :].squeeze(0)), writes=[f'xown{i}'])
            with contextlib.ExitStack() as st:
                cx = Ctx(nc, st, k, prefix=f"B{l}_")

                def yT_d(tl):
                    cs = slice(tl * 512, (tl + 1) * 512)
                    res = []
                    for kk in range(4):
                        res.append((kk, 0, 128, yown[kk * 256:kk * 256 + 128, cs]))
                    for j, r0 in ((0, 128), (1, 192)):
                        for half in range(2):
                            kk = 4 + 2 * j + half
                            for hh2 in range(2):
                                rank = 2 * half + hh2
                                res.append((kk, hh2 * 64, 64, yown[rank * 256 + r0:rank * 256 + r0 + 64, cs]))
                    return res

                if l == 0:
                    x_d = lambda t0: xown[t0:t0 + 128, :]
                    x_reads = [f'xown{i}' for i in range(4)]
                else:
                    x_d = lambda t0: xs1[t0:t0 + 128, :]
                    x_reads = ['xs1']
                d = lay[l]
                def after_block(t0):
                    if (t0 + 128) % 256 == 0:
                        c = t0 // 256
                        k.coll("AllGather", RG4, xs1[c * 256:(c + 1) * 256, :], xg1[c * 1024:(c + 1) * 1024, :],
                               reads=[f"xo{t0 - 128}", f"xo{t0}"], writes=[f'xg1_{c}'])
                emit_B(cx, last, x_d, yT_d, d['cT'], d['ng'], d['w_ada'], d['b_ada'], d['wg'], d['gb'], d['wbr'],
                       d['wout'], cst['fg'], cst['ident'], out_d if last else xs1, x_reads=x_reads, y_reads=ykeys,
                       after_block=None if last else after_block)
                bkeys = list(cx.outkeys)
                if last:
                    k.wait_all('sp', bkeys)
                else:
                    k.barrier()
                k.emit()
            if not last:
                k.last_w['xs1'] = k.last_w[bkeys[-1]]
    return nc


def host_inputs(inp):
    mtri, nui, sbm = _consts()
    ident = np.eye(128, dtype=np.float32)
    maps = [dict(mtri=mtri, nui=nui, sbm=sbm, ident=ident, fg=inp["final_g"].reshape(1, -1).astype(np.float32))
            for _ in range(8)]
    for core in range(8):
        maps[core]["x"] = np.ascontiguousarray(inp["x"][core // 4])
    dummy_x = None
    for l in range(DEPTH):
        ma = hostA_inputs(inp, l, None)
        mb = hostB_inputs(inp, l, None, None)
        for core in range(8):
            for n in A_NAMES:
                maps[core][f"{n}_{l}"] = ma[core][n]
            for n in B_NAMES:
                maps[core][f"{n}_{l}"] = mb[core][n]
    return maps


_NC = {}


def kernel(**inputs):
    inp = {k: np.asarray(v) for k, v in inputs.items()}
    if 'nc' not in _NC:
        _NC['nc'] = build_fused()
    res = run_bass_kernel_spmd(_NC['nc'], host_inputs(inp), core_ids=list(range(8)))
    out = np.stack([np.concatenate([res.results[b * 4 + j]["out"] for j in range(4)], 0) for b in range(NB)])
    return out.astype(np.float32)
```

```python
import contextlib
import os
import numpy as np
import ml_dtypes
import concourse.bass as bass
import concourse.mybir as mybir
from concourse.bass_utils import run_bass_kernel_spmd

F32 = mybir.dt.float32
BF16 = mybir.dt.bfloat16
AF = mybir.ActivationFunctionType
ALU = mybir.AluOpType
AX = mybir.AxisListType

D = 1024
SEQ = 8192
NB = 2
DEPTH = 2
EPS = 1e-6
EPOCH = 12000
POOL_WINDOWS = (2, 4, 8, 16)


class _Rec:
    def __init__(self):
        self.calls = []

    def __getattr__(self, name):
        def f(*a, **kw):
            self.calls.append((name, a, kw))
        return f


class K:
    def __init__(self, nc, stack, n_dma_sems=12):
        self.nc = nc
        self.stack = stack
        self.engs = ['pe', 'act', 'dve', 'pool', 'sp']
        self.q = {e: [] for e in self.engs}
        self.cnt = {e: 0 for e in self.engs}
        self.epoch = {e: 0 for e in self.engs}
        self.sems = {}
        self.known = {e: {} for e in self.engs}
        self.last_w = {}
        self.readers = {}
        self.dma_sems = {q: [stack.enter_context(nc.semaphore(f"dma_{q}{i}")) for i in range(n)]
                         for q, n in (('sp', 10), ('pool', 6), ('act', 2))}
        self.dma_cnt = {q: [0] * len(v) for q, v in self.dma_sems.items()}
        self.dma_rr = {q: 0 for q in self.dma_sems}

    def _sem(self, e):
        key = (e, self.epoch[e])
        if key not in self.sems:
            self.sems[key] = self.stack.enter_context(self.nc.semaphore(f"s_{e}_{self.epoch[e]}"))
        return self.sems[key]

    def _need(self, e, tok):
        if tok is None:
            return
        sem, val, src = tok[:3]
        if src == e and e == 'pe':
            return
        kn = self.known[e]
        if kn.get(id(sem), 0) >= val:
            return
        kn[id(sem)] = val
        self.q[e].append(('wait', sem, val))

    def _deps(self, e, reads, writes):
        for k in reads:
            self._need(e, self.last_w.get(k))
            if k.startswith('bank') or k == 'tpb':
                for r in self.readers.get(k, ()):
                    if r[2] != e:
                        self._need(e, r)
        for k in writes:
            self._need(e, self.last_w.get(k))
            for r in self.readers.get(k, ()):
                if r[2] == e:
                    continue
                self._need(e, r)

    def _commit(self, tok, reads, writes):
        for k in writes:
            self.last_w[k] = tok
            self.readers[k] = []
        for k in reads:
            self.readers.setdefault(k, []).append(tok)

    def op(self, e, fn, reads=(), writes=()):
        self._deps(e, reads, writes)
        if self.cnt[e] >= EPOCH:
            self.epoch[e] += 1
            self.cnt[e] = 0
        sem = self._sem(e)
        self.cnt[e] += 1
        tok = (sem, self.cnt[e], e)
        rec = _Rec()
        fn(rec)
        assert len(rec.calls) == 1
        name, a, kw = rec.calls[0]
        self.q[e].append(('op', (lambda eng, name=name, a=a, kw=kw: getattr(eng, name)(*a, **kw)), sem, 1))
        self._commit(tok, reads, writes)
        return tok

    def dma(self, e, out, in_, reads=(), writes=(), **kw):
        self._deps(e, reads, writes)
        i = self.dma_rr[e]
        self.dma_rr[e] = (i + 1) % len(self.dma_sems[e])
        sem = self.dma_sems[e][i]
        cnts = self.dma_cnt[e]
        if cnts[i] > 0:
            self._need(e, (sem, cnts[i] * 16, 'dma'))
        cnts[i] += 1
        tok = (sem, cnts[i] * 16, 'dma')
        self.q[e].append(('op', lambda eng: eng.dma_start(
            out=(out() if callable(out) else out), in_=(in_() if callable(in_) else in_), **kw), sem, 16))
        self._commit(tok, reads, writes)
        return tok

    def coll(self, kind, groups, src, dst, reads=(), writes=()):
        e = 'pool'
        self._deps(e, reads, writes)
        if not hasattr(self, 'cc_sem'):
            self.cc_sem = self.stack.enter_context(self.nc.semaphore("cc_sem"))
            self.cc_cnt = 0
        if self.cc_cnt > 0:
            self._need(e, (self.cc_sem, self.cc_cnt, 'dma'))
        self.cc_cnt += 1
        tok = (self.cc_sem, self.cc_cnt, 'dma')
        self.q[e].append(('op', lambda eng: eng.collective_compute(
            kind, ALU.bypass, groups, ins=[src.opt()], outs=[dst.opt()]), self.cc_sem, 1))
        self._commit(tok, reads, writes)
        return tok

    def wait_all(self, e, keys):
        for k in keys:
            self._need(e, self.last_w.get(k))

    def barrier(self):
        toks = []
        for f in self.engs:
            if self.cnt[f] > 0:
                toks.append((self._sem(f), self.cnt[f], f))
        for q, sems in self.dma_sems.items():
            for i, sem in enumerate(sems):
                if self.dma_cnt[q][i] > 0:
                    toks.append((sem, self.dma_cnt[q][i] * 16, 'dma'))
        if getattr(self, 'cc_cnt', 0) > 0:
            toks.append((self.cc_sem, self.cc_cnt, 'dma'))
        for e in self.engs:
            for t in toks:
                if t[2] != e:
                    self._need(e, t)

    def emit(self):
        nc = self.nc
        with nc.Block() as block:
            def replay(name, eng):
                for it in self.q[name]:
                    if it[0] == 'wait':
                        eng.wait_ge(it[1], it[2])
                    else:
                        it[1](eng).then_inc(it[2], it[3])

            @block.sync
            def _(eng):
                replay('sp', eng)

            @block.scalar
            def _(eng):
                replay('act', eng)

            @block.vector
            def _(eng):
                replay('dve', eng)

            @block.gpsimd
            def _(eng):
                replay('pool', eng)

            @block.tensor
            def _(eng):
                replay('pe', eng)
        for e in self.engs:
            self.q[e] = []


class Ctx:
    def __init__(self, nc, st, k=None, prefix=""):
        self.nc = nc
        self.st = st
        self.k = k if k is not None else K(nc, st)
        self.n = 0
        self.outkeys = []
        self.prefix = prefix

    def sb(self, name, shape, dt):
        return self.st.enter_context(self.nc.sbuf_tensor("s_" + self.prefix + name, list(shape), dt))

    def ps(self, name, shape, dt):
        return self.st.enter_context(self.nc.psum_tensor("p_" + self.prefix + name, list(shape), dt))


def emit_mod(cx, w_ada, b_ada, cT_d, ng_d, banks, need_gate):
    k = cx.k
    ncol = 3 if need_gate else 2
    cT = cx.sb("cT", [128, 8], F32)
    ng = cx.sb("ng", [128, 8], F32)
    modrow = cx.sb("modrow", [1, 3072], F32)
    one11 = cx.sb("one11", [1, 128], F32)
    s1 = cx.sb("s1", [128, 8], F32)
    s2 = cx.sb("s2", [128, 8], F32)
    gate_bc = cx.sb("gate_bc", [128, 1024], F32) if need_gate else None
    NWA = 4
    wa = [cx.sb(f"wa{i}", [128, 512], F32) for i in range(NWA)]
    k.dma('sp', cT[:], cT_d, writes=['cT'])
    k.dma('sp', ng[:], ng_d, writes=['ng'])
    k.op('dve', lambda e: e.memset(one11[:], 1.0), writes=['one11'])
    ngrp = ncol * 2
    i = 0
    for kc in range(8):
        for cg in range(ngrp):
            buf = wa[i % NWA]
            bk = f"wa{i % NWA}"
            i += 1
            k.dma('sp', buf[:], w_ada[kc * 128:(kc + 1) * 128, cg * 512:(cg + 1) * 512], writes=[bk])
            k.op('pe', lambda e, buf=buf, cg=cg, kc=kc: e.matmul(
                banks[cg][0:1, :], lhsT=cT[:, kc:kc + 1], rhs=buf[:], start=(kc == 0), stop=(kc == 7)),
                reads=[bk, 'cT'], writes=[f"bank{cg}"])
    for cg in range(ngrp):
        buf = wa[i % NWA]
        bk = f"wa{i % NWA}"
        i += 1
        k.dma('sp', buf[0:1, :], b_ada[0:1, cg * 512:(cg + 1) * 512], writes=[bk])
        k.op('dve', lambda e, cg=cg, buf=buf: e.tensor_tensor(
            out=modrow[0:1, cg * 512:(cg + 1) * 512], in0=banks[cg][0:1, :],
            in1=buf[0:1, :], op=ALU.add),
            reads=[f"bank{cg}", bk], writes=['modrow'])
    colb = banks[6]
    for cc in range(8):
        k.op('pe', lambda e, cc=cc: e.matmul(
            colb[:, cc:cc + 1], lhsT=modrow[0:1, 1024 + cc * 128:1024 + (cc + 1) * 128],
            rhs=one11[0:1, 0:1], start=True, stop=True), reads=['modrow', 'one11'], writes=['bank6'])
        k.op('pe', lambda e, cc=cc: e.matmul(
            colb[:, 8 + cc:9 + cc], lhsT=modrow[0:1, cc * 128:(cc + 1) * 128],
            rhs=one11[0:1, 0:1], start=True, stop=True), reads=['modrow', 'one11'], writes=['bank6'])
    k.op('dve', lambda e: e.scalar_tensor_tensor(
        out=s1[:], in0=colb[:, 0:8], scalar=1.0, in1=ng[:], op0=ALU.add, op1=ALU.mult),
        reads=['bank6', 'ng'], writes=['s1'])
    k.op('dve', lambda e: e.tensor_copy(out=s2[:], in_=colb[:, 8:16]), reads=['bank6'], writes=['s2'])
    if need_gate:
        for hh in range(2):
            k.op('pe', lambda e, hh=hh: e.matmul(
                banks[hh][:, :], lhsT=one11[0:1, 0:128], rhs=modrow[0:1, 2048 + hh * 512:2048 + (hh + 1) * 512],
                start=True, stop=True), reads=['modrow', 'one11'], writes=[f"bank{hh}"])
            k.op('dve', lambda e, hh=hh: e.tensor_copy(out=gate_bc[:, hh * 512:(hh + 1) * 512], in_=banks[hh][:, :]),
                 reads=[f"bank{hh}"], writes=['gate_bc'])
    return s1, s2, gate_bc


def emit_norm_tile(cx, xt, xkey, tb, s1, s2, ident, tp, tpkey, hT, hkey, tagn):
    k = cx.k
    i = cx.n
    cx.n += 1
    r = i % 2
    if not hasattr(cx, 'nrm'):
        cx.nrm = dict(
            sq=[cx.sb("nsq", [128, 1024], BF16)] * 2,
            st=[cx.sb(f"nst{j}", [128, 4], F32) for j in range(2)],
            xn=[cx.sb(f"nxn{j}", [128, 1024], BF16) for j in range(2)],
            nf=cx.sb("nrm_nf", [128, 8, 128], F32),
        )
    sq, stt, xn = cx.nrm['sq'][r], cx.nrm['st'][r], cx.nrm['xn'][r]
    ksq, kst, kxn = "nsq", f"nst{r}", f"nxn{r}"
    k.op('dve', lambda e: e.memset(stt[:], 0.0), writes=[kst])
    k.op('act', lambda e: e.activation(out=sq[:], in_=xt, func=AF.Square, accum_out=stt[:, 0:1]),
         reads=[xkey, kst], writes=[ksq, kst])
    k.op('act', lambda e: e.activation(out=stt[:, 1:2], in_=stt[:, 0:1], func=AF.Ln, scale=1.0 / D, bias=EPS),
         reads=[kst], writes=[kst])
    k.op('act', lambda e: e.activation(out=stt[:, 2:3], in_=stt[:, 1:2], func=AF.Exp, scale=-0.5),
         reads=[kst], writes=[kst])
    k.op('dve', lambda e: e.tensor_scalar(out=xn[:], in0=xt, scalar1=stt[:, 2:3], scalar2=None, op0=ALU.mult),
         reads=[xkey, kst], writes=[kxn])
    for kc in range(8):
        k.op('pe', lambda e, kc=kc: e.transpose(tp[:, kc * 128:(kc + 1) * 128], xn[:, kc * 128:(kc + 1) * 128], ident[:]),
             reads=[kxn, 'ident'], writes=[tpkey])
    hv = hT[:, :, tb * 128:(tb + 1) * 128]
    tpv = tp[:, :].rearrange("p (k t) -> p k t", k=8)
    nf = cx.nrm['nf']
    k.op('dve', lambda e: e.tensor_tensor(out=nf[:, :, :], in0=tpv, in1=s1[:, :].unsqueeze(2).broadcast_to([128, 8, 128]),
                                          op=ALU.mult), reads=[tpkey, 's1'], writes=['nrm_nf'])
    k.op('dve', lambda e: e.tensor_tensor(out=hv, in0=nf[:, :, :], in1=s2[:, :].unsqueeze(2).broadcast_to([128, 8, 128]),
                                          op=ALU.add), reads=['nrm_nf', 's2'], writes=[hkey])


NTB = 2048


def emit_B(cx, last, x_d, yT_d, cT_d, ng_d, wada_d, bada_d, wg_d, gb_d, wbr_d, wout_d, fg_d, id_d, xo_d, x_reads=(), y_reads=(), after_block=None, after_prologue=None, pool_hook=None):
    k = cx.k
    nc = cx.nc
    banks = [cx.ps(f"bank{i}", [128, 512], F32) for i in range(7)]
    tp = cx.ps("tpb", [128, 1024], BF16)
    ident = cx.sb("ident", [128, 128], BF16)
    k.dma('pool', ident[:], id_d, writes=['ident'])
    s1, s2, gate_bc = emit_mod(cx, wada_d, bada_d, cT_d, ng_d, banks, True)
    wg = cx.sb("wg", [128, 8, 3 * D], BF16)
    wbr = cx.sb("wbr", [128, 8, D], BF16)
    wout = cx.sb("wout", [128, 8, D], BF16)
    gb = cx.sb("gb", [128, 24], F32)
    k.dma('sp', gb[:], gb_d, writes=['gb'])
    wlist = [(wg, wg_d, 'wg', kc) for kc in range(8)] + [(wbr, wbr_d, 'wbr', kc) for kc in range(8)] + \
            [(wout, wout_d, 'wout', kc) for kc in range(8)]
    for i, (wt_, wd_, nm_, kc) in enumerate(wlist):
        if pool_hook is not None and i % 6 == 0:
            pool_hook(i // 6)
        k.dma('pool', wt_[:, kc, :], wd_[kc * 128:(kc + 1) * 128, :], writes=[f'{nm_}{kc}'])
    if last:
        fg_bc = cx.sb("fg_bc", [128, D], F32)
        k.dma('sp', fg_bc[:], fg_d.partition_broadcast(128), writes=['fg_bc'])
    if after_prologue is not None:
        after_prologue()
    xres = cx.sb("xres", [128, 4, D], F32)
    hT = [cx.sb(f"hT{i}", [128, 8, 512], BF16) for i in range(2)]
    yT = [cx.sb(f"yT{i}", [128, 8, 512], BF16) for i in range(2)]
    mT = cx.sb("mT", [128, 8, 512], BF16)
    sig = [cx.sb(f"sig{i}", [128, 512], F32) for i in range(3)]
    tmp = [cx.sb(f"tmp{i}", [128, 512], F32) for i in range(3)]
    xn_o = [cx.sb(f"xno{i}", [128, D], F32) for i in range(2)]
    fst = [cx.sb(f"fst{i}", [128, 4], F32) for i in range(2)]
    yo = [cx.sb(f"yo{i}", [128, D], F32) for i in range(2)]
    fsq = cx.sb("fsq", [128, D], BF16)
    nG = 0
    nP = 0
    nO = 0
    ntile = NTB // 512
    for tl in range(ntile):
        r = tl % 2
        for (kk, p0, pn, src) in yT_d(tl):
            k.dma('sp', yT[r][p0:p0 + pn, kk, :], src, reads=y_reads, writes=[f"yT{r}_{kk}_{p0}"])
        for tb in range(4):
            t0 = tl * 512 + tb * 128
            k.dma('sp', xres[:, tb, :], x_d(t0), reads=x_reads, writes=[f"xres{tb}"])
            emit_norm_tile(cx, xres[:, tb, :], f"xres{tb}", tb, s1, s2, ident, tp, 'tpb', hT[r], f"hT{r}", 'b')
        for dc in range(8):
            for gi in range(3):
                gbk = 0 + (nG % 2)
                nG += 1
                for kc in range(8):
                    k.op('pe', lambda e, gbk=gbk, gi=gi, kc=kc, dc=dc, r=r: e.matmul(
                        banks[gbk][:, :], lhsT=wg[:, kc, gi * D + dc * 128:gi * D + (dc + 1) * 128],
                        rhs=hT[r][:, kc, :], start=(kc == 0), stop=(kc == 7)),
                        reads=[f'wg{kc}', f"hT{r}"], writes=[f"bank{gbk}"])
                k.op('act', lambda e, gbk=gbk, gi=gi, dc=dc: e.activation(
                    out=sig[gi][:], in_=banks[gbk][:, :], func=AF.Sigmoid,
                    bias=gb[:, gi * 8 + dc:gi * 8 + dc + 1], scale=1.0),
                    reads=[f"bank{gbk}", 'gb'], writes=[f"sig{gi}"])
            for bi, (k0, k1) in enumerate(((0, 4), (4, 6), (6, 8))):
                pbk = 2 + (nP % 2)
                nP += 1
                for kc in range(k0, k1):
                    k.op('pe', lambda e, pbk=pbk, kc=kc, dc=dc, r=r, k0=k0, k1=k1: e.matmul(
                        banks[pbk][:, :], lhsT=wbr[:, kc, dc * 128:(dc + 1) * 128],
                        rhs=yT[r][:, kc, :], start=(kc == k0), stop=(kc == k1 - 1)),
                        reads=[f'wbr{kc}'] + [f"yT{r}_{kc}_{p0}" for p0 in (0, 64)], writes=[f"bank{pbk}"])
                k.op('dve', lambda e, pbk=pbk, bi=bi: e.tensor_tensor(
                    out=tmp[bi][:], in0=banks[pbk][:, :], in1=sig[bi][:], op=ALU.mult),
                    reads=[f"bank{pbk}", f"sig{bi}"], writes=[f"tmp{bi}"])
            k.op('dve', lambda e: e.tensor_tensor(out=tmp[0][:], in0=tmp[0][:], in1=tmp[1][:], op=ALU.add),
                 reads=['tmp0', 'tmp1'], writes=['tmp0'])
            k.op('dve', lambda e, dc=dc: e.tensor_tensor(out=mT[:, dc, :], in0=tmp[0][:], in1=tmp[2][:], op=ALU.add),
                 reads=['tmp0', 'tmp2'], writes=['mT'])
        for tb in range(4):
            t0 = tl * 512 + tb * 128
            ro = nO % 2
            nO += 1
            for ch in range(2):
                obk = 4 + ch
                for kc in range(8):
                    k.op('pe', lambda e, obk=obk, kc=kc, tb=tb, ch=ch: e.matmul(
                        banks[obk][:, :], lhsT=mT[:, kc, tb * 128:(tb + 1) * 128],
                        rhs=wout[:, kc, ch * 512:(ch + 1) * 512], start=(kc == 0), stop=(kc == 7)),
                        reads=['mT', f'wout{kc}'], writes=[f"bank{obk}"])
                k.op('dve', lambda e, obk=obk, ch=ch, ro=ro: e.tensor_tensor(
                    out=xn_o[ro][:, ch * 512:(ch + 1) * 512], in0=banks[obk][:, :],
                    in1=gate_bc[:, ch * 512:(ch + 1) * 512], op=ALU.mult),
                    reads=[f"bank{obk}", 'gate_bc'], writes=[f"xno{ro}"])
            k.op('dve', lambda e, ro=ro, tb=tb: e.tensor_tensor(
                out=xn_o[ro][:], in0=xn_o[ro][:], in1=xres[:, tb, :], op=ALU.add),
                reads=[f"xno{ro}", f"xres{tb}"], writes=[f"xno{ro}"])
            if not last:
                k.dma('sp', xo_d[t0:t0 + 128, :], xn_o[ro][:], reads=[f"xno{ro}"], writes=[f"xo{t0}"])
                cx.outkeys.append(f"xo{t0}")
                if after_block is not None:
                    after_block(t0)
            else:
                k.op('dve', lambda e, ro=ro: e.memset(fst[ro][:], 0.0), writes=[f"fst{ro}"])
                k.op('act', lambda e, ro=ro: e.activation(out=fsq[:], in_=xn_o[ro][:], func=AF.Square,
                                                           accum_out=fst[ro][:, 0:1]),
                     reads=[f"xno{ro}", f"fst{ro}"], writes=['fsq', f"fst{ro}"])
                k.op('act', lambda e, ro=ro: e.activation(out=fst[ro][:, 1:2], in_=fst[ro][:, 0:1], func=AF.Ln,
                                                           scale=1.0 / D, bias=EPS),
                     reads=[f"fst{ro}"], writes=[f"fst{ro}"])
                k.op('act', lambda e, ro=ro: e.activation(out=fst[ro][:, 2:3], in_=fst[ro][:, 1:2], func=AF.Exp, scale=-0.5),
                     reads=[f"fst{ro}"], writes=[f"fst{ro}"])
                k.op('dve', lambda e, ro=ro: e.scalar_tensor_tensor(
                    out=yo[ro][:], in0=xn_o[ro][:], scalar=fst[ro][:, 2:3], in1=fg_bc[:],
                    op0=ALU.mult, op1=ALU.mult), reads=[f"xno{ro}", f"fst{ro}", 'fg_bc'], writes=[f"yo{ro}"])
                k.dma('sp', xo_d[t0:t0 + 128, :], yo[ro][:], reads=[f"yo{ro}"], writes=[f"xo{t0}"])
                cx.outkeys.append(f"xo{t0}")


def emit_A(cx, a, yT_o, ntile):
    import os
    STAGE = int(os.environ.get('A_STAGE', '9'))
    SUB = int(os.environ.get('A_SUB', '9'))
    DIS = os.environ.get('A_DIS', '')
    k = cx.k
    banks = [cx.ps(f"bank{i}", [128, 512], F32) for i in range(7)]
    tp = cx.ps("tpb", [128, 1024], BF16)
    ZB = (0, 1)
    LB = (2, 3)
    OB = 4
    PB = 5
    MB = 6
    ident = cx.sb("ident", [128, 128], BF16)
    k.dma('pool', ident[:], a['ident'], writes=['ident'])
    s1, s2, _ = emit_mod(cx, a['w_ada'], a['b_ada'], a['cT'], a['ng'], banks, False)
    wtm = cx.sb("wtm", [128, 8, 512], BF16)
    wfm = cx.sb("wfm", [128, 8, 512], BF16)
    wgt = cx.sb("wgt", [128, 8, 16], BF16)
    for kc in range(8):
        k.dma('pool', wtm[:, kc, :], a['wtm'][kc * 128:(kc + 1) * 128, :], writes=[f'wtm{kc}'])
        k.dma('pool', wfm[:, kc, :], a['wfm'][kc * 128:(kc + 1) * 128, :], writes=[f'wfm{kc}'])
        k.dma('pool', wgt[:, kc, :], a['wgt'][kc * 128:(kc + 1) * 128, :], writes=[f'wgt{kc}'])
    mgb = cx.sb("mgb", [128, 2], F32)
    nmgb = cx.sb("nmgb", [128, 2], F32)
    cw = cx.sb("cw", [128, 8], F32)
    cb = cx.sb("cb", [128, 2], F32)
    mng = cx.sb("mng", [128, 128], F32)
    poolw = cx.sb("poolw", [64, 64], BF16)
    pscale = cx.sb("pscale", [64, 1], F32)
    bands = cx.sb("bands", [128, 3, 128], BF16)
    mtri = cx.sb("mtri", [128, 128], F32)
    onesf = cx.sb("onesf", [128, 128], F32)
    sbm = cx.sb("sbm", [128, 4, 512], BF16)
    nui = cx.sb("nui", [128, 128], BF16)
    nones = cx.sb("nones", [128, 128], BF16)
    k.dma('sp', mgb[:], a['mgb'], writes=['mgb'])
    k.dma('sp', cw[:], a['cw'], writes=['cw'])
    k.dma('sp', cb[:], a['cb'], writes=['cb'])
    k.dma('sp', mng[:], a['mng'].partition_broadcast(128), writes=['mng'])
    k.dma('pool', poolw[:], a['poolw'], writes=['poolw'])
    k.dma('sp', pscale[:], a['pscale'], writes=['pscale'])
    for i in range(3):
        k.dma('pool', bands[:, i, :], a['bands'][i], writes=['bands'])
    k.dma('sp', mtri[:], a['mtri'], writes=['mtri'])
    for i in range(4):
        k.dma('pool', sbm[:, i, :], a['sbm'][i], writes=['sbm'])
    k.dma('pool', nui[:], a['nui'], writes=['nui'])
    k.op('dve', lambda e: e.memset(onesf[:], 1.0), writes=['onesf'])
    k.op('dve', lambda e: e.memset(nones[:], -1.0), writes=['nones'])
    k.op('dve', lambda e: e.tensor_scalar(out=nmgb[:], in0=mgb[:], scalar1=-1.0, scalar2=None, op0=ALU.mult),
         reads=['mgb'], writes=['nmgb'])
    sqT = cx.sb("sqT", [64, SEQ], BF16)
    skT = cx.sb("skT", [64, SEQ], BF16)
    SV = cx.sb("SV", [128, SEQ // 128, 64], BF16)
    Cn32 = cx.sb("Cn32", [128, 132], F32)
    Cnb = cx.sb("Cnb", [128, 132], BF16)
    k.op('dve', lambda e: e.memset(Cn32[:], 0.0), writes=['Cn32'])
    k.op('dve', lambda e: e.memset(Cnb[:], 0.0), writes=['Cnb'])
    xt = [cx.sb(f"xt{i}", [128, D], F32) for i in range(2)]
    hT = [cx.sb(f"hT{i}", [128, 8, 512], BF16) for i in range(2)]
    qkr = [cx.sb(f"qkr{i}", [128, 516], F32) for i in range(2)]
    for g in range(2):
        k.op('pool', lambda e, g=g: e.memset(qkr[g][:], 0.0), writes=[f"qkr{g}"])
    cacc = [cx.sb(f"cacc{i}", [128, 512], F32) for i in range(2)]
    csg = [cx.sb(f"csg{i}", [128, 512], F32) for i in range(2)]
    qT = cx.sb("qT", [128, 512], BF16)
    kT = cx.sb("kT", [128, 512], BF16)
    spz = cx.sb("spz", [64, 512], F32)
    ssz = cx.sb("ssz", [64, 512], F32)
    sgz = cx.sb("sgz", [64, 512], F32)
    Ub = [cx.sb(f"Ub{i}", [128, 64], BF16) for i in range(3)]
    tmS = [cx.sb(f"tmS{i}", [128, 512], F32) for i in range(2)]
    gsb = [cx.sb(f"gsb{i}", [128, 16], F32) for i in range(2)]
    sgo = [cx.sb(f"sgo{i}", [128, 256], F32) for i in range(2)]
    gz = [cx.sb(f"gz{i}", [128, 128], F32) for i in range(2)]
    V2 = [cx.sb(f"V2{i}", [128, 132], BF16) for i in range(2)]
    Ktm = [cx.sb(f"Ktm{i}", [128, 128], BF16) for i in range(2)]
    for i in range(2):
        k.op('pool', lambda e, i=i: e.memset(V2[i][:], 0.0), writes=[f"V2{i}"])
    Sm = [cx.sb(f"Sm{i}", [128, 128], BF16) for i in range(2)]
    t1 = [cx.sb(f"t1{i}", [128, 128], F32) for i in range(2)]
    t1sq = cx.sb("t1sq", [128, 128], BF16)
    ymb = [cx.sb(f"ymb{i}", [128, 128], BF16) for i in range(2)]
    ymT = [cx.sb(f"ymT{i}", [128, 512], BF16) for i in range(2)]
    pTs = cx.sb("pTs", [64, 512], BF16)
    ypT = [cx.sb(f"ypT{i}", [64, 512], BF16) for i in range(2)]
    ysT = [cx.sb(f"ysT{i}", [64, 512], BF16) for i in range(2)]
    Eb = [cx.sb(f"Eb{i}", [128, 512], F32) for i in range(2)]
    L32 = cx.sb("L32", [128, 512], F32)
    Lb = [cx.sb(f"Lb{i}", [128, 512], BF16) for i in range(2)]
    S32 = cx.sb("S32", [128, 512], F32)
    Sb = [cx.sb(f"Sb{i}", [128, 512], BF16) for i in range(2)]
    At = [cx.sb(f"At{i}", [128, 512], BF16) for i in range(2)]
    Am = [cx.sb(f"Am{i}", [128, 512], BF16) for i in range(2)]
    mb = banks[MB]
    cnt = dict(z=0, l=0, u=0, g=0, p=0)
    pend = dict(f=None)
    PR = [PB, 0, 1, 2, 3]

    def nextpb():
        i = PR[cnt['p'] % len(PR)]
        cnt['p'] += 1
        return banks[i], f"bank{i}"
    KSCALE = 128.0 ** -0.5
    nx = 0
    for tl in range(ntile if STAGE >= 2 else 0):
        r = tl % 2
        for tb in range(4):
            t0 = tl * 512 + tb * 128
            xr = nx % 2
            nx += 1
            k.dma('sp', xt[xr][:], a['x'](t0), reads=(a['xreads'](t0) if 'xreads' in a else ()), writes=[f"xt{xr}"])
            emit_norm_tile(cx, xt[xr][:], f"xt{xr}", tb, s1, s2, ident, tp, 'tpb', hT[r], f"hT{r}", 'a')
        hk = f"hT{r}"
        for g in range(2):
            pb, pk = nextpb()
            for kc in range(8):
                k.op('pe', lambda e, g=g, kc=kc, r=r: e.matmul(
                    pb[:, :], lhsT=wfm[:, kc, g * 128:(g + 1) * 128], rhs=hT[r][:, kc, :],
                    start=(kc == 0), stop=(kc == 7)), reads=[f'wfm{kc}', hk], writes=[pk])
            k.op('pool', lambda e, g=g: e.tensor_copy(out=qkr[g][:, 0:3], in_=qkr[g][:, 512:515]),
                 reads=[f"qkr{g}"], writes=[f"qkr{g}h"])
            k.op('act', lambda e, g=g: e.activation(out=qkr[g][:, 3:515], in_=pb[:, :], func=AF.Identity),
                 reads=[pk, f"qkr{g}h"], writes=[f"qkr{g}"])
            k.op('pool', lambda e, g=g: e.tensor_scalar(
                out=cacc[g][:], in0=qkr[g][:, 0:512], scalar1=cw[:, 4 * g:4 * g + 1], scalar2=cb[:, g:g + 1],
                op0=ALU.mult, op1=ALU.add), reads=[f"qkr{g}", f"qkr{g}h", 'cw', 'cb'], writes=[f"cacc{g}"])
            for j in range(1, 4):
                k.op('dve', lambda e, g=g, j=j: e.scalar_tensor_tensor(
                    out=cacc[g][:], in0=qkr[g][:, j:j + 512], scalar=cw[:, 4 * g + j:4 * g + j + 1],
                    in1=cacc[g][:], op0=ALU.mult, op1=ALU.add),
                    reads=[f"qkr{g}", f"qkr{g}h", f"cacc{g}", 'cw'], writes=[f"cacc{g}"])
            k.op('act', lambda e, g=g: e.activation(out=csg[g][:], in_=cacc[g][:], func=AF.Sigmoid),
                 reads=[f"cacc{g}"], writes=[f"csg{g}"])
            dst, dk, scl = (qT, 'qT', 1.0) if g == 0 else (kT, 'kT', KSCALE)
            k.op('dve', lambda e, g=g, dst=dst, scl=scl: e.scalar_tensor_tensor(
                out=dst[:], in0=cacc[g][:], scalar=scl, in1=csg[g][:], op0=ALU.mult, op1=ALU.mult),
                reads=[f"cacc{g}", f"csg{g}"], writes=[dk])
        for i4, nm in enumerate(('pz', 'sz', 'sq', 'sk')):
            c0 = 256 + i4 * 64
            pb, pk = nextpb()
            for kc in range(8):
                k.op('pe', lambda e, kc=kc, r=r, c0=c0: e.matmul(
                    pb[0:64, :], lhsT=wfm[:, kc, c0:c0 + 64], rhs=hT[r][:, kc, :],
                    start=(kc == 0), stop=(kc == 7)), reads=[f'wfm{kc}', hk], writes=[pk])
            if nm in ('pz', 'sz'):
                dst, dk = (spz, 'spz') if nm == 'pz' else (ssz, 'ssz')
                k.op('act', lambda e: e.activation(out=sgz[:], in_=pb[0:64, :], func=AF.Sigmoid),
                     reads=[pk], writes=['sgz'])
                k.op('dve', lambda e, dst=dst: e.tensor_tensor(out=dst[:], in0=pb[0:64, :], in1=sgz[:], op=ALU.mult),
                     reads=[pk, 'sgz'], writes=[dk])
            elif nm == 'sq':
                k.op('act', lambda e, tl=tl: e.activation(out=sqT[:, tl * 512:(tl + 1) * 512], in_=pb[0:64, :],
                                                           func=AF.Identity, scale=0.125),
                     reads=[pk], writes=[f"sqT{tl}"])
            else:
                k.op('dve', lambda e, tl=tl: e.tensor_copy(out=skT[:, tl * 512:(tl + 1) * 512], in_=pb[0:64, :]),
                     reads=[pk], writes=[f"skT{tl}"])
        if STAGE < 3:
            continue
        for tb in range(4):
            n = tl * 4 + tb
            tsl = slice(tb * 128, (tb + 1) * 128)
            pb, pk = nextpb()
            for kc in range(8):
                k.op('pe', lambda e, kc=kc, r=r, tsl=tsl: e.matmul(
                    pb[:, :], lhsT=hT[r][:, kc, tsl], rhs=wtm[:, kc, :], start=(kc == 0), stop=(kc == 7)),
                    reads=[f'wtm{kc}', hk], writes=[pk])
            for kc in range(8):
                k.op('pe', lambda e, kc=kc, r=r, tsl=tsl: e.matmul(
                    mb[:, 400:416], lhsT=hT[r][:, kc, tsl], rhs=wgt[:, kc, :], start=(kc == 0), stop=(kc == 7)),
                    reads=[f'wgt{kc}', hk], writes=['bank6'])
            gi = cnt['g'] % 2
            cnt['g'] += 1
            G = gsb[gi]
            gk = f"gsb{gi}"
            vi = n % 2
            k.op('pe', lambda e, tsl=tsl: e.transpose(tp[:, 0:128], kT[:, tsl], ident[:]),
                 reads=['kT', 'ident'], writes=['tpb'])
            k.op('dve', lambda e, vi=vi: e.tensor_copy(out=Ktm[vi][:], in_=tp[:, 0:128]),
                 reads=['tpb'], writes=[f"Ktm{vi}"])
            k.op('pe', lambda e, tsl=tsl: e.matmul(mb[:, 0:128], lhsT=kT[:, tsl], rhs=qT[:, tsl], start=True, stop=True),
                 reads=['kT', 'qT'], writes=['bank6'])
            k.op('dve', lambda e, vi=vi: e.tensor_tensor(out=Sm[vi][:], in0=mb[:, 0:128], in1=mtri[:], op=ALU.mult),
                 reads=['bank6', 'mtri'], writes=[f"Sm{vi}"])
            if pend['f'] is not None:
                pend['f']()
                pend['f'] = None
            ts = tmS[n % 2]
            tk = f"tmS{n % 2}"
            k.op('act', lambda e, ts=ts, pb=pb: e.activation(out=ts[:], in_=pb[:, :], func=AF.Identity),
                 reads=[pk], writes=[tk])
            k.op('pool', lambda e, n=n, ts=ts: e.tensor_copy(out=SV[:, n, :], in_=ts[:, 448:512]),
                 reads=[tk], writes=[f"SV{n}"])
            ui = n % 3
            k.op('pool', lambda e, ui=ui, ts=ts: e.tensor_copy(out=Ub[ui][:], in_=ts[:, 384:448]),
                 reads=[tk], writes=[f"Ub{ui}"])
            k.op('act', lambda e, gi=gi, ts=ts: e.activation(out=sgo[gi][:], in_=ts[:, 128:384], func=AF.Sigmoid),
                 reads=[tk], writes=[f"sgo{gi}"])
            k.op('dve', lambda e, gi=gi, ts=ts: e.tensor_tensor(out=gz[gi][:], in0=ts[:, 256:384], in1=sgo[gi][:, 128:256], op=ALU.mult),
                 reads=[tk, f"sgo{gi}"], writes=[f"gz{gi}"])
            k.op('pool', lambda e, gi=gi: e.tensor_tensor(out=gz[gi][:], in0=gz[gi][:], in1=mng[:], op=ALU.mult),
                 reads=[f"gz{gi}", 'mng'], writes=[f"gz{gi}"])
            if SUB < 1:
                continue
            k.op('act', lambda e, G=G: e.activation(out=G[:, 0:2], in_=mb[:, 400:402], func=AF.Identity),
                 reads=['bank6'], writes=[gk])
            k.op('act', lambda e, G=G: e.activation(out=G[:, 2:3], in_=G[:, 1:2], func=AF.Exp, scale=-1.0, bias=nmgb[:, 1:2]),
                 reads=[gk, 'nmgb'], writes=[gk])
            k.op('act', lambda e, G=G: e.activation(out=G[:, 3:4], in_=G[:, 2:3], func=AF.Ln, scale=1.0, bias=1.0),
                 reads=[gk], writes=[gk])
            k.op('pe', lambda e, G=G: e.matmul(mb[:, 404:405], lhsT=mtri[:], rhs=G[:, 3:4], start=True, stop=True),
                 reads=[gk, 'mtri'], writes=['bank6'])
            k.op('pe', lambda e, G=G: e.matmul(mb[:, 405:406], lhsT=onesf[:], rhs=G[:, 3:4], start=True, stop=True),
                 reads=[gk, 'onesf'], writes=['bank6'])
            k.op('act', lambda e, G=G: e.activation(out=G[:, 4:6], in_=mb[:, 404:406], func=AF.Exp, scale=-1.0),
                 reads=['bank6'], writes=[gk])
            k.op('dve', lambda e, G=G: e.tensor_tensor(out=G[:, 6:7], in0=mb[:, 404:405], in1=G[:, 0:1], op=ALU.add),
                 reads=['bank6', gk], writes=[gk])
            k.op('act', lambda e, G=G: e.activation(out=G[:, 7:8], in_=G[:, 6:7], func=AF.Exp, scale=1.0, bias=mgb[:, 0:1]),
                 reads=[gk, 'mgb'], writes=[gk])
            if SUB < 2:
                continue
            vi = n % 2
            k.op('dve', lambda e, vi=vi, G=G, ts=ts: e.tensor_scalar(out=V2[vi][:, 0:128], in0=ts[:, 0:128], scalar1=G[:, 7:8],
                                                              scalar2=None, op0=ALU.mult),
                 reads=[tk, gk], writes=[f"V2{vi}"])
            k.op('dve', lambda e, vi=vi, G=G: e.tensor_copy(out=V2[vi][:, 128:129], in_=G[:, 7:8]),
                 reads=[gk], writes=[f"V2{vi}"])
            if SUB < 3:
                continue
            k.op('pe', lambda e, tsl=tsl: e.matmul(mb[:, 128:258], lhsT=qT[:, tsl], rhs=Cnb[:, 0:130], start=True, stop=False),
                 reads=['qT', 'Cnb'], writes=['bank6'])
            k.op('pe', lambda e, vi=vi: e.matmul(mb[:, 128:258], lhsT=Sm[vi][:], rhs=V2[vi][:, 0:130], start=False, stop=True),
                 reads=[f"Sm{vi}", f"V2{vi}"], writes=['bank6'])
            k.op('pe', lambda e, vi=vi: e.matmul(mb[:, 260:390], lhsT=Ktm[vi][:], rhs=V2[vi][:, 0:130], start=True, stop=True),
                 reads=[f"Ktm{vi}", f"V2{vi}"], writes=['bank6'])
            k.op('dve', lambda e: e.tensor_tensor(out=Cn32[:, 0:130], in0=mb[:, 260:390], in1=Cn32[:, 0:130], op=ALU.add),
                 reads=['bank6', 'Cn32'], writes=['Cn32'])
            k.op('dve', lambda e, G=G: e.tensor_scalar(out=Cn32[:, 0:130], in0=Cn32[:, 0:130], scalar1=G[:, 5:6],
                                                       scalar2=None, op0=ALU.mult),
                 reads=['Cn32', gk], writes=['Cn32'])
            k.op('pool', lambda e: e.tensor_copy(out=Cnb[:, 0:130], in_=Cn32[:, 0:130]),
                 reads=['Cn32'], writes=['Cnb'])
            if SUB < 4:
                continue
            k.op('dve', lambda e, G=G: e.tensor_tensor(out=G[:, 8:9], in0=mb[:, 256:257], in1=G[:, 4:5], op=ALU.mult),
                 reads=['bank6', gk], writes=[gk])
            k.op('dve', lambda e, G=G: e.tensor_tensor(out=G[:, 8:9], in0=G[:, 8:9], in1=G[:, 8:9], op=ALU.mult),
                 reads=[gk], writes=[gk])
            k.op('dve', lambda e, G=G: e.tensor_scalar(out=G[:, 8:9], in0=G[:, 8:9], scalar1=1.0, scalar2=None, op0=ALU.max),
                 reads=[gk], writes=[gk])
            k.op('act', lambda e, G=G: e.activation(out=G[:, 14:15], in_=G[:, 8:9], func=AF.Ln),
                 reads=[gk], writes=[gk])
            k.op('act', lambda e, G=G: e.activation(out=G[:, 9:10], in_=G[:, 14:15], func=AF.Exp, scale=-0.5),
                 reads=[gk], writes=[gk])
            k.op('dve', lambda e, G=G: e.tensor_tensor(out=G[:, 10:11], in0=G[:, 9:10], in1=G[:, 4:5], op=ALU.mult),
                 reads=[gk], writes=[gk])
            k.op('dve', lambda e, G=G, gi=gi: e.scalar_tensor_tensor(
                out=t1[gi][:], in0=mb[:, 128:256], scalar=G[:, 10:11], in1=sgo[gi][:, 0:128], op0=ALU.mult, op1=ALU.mult),
                reads=['bank6', gk, f"sgo{gi}"], writes=[f"t1{gi}"])
            k.op('pool', lambda e, G=G: e.memset(G[:, 11:12], 0.0), reads=[], writes=[gk + 'a'])
            k.op('act', lambda e, G=G, gi=gi: e.activation(out=t1sq[:], in_=t1[gi][:], func=AF.Square, accum_out=G[:, 11:12]),
                 reads=[f"t1{gi}", gk + 'a'], writes=['t1sq', gk + 'a'])
            k.op('act', lambda e, G=G: e.activation(out=G[:, 12:13], in_=G[:, 11:12], func=AF.Ln, scale=1.0 / 128, bias=EPS),
                 reads=[gk + 'a'], writes=[gk + 'b'])
            k.op('act', lambda e, G=G: e.activation(out=G[:, 13:14], in_=G[:, 12:13], func=AF.Exp, scale=-0.5),
                 reads=[gk + 'b'], writes=[gk + 'c'])
            k.op('dve', lambda e, G=G, gi=gi: e.scalar_tensor_tensor(
                out=ymb[gi][:], in0=t1[gi][:], scalar=G[:, 13:14], in1=gz[gi][:], op0=ALU.mult, op1=ALU.mult),
                reads=[f"t1{gi}", gk + 'c', f"gz{gi}"], writes=[f"ymb{gi}"])
            def _flush(gi=gi, r=r, tsl=tsl):
                k.op('pe', lambda e: e.transpose(tp[:, 128:256], ymb[gi][:], ident[:]),
                     reads=[f"ymb{gi}", 'ident'], writes=['tpb'])
                k.op('act', lambda e: e.activation(out=ymT[r][:, tsl], in_=tp[:, 128:256], func=AF.Identity),
                     reads=['tpb'], writes=[f"ymT{r}"])
            pend['f'] = _flush
            if SUB < 5:
                continue
            bi = 0 if n == 0 else 1
            k.op('pe', lambda e, ui=ui, bi=bi, tsl=tsl, n=n: e.matmul(
                banks[OB][0:64, 0:128], lhsT=Ub[ui][:], rhs=bands[:, bi, :], start=True, stop=(n == 0)),
                reads=[f"Ub{ui}", 'bands'], writes=['bank4'])
            if n > 0:
                up = (n - 1) % 3
                k.op('pe', lambda e, up=up: e.matmul(
                    banks[OB][0:64, 0:128], lhsT=Ub[up][:], rhs=bands[:, 2, :], start=False, stop=True),
                    reads=[f"Ub{up}", 'bands'], writes=['bank4'])
            k.op('dve', lambda e, tsl=tsl: e.tensor_copy(out=pTs[:, tsl], in_=banks[OB][0:64, 0:128]),
                 reads=['bank4'], writes=['pTs'])
        if os.environ.get('A_SKIPPOST'):
            continue
        if pend['f'] is not None:
            pend['f']()
            pend['f'] = None
        k.dma('sp', yT_o('m', tl), ymT[r][:], reads=[f"ymT{r}"], writes=[f"oym{tl}"])
        cx.outkeys.append(f"oym{tl}")
        pb, pk = nextpb()
        k.op('pe', lambda e: e.matmul(pb[0:64, :], lhsT=poolw[:], rhs=pTs[:], start=True, stop=True),
             reads=['poolw', 'pTs'], writes=[pk])
        k.op('dve', lambda e, r=r: e.scalar_tensor_tensor(
            out=ypT[r][:], in0=pb[0:64, :], scalar=pscale[:, 0:1], in1=spz[:], op0=ALU.mult, op1=ALU.mult),
            reads=[pk, 'pscale', 'spz'], writes=[f"ypT{r}"])
        k.dma('sp', yT_o('p', tl), ypT[r][:], reads=[f"ypT{r}"], writes=[f"oyp{tl}"])
        cx.outkeys.append(f"oyp{tl}")
        if STAGE < 4:
            continue
        top = 4 * tl + 3
        qsl = slice(tl * 512, (tl + 1) * 512)
        sq_keys = [f"sqT{tl}"]
        U = top + 1

        def st_Z(u):
            kb = top - u
            ci = u % 2
            zi = ZB[ci]
            zb = banks[zi]
            ksl = slice(kb * 128, (kb + 1) * 128)
            kkey = f"skT{kb // 4}"
            diag = kb >= 4 * tl
            k.op('pe', lambda e: e.matmul(zb[:, :], lhsT=skT[:, ksl], rhs=sqT[:, qsl], start=True, stop=True),
                 reads=[kkey] + sq_keys, writes=[f"bank{zi}"])
            k.op('act', lambda e: e.activation(out=Eb[ci][:], in_=zb[:, :], func=AF.Exp),
                 reads=[f"bank{zi}"], writes=[f"Eb{ci}"])
            if diag:
                k.op('act', lambda e: e.activation(out=L32[:], in_=Eb[ci][:], func=AF.Ln, scale=1.0, bias=1.0),
                     reads=[f"Eb{ci}"], writes=['L32'])
                k.op('dve', lambda e: e.tensor_tensor(out=Lb[ci][:], in0=L32[:], in1=sbm[:, kb - 4 * tl, :], op=ALU.mult),
                     reads=['L32', 'sbm'], writes=[f"Lb{ci}"])
            else:
                k.op('act', lambda e: e.activation(out=Lb[ci][:], in_=Eb[ci][:], func=AF.Ln, scale=1.0, bias=1.0),
                     reads=[f"Eb{ci}"], writes=[f"Lb{ci}"])
        def st_S(u, tl=tl, top=top):
            kb = top - u
            ci = u % 2
            if kb > 0:
                if kb == top:
                    k.op('dve', lambda e: e.tensor_copy(out=S32[:], in_=Lb[ci][:]), reads=[f"Lb{ci}"], writes=['S32'])
                else:
                    k.op('dve', lambda e: e.tensor_tensor(out=S32[:], in0=S32[:], in1=Lb[ci][:], op=ALU.add),
                         reads=['S32', f"Lb{ci}"], writes=['S32'])
                k.op('dve', lambda e: e.tensor_copy(out=Sb[1 - ci][:], in_=S32[:]), reads=['S32'], writes=[f"Sb{1 - ci}"])

        def st_L(u):
            kb = top - u
            ci = u % 2
            li = LB[ci]
            lb = banks[li]
            ksl = slice(kb * 128, (kb + 1) * 128)
            kkey = f"skT{kb // 4}"
            diag = kb >= 4 * tl
            k.op('pe', lambda e: e.matmul(lb[:, :], lhsT=skT[:, ksl], rhs=sqT[:, qsl], start=True, stop=False),
                 reads=[kkey] + sq_keys, writes=[f"bank{li}"])
            k.op('pe', lambda e: e.matmul(lb[:, :], lhsT=nui[:], rhs=Lb[ci][:], start=False, stop=(kb == top)),
                 reads=['nui', f"Lb{ci}"], writes=[f"bank{li}"])
            if kb != top:
                k.op('pe', lambda e: e.matmul(lb[:, :], lhsT=nones[:], rhs=Sb[ci][:], start=False, stop=True),
                     reads=['nones', f"Sb{ci}"], writes=[f"bank{li}"])
            k.op('act', lambda e: e.activation(out=At[ci][:], in_=lb[:, :], func=AF.Exp),
                 reads=[f"bank{li}"], writes=[f"At{ci}"])
            if diag:
                k.op('dve', lambda e: e.tensor_tensor(out=Am[ci][:], in0=At[ci][:], in1=sbm[:, kb - 4 * tl, :], op=ALU.mult),
                     reads=[f"At{ci}", 'sbm'], writes=[f"Am{ci}"])

        def st_V(u):
            kb = top - u
            ci = u % 2
            diag = kb >= 4 * tl
            asrc, akey = (Am[ci], f"Am{ci}") if diag else (At[ci], f"At{ci}")
            k.op('pe', lambda e: e.matmul(banks[OB][0:64, :], lhsT=SV[:, kb, :], rhs=asrc[:],
                                          start=(kb == top), stop=(kb == 0)),
                 reads=[f"SV{kb}", akey], writes=['bank4'])

        for step in range(U + 2):
            if step < U:
                st_Z(step)
            if 1 <= step <= U:
                st_L(step - 1)
            if step < U:
                st_S(step)
            if step >= 2:
                st_V(step - 2)
        k.op('dve', lambda e, r=r: e.tensor_tensor(out=ysT[r][:], in0=banks[OB][0:64, :], in1=ssz[:], op=ALU.mult),
             reads=['bank4', 'ssz'], writes=[f"ysT{r}"])
        k.dma('sp', yT_o('s', tl), ysT[r][:], reads=[f"ysT{r}"], writes=[f"oys{tl}"])
        cx.outkeys.append(f"oys{tl}")


def _consts():
    s = np.arange(128)
    mtri = (s[:, None] <= s[None, :]).astype(np.float32)
    nui = -(s[:, None] >= s[None, :]).astype(np.float32)
    t = np.arange(512)
    sbm = np.stack([((i * 128 + s)[:, None] < t[None, :]).astype(np.float32) for i in range(4)])
    return mtri, nui, sbm


def _bands(w):
    s = np.arange(128)
    out = np.zeros((3, 128, 128), np.float32)
    for t in range(128):
        lo = max(t + 1 - w, 0)
        out[0, lo:t + 1, t] += 1.0 / (t + 1 - lo)
        out[0, t, t] -= 1.0
        lo = max(t + 1 - w, 0)
        out[1, lo:t + 1, t] += 1.0 / w
        out[1, t, t] -= 1.0
        nprev = w - (t + 1)
        if nprev > 0:
            out[2, 128 - nprev:, t] += 1.0 / w
    return out


def hostA_inputs(inp, l, x_full):
    mtri, nui, sbm = _consts()
    ident = np.eye(128, dtype=np.float32)
    w_in = inp["w_in"][l]
    maps = []
    for core in range(8):
        b, hh = core // 4, core % 4
        c128 = slice(hh * 128, (hh + 1) * 128)
        c64 = slice(hh * 64, (hh + 1) * 64)
        off = dict(mq=0, mk=512, mv=1024, mi=1536, mf=1540, mo=1544, mz=2056, pu=2568, pz=2824,
                   sq=3080, sk=3336, sv=3592, sz=3848)
        col = lambda nm, sl: w_in[:, off[nm] + sl.start: off[nm] + sl.stop]
        wtm = np.concatenate([col('mv', c128), col('mo', c128), col('mz', c128), col('pu', c64), col('sv', c64)], 1)
        wgt = np.zeros((D, 16), np.float32)
        wgt[:, 0] = w_in[:, off['mi'] + hh]
        wgt[:, 1] = w_in[:, off['mf'] + hh]
        wfm = np.concatenate([col('mq', c128), col('mk', c128), col('pz', c64), col('sz', c64),
                              col('sq', c64), col('sk', c64)], 1)
        mg = inp["m_gate_b"][l]
        mgb = np.tile(np.array([[mg[hh], mg[4 + hh]]], np.float32), (128, 1))
        cwl = inp["conv_w"][l]
        cw = np.concatenate([cwl[:, c128].T, cwl[:, 512 + hh * 128: 512 + (hh + 1) * 128].T], 1)
        cbl = inp["conv_b"][l]
        cb = np.stack([cbl[c128], cbl[512 + hh * 128: 512 + (hh + 1) * 128]], 1)
        maps.append(dict(
            cT=np.ascontiguousarray(inp["c"][b].reshape(8, 128).T),
            ng=np.ascontiguousarray(inp["norm_g"][l].reshape(8, 128).T),
            w_ada=inp["w_ada"][l], b_ada=inp["b_ada"][l].reshape(1, -1),
            wtm=np.ascontiguousarray(wtm), wgt=np.ascontiguousarray(wgt), wfm=np.ascontiguousarray(wfm),
            mgb=mgb, cw=np.ascontiguousarray(cw), cb=np.ascontiguousarray(cb),
            mng=np.ascontiguousarray(inp["m_norm_g"][l][c128].reshape(1, 128)),
            poolw=np.ascontiguousarray(inp["pool_w"][l][hh]),
            pscale=np.ascontiguousarray(inp["pool_scale"][l][c64].reshape(64, 1)),
            bands=_bands(POOL_WINDOWS[hh]), mtri=mtri, sbm=sbm, nui=nui, ident=ident))
    return maps


def hostA_gather(results):
    out = np.zeros((NB, D, SEQ), dtype=ml_dtypes.bfloat16)
    for core in range(8):
        b, hh = core // 4, core % 4
        y = results[core]["yTo"]
        out[b, hh * 128:(hh + 1) * 128] = y[0:128]
        out[b, 512 + hh * 64:512 + (hh + 1) * 64] = y[128:192]
        out[b, 768 + hh * 64:768 + (hh + 1) * 64] = y[192:256]
    return out


def hostB_inputs(inp, l, x_full, yT_full):
    maps = []
    ident = np.eye(128, dtype=np.float32)
    wbr = np.concatenate([inp["w_br_m"][l], inp["w_br_p"][l], inp["w_br_s"][l]], 0)
    for core in range(8):
        b, j = core // 4, core % 4
        sl = slice(j * NTB, (j + 1) * NTB)
        maps.append(dict(
            cT=np.ascontiguousarray(inp["c"][b].reshape(8, 128).T),
            ng=np.ascontiguousarray(inp["norm_g"][l].reshape(8, 128).T),
            w_ada=inp["w_ada"][l], b_ada=inp["b_ada"][l].reshape(1, -1),
            wg=np.ascontiguousarray(inp["w_in"][l][:, 4104:]),
            gb=np.ascontiguousarray(inp["gate_b"][l].reshape(24, 128).T),
            wbr=wbr, wout=inp["w_out"][l], fg=inp["final_g"].reshape(1, -1), ident=ident))
    return maps


RG4 = [[0, 1, 2, 3], [4, 5, 6, 7]]
A_NAMES = dict(cT=[128, 8], ng=[128, 8], w_ada=[D, 3 * D], b_ada=[1, 3 * D], wtm=[D, 512], wgt=[D, 16],
               wfm=[D, 512], mgb=[128, 2], cw=[128, 8], cb=[128, 2], mng=[1, 128], poolw=[64, 64],
               pscale=[64, 1], bands=[3, 128, 128])
B_NAMES = dict(wg=[D, 3 * D], gb=[128, 24], wbr=[D, D], wout=[D, D])
C_NAMES = dict(mtri=[128, 128], sbm=[4, 128, 512], nui=[128, 128], ident=[128, 128], fg=[1, D])


def build_fused(ntile=SEQ // 512):
    nc = bass.Bass("TRN2", target_bir_lowering=False)
    dt_in = lambda name, shape, dt=F32: nc.dram_tensor(name, list(shape), dt, kind="ExternalInput").ap()
    x_in = dt_in("x", [SEQ, D])
    cst = {n: dt_in(n, sh) for n, sh in C_NAMES.items()}
    lay = []
    for l in range(DEPTH):
        d = {n: dt_in(f"{n}_{l}", sh) for n, sh in A_NAMES.items()}
        d.update({n: dt_in(f"{n}_{l}", sh) for n, sh in B_NAMES.items()})
        lay.append(d)
    out_d = nc.dram_tensor("out", [NTB, D], F32, kind="ExternalOutput").ap()
    internal = lambda name, shape, dt: nc.dram_tensor(name, list(shape), dt).ap()
    ys = internal("ys", [4 * 256, NTB], BF16)
    yr = internal("yr", [4 * 1024, NTB], BF16)
    yown = internal("yown", [D, NTB], BF16)
    xown = internal("xown", [NTB, D], F32)
    xs1 = internal("xs1", [NTB, D], F32)
    xg1 = internal("xg1", [8 * 1024, D], F32)
    ROW = dict(m=(0, 128), p=(128, 64), s=(192, 64))

    def ys_out(nm, tl):
        r0, nr = ROW[nm]
        q = tl // 4
        return ys[q * 256 + r0:q * 256 + r0 + nr, (tl % 4) * 512:(tl % 4 + 1) * 512]

    with contextlib.ExitStack() as outer:
        k = K(nc, outer)
        PID = nc.partition_id()

        def quarter():
            return PID % 4

        for l in range(DEPTH):
            last = (l == DEPTH - 1)
            with contextlib.ExitStack() as st:
                cx = Ctx(nc, st, k, prefix=f"A{l}_")
                a = dict(lay[l])
                a.update(cst)
                if l == 0:
                    a['x'] = lambda t0: x_in[t0:t0 + 128, :]
                else:
                    def xg_tile(t0):
                        rank, c, r0 = t0 // NTB, (t0 % NTB) // 256, t0 % 256
                        return xg1[c * 1024 + rank * 256 + r0:c * 1024 + rank * 256 + r0 + 128, :]
                    a['x'] = xg_tile
                    a['xreads'] = lambda t0: [f'xg1_{(t0 % NTB) // 256}']
                emit_A(cx, a, ys_out, ntile)
                akeys = list(cx.outkeys)
                k.barrier()
                k.emit()
            def pool_hook(q, akeys=akeys):
                k.coll("AllGather", RG4, ys[q * 256:(q + 1) * 256, :], yr[q * 1024:(q + 1) * 1024, :],
                       reads=akeys, writes=[f'yrecv{q}'])

            def after_prologue():
                for i in range(2):
                    k.dma('sp', yown[i * 512:(i + 1) * 512, :],
                          (lambda i=i: yr.rearrange("(q r) t -> q r t", q=4)[
                              bass.ds(quarter(), 1), i * 512:(i + 1) * 512, :].squeeze(0)),
                          reads=[f'yrecv{q}' for q in range(4)], writes=[f'yown_{i}'])
            ykeys = ['yown_0', 'yown_1']
            if l == 0:
                for i in range(4):
                    k.dma('sp', xown[i * 512:(i + 1) * 512, :],
                          (lambda i=i: x_in.rearrange("(q r) d -> q r d", q=4)[
                              bass.ds(quarter(), 1), i * 512:(i + 1) * 512, :].squeeze(0)), writes=[f'xown{i}'])
            with contextlib.ExitStack() as st:
                cx = Ctx(nc, st, k, prefix=f"B{l}_")

                def yT_d(tl):
                    cs = slice(tl * 512, (tl + 1) * 512)
                    res = []
                    for kk in range(4):
                        res.append((kk, 0, 128, yown[kk * 256:kk * 256 + 128, cs]))
                    for j, r0 in ((0, 128), (1, 192)):
                        for half in range(2):
                            kk = 4 + 2 * j + half
                            for hh2 in range(2):
                                rank = 2 * half + hh2
                                res.append((kk, hh2 * 64, 64, yown[rank * 256 + r0:rank * 256 + r0 + 64, cs]))
                    return res

                if l == 0:
                    x_d = lambda t0: xown[t0:t0 + 128, :]
                    x_reads = [f'xown{i}' for i in range(4)]
                else:
                    x_d = lambda t0: xs1[t0:t0 + 128, :]
                    x_reads = ['xs1']
                d = lay[l]
                def after_block(t0):
                    if (t0 + 128) % 256 == 0:
                        c = t0 // 256
                        k.coll("AllGather", RG4, xs1[c * 256:(c + 1) * 256, :], xg1[c * 1024:(c + 1) * 1024, :],
                               reads=[f"xo{t0 - 128}", f"xo{t0}"], writes=[f'xg1_{c}'])
                emit_B(cx, last, x_d, yT_d, d['cT'], d['ng'], d['w_ada'], d['b_ada'], d['wg'], d['gb'], d['wbr'],
                       d['wout'], cst['fg'], cst['ident'], out_d if last else xs1, x_reads=x_reads, y_reads=ykeys,
                       after_block=None if last else after_block, after_prologue=after_prologue, pool_hook=pool_hook)
                bkeys = list(cx.outkeys)
                if last:
                    k.wait_all('sp', bkeys)
                else:
                    k.barrier()
                k.emit()
            if not last:
                k.last_w['xs1'] = k.last_w[bkeys[-1]]
    return nc


def host_inputs(inp):
    mtri, nui, sbm = _consts()
    ident = np.eye(128, dtype=np.float32)
    maps = [dict(mtri=mtri, nui=nui, sbm=sbm, ident=ident, fg=inp["final_g"].reshape(1, -1).astype(np.float32))
            for _ in range(8)]
    for core in range(8):
        maps[core]["x"] = np.ascontiguousarray(inp["x"][core // 4])
    dummy_x = None
    for l in range(DEPTH):
        ma = hostA_inputs(inp, l, None)
        mb = hostB_inputs(inp, l, None, None)
        for core in range(8):
            for n in A_NAMES:
                maps[core][f"{n}_{l}"] = ma[core][n]
            for n in B_NAMES:
                maps[core][f"{n}_{l}"] = mb[core][n]
    return maps


_NC = {}


def kernel(**inputs):
    inp = {k: np.asarray(v) for k, v in inputs.items()}
    if 'nc' not in _NC:
        _NC['nc'] = build_fused()
    res = run_bass_kernel_spmd(_NC['nc'], host_inputs(inp), core_ids=list(range(8)))
    out = np.stack([np.concatenate([res.results[b * 4 + j]["out"] for j in range(4)], 0) for b in range(NB)])
    return out.astype(np.float32)
```

```python
import contextlib
import os
import numpy as np
import ml_dtypes
import concourse.bass as bass
import concourse.mybir as mybir
from concourse.bass_utils import run_bass_kernel_spmd

F32 = mybir.dt.float32
BF16 = mybir.dt.bfloat16
AF = mybir.ActivationFunctionType
ALU = mybir.AluOpType
AX = mybir.AxisListType

D = 1024
SEQ = 8192
NB = 2
DEPTH = 2
EPS = 1e-6
EPOCH = 12000
POOL_WINDOWS = (2, 4, 8, 16)


class _Rec:
    def __init__(self):
        self.calls = []

    def __getattr__(self, name):
        def f(*a, **kw):
            self.calls.append((name, a, kw))
        return f


class K:
    def __init__(self, nc, stack, n_dma_sems=12):
        self.nc = nc
        self.stack = stack
        self.engs = ['pe', 'act', 'dve', 'pool', 'sp']
        self.q = {e: [] for e in self.engs}
        self.cnt = {e: 0 for e in self.engs}
        self.epoch = {e: 0 for e in self.engs}
        self.sems = {}
        self.known = {e: {} for e in self.engs}
        self.last_w = {}
        self.readers = {}
        self.dma_sems = {q: [stack.enter_context(nc.semaphore(f"dma_{q}{i}")) for i in range(n)]
                         for q, n in (('sp', 10), ('pool', 6), ('act', 2))}
        self.dma_cnt = {q: [0] * len(v) for q, v in self.dma_sems.items()}
        self.dma_rr = {q: 0 for q in self.dma_sems}

    def _sem(self, e):
        key = (e, self.epoch[e])
        if key not in self.sems:
            self.sems[key] = self.stack.enter_context(self.nc.semaphore(f"s_{e}_{self.epoch[e]}"))
        return self.sems[key]

    def _need(self, e, tok):
        if tok is None:
            return
        sem, val, src = tok[:3]
        if src == e and e == 'pe':
            return
        kn = self.known[e]
        if kn.get(id(sem), 0) >= val:
            return
        kn[id(sem)] = val
        self.q[e].append(('wait', sem, val))

    def _deps(self, e, reads, writes):
        for k in reads:
            self._need(e, self.last_w.get(k))
            if k.startswith('bank') or k == 'tpb':
                for r in self.readers.get(k, ()):
                    if r[2] != e:
                        self._need(e, r)
        for k in writes:
            self._need(e, self.last_w.get(k))
            for r in self.readers.get(k, ()):
                if r[2] == e:
                    continue
                self._need(e, r)

    def _commit(self, tok, reads, writes):
        for k in writes:
            self.last_w[k] = tok
            self.readers[k] = []
        for k in reads:
            self.readers.setdefault(k, []).append(tok)

    def op(self, e, fn, reads=(), writes=()):
        self._deps(e, reads, writes)
        if self.cnt[e] >= EPOCH:
            self.epoch[e] += 1
            self.cnt[e] = 0
        sem = self._sem(e)
        self.cnt[e] += 1
        tok = (sem, self.cnt[e], e)
        rec = _Rec()
        fn(rec)
        assert len(rec.calls) == 1
        name, a, kw = rec.calls[0]
        self.q[e].append(('op', (lambda eng, name=name, a=a, kw=kw: getattr(eng, name)(*a, **kw)), sem, 1))
        self._commit(tok, reads, writes)
        return tok

    def dma(self, e, out, in_, reads=(), writes=(), **kw):
        self._deps(e, reads, writes)
        i = self.dma_rr[e]
        self.dma_rr[e] = (i + 1) % len(self.dma_sems[e])
        sem = self.dma_sems[e][i]
        cnts = self.dma_cnt[e]
        if cnts[i] > 0:
            self._need(e, (sem, cnts[i] * 16, 'dma'))
        cnts[i] += 1
        tok = (sem, cnts[i] * 16, 'dma')
        self.q[e].append(('op', lambda eng: eng.dma_start(
            out=(out() if callable(out) else out), in_=(in_() if callable(in_) else in_), **kw), sem, 16))
        self._commit(tok, reads, writes)
        return tok

    def coll(self, kind, groups, src, dst, reads=(), writes=()):
        e = 'pool'
        self._deps(e, reads, writes)
        if not hasattr(self, 'cc_sem'):
            self.cc_sem = self.stack.enter_context(self.nc.semaphore("cc_sem"))
            self.cc_cnt = 0
        if self.cc_cnt > 0:
            self._need(e, (self.cc_sem, self.cc_cnt, 'dma'))
        self.cc_cnt += 1
        tok = (self.cc_sem, self.cc_cnt, 'dma')
        self.q[e].append(('op', lambda eng: eng.collective_compute(
            kind, ALU.bypass, groups, ins=[src.opt()], outs=[dst.opt()]), self.cc_sem, 1))
        self._commit(tok, reads, writes)
        return tok

    def wait_all(self, e, keys):
        for k in keys:
            self._need(e, self.last_w.get(k))

    def barrier(self):
        toks = []
        for f in self.engs:
            if self.cnt[f] > 0:
                toks.append((self._sem(f), self.cnt[f], f))
        for q, sems in self.dma_sems.items():
            for i, sem in enumerate(sems):
                if self.dma_cnt[q][i] > 0:
                    toks.append((sem, self.dma_cnt[q][i] * 16, 'dma'))
        if getattr(self, 'cc_cnt', 0) > 0:
            toks.append((self.cc_sem, self.cc_cnt, 'dma'))
        for e in self.engs:
            for t in toks:
                if t[2] != e:
                    self._need(e, t)

    def emit(self):
        nc = self.nc
        with nc.Block() as block:
            def replay(name, eng):
                for it in self.q[name]:
                    if it[0] == 'wait':
                        eng.wait_ge(it[1], it[2])
                    else:
                        it[1](eng).then_inc(it[2], it[3])

            @block.sync
            def _(eng):
                replay('sp', eng)

            @block.scalar
            def _(eng):
                replay('act', eng)

            @block.vector
            def _(eng):
                replay('dve', eng)

            @block.gpsimd
            def _(eng):
                replay('pool', eng)

            @block.tensor
            def _(eng):
                replay('pe', eng)
        for e in self.engs:
            self.q[e] = []


class Ctx:
    def __init__(self, nc, st, k=None, prefix=""):
        self.nc = nc
        self.st = st
        self.k = k if k is not None else K(nc, st)
        self.n = 0
        self.outkeys = []
        self.prefix = prefix

    def sb(self, name, shape, dt):
        return self.st.enter_context(self.nc.sbuf_tensor("s_" + self.prefix + name, list(shape), dt))

    def ps(self, name, shape, dt):
        return self.st.enter_context(self.nc.psum_tensor("p_" + self.prefix + name, list(shape), dt))


def emit_mod(cx, w_ada, b_ada, cT_d, ng_d, banks, need_gate):
    k = cx.k
    ncol = 3 if need_gate else 2
    cT = cx.sb("cT", [128, 8], F32)
    ng = cx.sb("ng", [128, 8], F32)
    modrow = cx.sb("modrow", [1, 3072], F32)
    one11 = cx.sb("one11", [1, 128], F32)
    s1 = cx.sb("s1", [128, 8], F32)
    s2 = cx.sb("s2", [128, 8], F32)
    gate_bc = cx.sb("gate_bc", [128, 1024], F32) if need_gate else None
    NWA = 4
    wa = [cx.sb(f"wa{i}", [128, 512], F32) for i in range(NWA)]
    k.dma('sp', cT[:], cT_d, writes=['cT'])
    k.dma('sp', ng[:], ng_d, writes=['ng'])
    k.op('dve', lambda e: e.memset(one11[:], 1.0), writes=['one11'])
    ngrp = ncol * 2
    i = 0
    for kc in range(8):
        for cg in range(ngrp):
            buf = wa[i % NWA]
            bk = f"wa{i % NWA}"
            i += 1
            k.dma('sp', buf[:], w_ada[kc * 128:(kc + 1) * 128, cg * 512:(cg + 1) * 512], writes=[bk])
            k.op('pe', lambda e, buf=buf, cg=cg, kc=kc: e.matmul(
                banks[cg][0:1, :], lhsT=cT[:, kc:kc + 1], rhs=buf[:], start=(kc == 0), stop=(kc == 7)),
                reads=[bk, 'cT'], writes=[f"bank{cg}"])
    for cg in range(ngrp):
        buf = wa[i % NWA]
        bk = f"wa{i % NWA}"
        i += 1
        k.dma('sp', buf[0:1, :], b_ada[0:1, cg * 512:(cg + 1) * 512], writes=[bk])
        k.op('dve', lambda e, cg=cg, buf=buf: e.tensor_tensor(
            out=modrow[0:1, cg * 512:(cg + 1) * 512], in0=banks[cg][0:1, :],
            in1=buf[0:1, :], op=ALU.add),
            reads=[f"bank{cg}", bk], writes=['modrow'])
    colb = banks[6]
    for cc in range(8):
        k.op('pe', lambda e, cc=cc: e.matmul(
            colb[:, cc:cc + 1], lhsT=modrow[0:1, 1024 + cc * 128:1024 + (cc + 1) * 128],
            rhs=one11[0:1, 0:1], start=True, stop=True), reads=['modrow', 'one11'], writes=['bank6'])
        k.op('pe', lambda e, cc=cc: e.matmul(
            colb[:, 8 + cc:9 + cc], lhsT=modrow[0:1, cc * 128:(cc + 1) * 128],
            rhs=one11[0:1, 0:1], start=True, stop=True), reads=['modrow', 'one11'], writes=['bank6'])
    k.op('dve', lambda e: e.scalar_tensor_tensor(
        out=s1[:], in0=colb[:, 0:8], scalar=1.0, in1=ng[:], op0=ALU.add, op1=ALU.mult),
        reads=['bank6', 'ng'], writes=['s1'])
    k.op('dve', lambda e: e.tensor_copy(out=s2[:], in_=colb[:, 8:16]), reads=['bank6'], writes=['s2'])
    if need_gate:
        for hh in range(2):
            k.op('pe', lambda e, hh=hh: e.matmul(
                banks[hh][:, :], lhsT=one11[0:1, 0:128], rhs=modrow[0:1, 2048 + hh * 512:2048 + (hh + 1) * 512],
                start=True, stop=True), reads=['modrow', 'one11'], writes=[f"bank{hh}"])
            k.op('dve', lambda e, hh=hh: e.tensor_copy(out=gate_bc[:, hh * 512:(hh + 1) * 512], in_=banks[hh][:, :]),
                 reads=[f"bank{hh}"], writes=['gate_bc'])
    return s1, s2, gate_bc


def _nrm_bufs(cx, nslot):
    if not hasattr(cx, 'nrm'):
        cx.nrm = dict(
            sq=cx.sb("nsq", [128, 1024], BF16),
            st=[cx.sb(f"nst{j}", [128, 4], F32) for j in range(nslot)],
            xn=[cx.sb(f"nxn{j}", [128, 1024], BF16) for j in range(nslot)],
            nf=cx.sb("nrm_nf", [128, 8, 128], F32),
        )
    return cx.nrm


def emit_norm_part1(cx, xt, xkey, slot, nslot=2):
    k = cx.k
    nb = _nrm_bufs(cx, nslot)
    sq, stt, xn = nb['sq'], nb['st'][slot], nb['xn'][slot]
    ksq, kst, kxn = "nsq", f"nst{slot}", f"nxn{slot}"
    k.op('dve', lambda e: e.memset(stt[:], 0.0), writes=[kst])
    k.op('act', lambda e: e.activation(out=sq[:], in_=xt, func=AF.Square, accum_out=stt[:, 0:1]),
         reads=[xkey, kst], writes=[ksq, kst])
    k.op('act', lambda e: e.activation(out=stt[:, 1:2], in_=stt[:, 0:1], func=AF.Ln, scale=1.0 / D, bias=EPS),
         reads=[kst], writes=[kst])
    k.op('act', lambda e: e.activation(out=stt[:, 2:3], in_=stt[:, 1:2], func=AF.Exp, scale=-0.5),
         reads=[kst], writes=[kst])
    k.op('dve', lambda e: e.tensor_scalar(out=xn[:], in0=xt, scalar1=stt[:, 2:3], scalar2=None, op0=ALU.mult),
         reads=[xkey, kst], writes=[kxn])


def emit_norm_part2(cx, slot, tb, s1, s2, ident, tp, tpkey, hT, hkey):
    k = cx.k
    nb = cx.nrm
    xn = nb['xn'][slot]
    kxn = f"nxn{slot}"
    for kc in range(8):
        k.op('pe', lambda e, kc=kc: e.transpose(tp[:, kc * 128:(kc + 1) * 128], xn[:, kc * 128:(kc + 1) * 128], ident[:]),
             reads=[kxn, 'ident'], writes=[tpkey])
    hv = hT[:, :, tb * 128:(tb + 1) * 128]
    tpv = tp[:, :].rearrange("p (k t) -> p k t", k=8)
    nf = nb['nf']
    k.op('dve', lambda e: e.tensor_tensor(out=nf[:, :, :], in0=tpv, in1=s1[:, :].unsqueeze(2).broadcast_to([128, 8, 128]),
                                          op=ALU.mult), reads=[tpkey, 's1'], writes=['nrm_nf'])
    k.op('dve', lambda e: e.tensor_tensor(out=hv, in0=nf[:, :, :], in1=s2[:, :].unsqueeze(2).broadcast_to([128, 8, 128]),
                                          op=ALU.add), reads=['nrm_nf', 's2'], writes=[hkey])


def emit_norm_tile(cx, xt, xkey, tb, s1, s2, ident, tp, tpkey, hT, hkey, tagn):
    i = cx.n
    cx.n += 1
    emit_norm_part1(cx, xt, xkey, i % 2)
    emit_norm_part2(cx, i % 2, tb, s1, s2, ident, tp, tpkey, hT, hkey)


NTB = 2048


def emit_B(cx, last, x_d, yT_d, cT_d, ng_d, wada_d, bada_d, wg_d, gb_d, wbr_d, wout_d, fg_d, id_d, xo_d, x_reads=(), y_reads=(), after_block=None, after_prologue=None, pool_hook=None):
    k = cx.k
    nc = cx.nc
    banks = [cx.ps(f"bank{i}", [128, 512], F32) for i in range(7)]
    tp = cx.ps("tpb", [128, 1024], BF16)
    ident = cx.sb("ident", [128, 128], BF16)
    k.dma('pool', ident[:], id_d, writes=['ident'])
    s1, s2, gate_bc = emit_mod(cx, wada_d, bada_d, cT_d, ng_d, banks, True)
    wg = cx.sb("wg", [128, 8, 3 * D], BF16)
    wbr = cx.sb("wbr", [128, 8, D], BF16)
    wout = cx.sb("wout", [128, 8, D], BF16)
    gb = cx.sb("gb", [128, 24], F32)
    k.dma('sp', gb[:], gb_d, writes=['gb'])
    wlist = [(wg, wg_d, 'wg', kc) for kc in range(8)] + [(wbr, wbr_d, 'wbr', kc) for kc in range(8)] + \
            [(wout, wout_d, 'wout', kc) for kc in range(8)]
    for i, (wt_, wd_, nm_, kc) in enumerate(wlist):
        if pool_hook is not None and i % 6 == 0:
            pool_hook(i // 6)
        k.dma('pool', wt_[:, kc, :], wd_[kc * 128:(kc + 1) * 128, :], writes=[f'{nm_}{kc}'])
    if last:
        fg_bc = cx.sb("fg_bc", [128, D], F32)
        k.dma('sp', fg_bc[:], fg_d.partition_broadcast(128), writes=['fg_bc'])
    if after_prologue is not None:
        after_prologue()
    xres = cx.sb("xres", [128, 4, D], F32)
    hT = [cx.sb(f"hT{i}", [128, 8, 512], BF16) for i in range(2)]
    yT = [cx.sb(f"yT{i}", [128, 8, 512], BF16) for i in range(2)]
    mT = cx.sb("mT", [128, 8, 512], BF16)
    sig = [cx.sb(f"sig{i}", [128, 512], F32) for i in range(3)]
    tmp = [cx.sb(f"tmp{i}", [128, 512], F32) for i in range(3)]
    xn_o = [cx.sb(f"xno{i}", [128, D], F32) for i in range(2)]
    fst = [cx.sb(f"fst{i}", [128, 4], F32) for i in range(2)]
    yo = [cx.sb(f"yo{i}", [128, D], F32) for i in range(2)]
    fsq = cx.sb("fsq", [128, D], BF16)
    nG = 0
    nP = 0
    nO = 0
    ntile = NTB // 512
    for tl in range(ntile):
        r = tl % 2
        for (kk, p0, pn, src) in yT_d(tl):
            k.dma('sp', yT[r][p0:p0 + pn, kk, :], src, reads=y_reads, writes=[f"yT{r}_{kk}_{p0}"])
        for tb in range(4):
            t0 = tl * 512 + tb * 128
            k.dma('sp', xres[:, tb, :], x_d(t0), reads=x_reads, writes=[f"xres{tb}"])
            emit_norm_tile(cx, xres[:, tb, :], f"xres{tb}", tb, s1, s2, ident, tp, 'tpb', hT[r], f"hT{r}", 'b')
        for dc in range(8):
            for gi in range(3):
                gbk = 0 + (nG % 2)
                nG += 1
                for kc in range(8):
                    k.op('pe', lambda e, gbk=gbk, gi=gi, kc=kc, dc=dc, r=r: e.matmul(
                        banks[gbk][:, :], lhsT=wg[:, kc, gi * D + dc * 128:gi * D + (dc + 1) * 128],
                        rhs=hT[r][:, kc, :], start=(kc == 0), stop=(kc == 7)),
                        reads=[f'wg{kc}', f"hT{r}"], writes=[f"bank{gbk}"])
                k.op('act', lambda e, gbk=gbk, gi=gi, dc=dc: e.activation(
                    out=sig[gi][:], in_=banks[gbk][:, :], func=AF.Sigmoid,
                    bias=gb[:, gi * 8 + dc:gi * 8 + dc + 1], scale=1.0),
                    reads=[f"bank{gbk}", 'gb'], writes=[f"sig{gi}"])
            for bi, (k0, k1) in enumerate(((0, 4), (4, 6), (6, 8))):
                pbk = 2 + (nP % 2)
                nP += 1
                for kc in range(k0, k1):
                    k.op('pe', lambda e, pbk=pbk, kc=kc, dc=dc, r=r, k0=k0, k1=k1: e.matmul(
                        banks[pbk][:, :], lhsT=wbr[:, kc, dc * 128:(dc + 1) * 128],
                        rhs=yT[r][:, kc, :], start=(kc == k0), stop=(kc == k1 - 1)),
                        reads=[f'wbr{kc}'] + [f"yT{r}_{kc}_{p0}" for p0 in (0, 64)], writes=[f"bank{pbk}"])
                k.op('dve', lambda e, pbk=pbk, bi=bi: e.tensor_tensor(
                    out=tmp[bi][:], in0=banks[pbk][:, :], in1=sig[bi][:], op=ALU.mult),
                    reads=[f"bank{pbk}", f"sig{bi}"], writes=[f"tmp{bi}"])
            k.op('dve', lambda e: e.tensor_tensor(out=tmp[0][:], in0=tmp[0][:], in1=tmp[1][:], op=ALU.add),
                 reads=['tmp0', 'tmp1'], writes=['tmp0'])
            k.op('dve', lambda e, dc=dc: e.tensor_tensor(out=mT[:, dc, :], in0=tmp[0][:], in1=tmp[2][:], op=ALU.add),
                 reads=['tmp0', 'tmp2'], writes=['mT'])
        for tb in range(4):
            t0 = tl * 512 + tb * 128
            ro = nO % 2
            nO += 1
            for ch in range(2):
                obk = 4 + ch
                for kc in range(8):
                    k.op('pe', lambda e, obk=obk, kc=kc, tb=tb, ch=ch: e.matmul(
                        banks[obk][:, :], lhsT=mT[:, kc, tb * 128:(tb + 1) * 128],
                        rhs=wout[:, kc, ch * 512:(ch + 1) * 512], start=(kc == 0), stop=(kc == 7)),
                        reads=['mT', f'wout{kc}'], writes=[f"bank{obk}"])
                k.op('dve', lambda e, obk=obk, ch=ch, ro=ro: e.tensor_tensor(
                    out=xn_o[ro][:, ch * 512:(ch + 1) * 512], in0=banks[obk][:, :],
                    in1=gate_bc[:, ch * 512:(ch + 1) * 512], op=ALU.mult),
                    reads=[f"bank{obk}", 'gate_bc'], writes=[f"xno{ro}"])
            k.op('dve', lambda e, ro=ro, tb=tb: e.tensor_tensor(
                out=xn_o[ro][:], in0=xn_o[ro][:], in1=xres[:, tb, :], op=ALU.add),
                reads=[f"xno{ro}", f"xres{tb}"], writes=[f"xno{ro}"])
            if not last:
                k.dma('sp', xo_d[t0:t0 + 128, :], xn_o[ro][:], reads=[f"xno{ro}"], writes=[f"xo{t0}"])
                cx.outkeys.append(f"xo{t0}")
                if after_block is not None:
                    after_block(t0)
            else:
                k.op('dve', lambda e, ro=ro: e.memset(fst[ro][:], 0.0), writes=[f"fst{ro}"])
                k.op('act', lambda e, ro=ro: e.activation(out=fsq[:], in_=xn_o[ro][:], func=AF.Square,
                                                           accum_out=fst[ro][:, 0:1]),
                     reads=[f"xno{ro}", f"fst{ro}"], writes=['fsq', f"fst{ro}"])
                k.op('act', lambda e, ro=ro: e.activation(out=fst[ro][:, 1:2], in_=fst[ro][:, 0:1], func=AF.Ln,
                                                           scale=1.0 / D, bias=EPS),
                     reads=[f"fst{ro}"], writes=[f"fst{ro}"])
                k.op('act', lambda e, ro=ro: e.activation(out=fst[ro][:, 2:3], in_=fst[ro][:, 1:2], func=AF.Exp, scale=-0.5),
                     reads=[f"fst{ro}"], writes=[f"fst{ro}"])
                k.op('dve', lambda e, ro=ro: e.scalar_tensor_tensor(
                    out=yo[ro][:], in0=xn_o[ro][:], scalar=fst[ro][:, 2:3], in1=fg_bc[:],
                    op0=ALU.mult, op1=ALU.mult), reads=[f"xno{ro}", f"fst{ro}", 'fg_bc'], writes=[f"yo{ro}"])
                k.dma('sp', xo_d[t0:t0 + 128, :], yo[ro][:], reads=[f"yo{ro}"], writes=[f"xo{t0}"])
                cx.outkeys.append(f"xo{t0}")


def emit_A(cx, a, yT_o, ntile):
    import os
    STAGE = int(os.environ.get('A_STAGE', '9'))
    SUB = int(os.environ.get('A_SUB', '9'))
    DIS = os.environ.get('A_DIS', '')
    k = cx.k
    banks = [cx.ps(f"bank{i}", [128, 512], F32) for i in range(7)]
    tp = cx.ps("tpb", [128, 1024], BF16)
    ZB = (0, 1)
    LB = (2, 3)
    OB = 4
    PB = 5
    MB = 6
    ident = cx.sb("ident", [128, 128], BF16)
    k.dma('pool', ident[:], a['ident'], writes=['ident'])
    s1, s2, _ = emit_mod(cx, a['w_ada'], a['b_ada'], a['cT'], a['ng'], banks, False)
    wtm = cx.sb("wtm", [128, 8, 512], BF16)
    wfm = cx.sb("wfm", [128, 8, 512], BF16)
    wgt = cx.sb("wgt", [128, 8, 16], BF16)
    for kc in range(8):
        k.dma('pool', wtm[:, kc, :], a['wtm'][kc * 128:(kc + 1) * 128, :], writes=[f'wtm{kc}'])
        k.dma('pool', wfm[:, kc, :], a['wfm'][kc * 128:(kc + 1) * 128, :], writes=[f'wfm{kc}'])
        k.dma('pool', wgt[:, kc, :], a['wgt'][kc * 128:(kc + 1) * 128, :], writes=[f'wgt{kc}'])
    mgb = cx.sb("mgb", [128, 2], F32)
    nmgb = cx.sb("nmgb", [128, 2], F32)
    cw = cx.sb("cw", [128, 8], F32)
    cb = cx.sb("cb", [128, 2], F32)
    mng = cx.sb("mng", [128, 128], F32)
    poolw = cx.sb("poolw", [64, 64], BF16)
    pscale = cx.sb("pscale", [64, 1], F32)
    bands = cx.sb("bands", [128, 3, 128], BF16)
    mtri = cx.sb("mtri", [128, 128], F32)
    onesf = cx.sb("onesf", [128, 128], F32)
    sbm = cx.sb("sbm", [128, 4, 512], BF16)
    nui = cx.sb("nui", [128, 128], BF16)
    nones = cx.sb("nones", [128, 128], BF16)
    k.dma('sp', mgb[:], a['mgb'], writes=['mgb'])
    k.dma('sp', cw[:], a['cw'], writes=['cw'])
    k.dma('sp', cb[:], a['cb'], writes=['cb'])
    k.dma('sp', mng[:], a['mng'].partition_broadcast(128), writes=['mng'])
    k.dma('pool', poolw[:], a['poolw'], writes=['poolw'])
    k.dma('sp', pscale[:], a['pscale'], writes=['pscale'])
    for i in range(3):
        k.dma('pool', bands[:, i, :], a['bands'][i], writes=['bands'])
    k.dma('sp', mtri[:], a['mtri'], writes=['mtri'])
    for i in range(4):
        k.dma('pool', sbm[:, i, :], a['sbm'][i], writes=['sbm'])
    k.dma('pool', nui[:], a['nui'], writes=['nui'])
    k.op('dve', lambda e: e.memset(onesf[:], 1.0), writes=['onesf'])
    k.op('dve', lambda e: e.memset(nones[:], -1.0), writes=['nones'])
    k.op('dve', lambda e: e.tensor_scalar(out=nmgb[:], in0=mgb[:], scalar1=-1.0, scalar2=None, op0=ALU.mult),
         reads=['mgb'], writes=['nmgb'])
    sqT = cx.sb("sqT", [64, SEQ], BF16)
    skT = cx.sb("skT", [64, SEQ], BF16)
    SV = cx.sb("SV", [128, SEQ // 128, 64], BF16)
    Cn32 = cx.sb("Cn32", [128, 132], F32)
    Cnb = cx.sb("Cnb", [128, 132], BF16)
    k.op('dve', lambda e: e.memset(Cn32[:], 0.0), writes=['Cn32'])
    k.op('dve', lambda e: e.memset(Cnb[:], 0.0), writes=['Cnb'])
    xt = [cx.sb(f"xt{i}", [128, D], F32) for i in range(2)]
    hT = [cx.sb(f"hT{i}", [128, 8, 512], BF16) for i in range(2)]
    qkr = [cx.sb(f"qkr{i}", [128, 516], F32) for i in range(2)]
    for g in range(2):
        k.op('pool', lambda e, g=g: e.memset(qkr[g][:], 0.0), writes=[f"qkr{g}"])
    cacc = [cx.sb(f"cacc{i}", [128, 512], F32) for i in range(2)]
    csg = [cx.sb(f"csg{i}", [128, 512], F32) for i in range(2)]
    qT = cx.sb("qT", [128, 512], BF16)
    kT = cx.sb("kT", [128, 512], BF16)
    spz = cx.sb("spz", [64, 512], F32)
    ssz = cx.sb("ssz", [64, 512], F32)
    sgz = cx.sb("sgz", [64, 512], F32)
    Ub = [cx.sb(f"Ub{i}", [128, 64], BF16) for i in range(3)]
    tmS = [cx.sb(f"tmS{i}", [128, 512], F32) for i in range(2)]
    gsb = [cx.sb(f"gsb{i}", [128, 16], F32) for i in range(2)]
    sgo = [cx.sb(f"sgo{i}", [128, 256], F32) for i in range(2)]
    gz = [cx.sb(f"gz{i}", [128, 128], F32) for i in range(2)]
    V2 = [cx.sb(f"V2{i}", [128, 132], BF16) for i in range(2)]
    Ktm = [cx.sb(f"Ktm{i}", [128, 128], BF16) for i in range(2)]
    for i in range(2):
        k.op('pool', lambda e, i=i: e.memset(V2[i][:], 0.0), writes=[f"V2{i}"])
    Sm = [cx.sb(f"Sm{i}", [128, 128], BF16) for i in range(2)]
    t1 = [cx.sb(f"t1{i}", [128, 128], F32) for i in range(2)]
    t1sq = cx.sb("t1sq", [128, 128], BF16)
    ymb = [cx.sb(f"ymb{i}", [128, 128], BF16) for i in range(2)]
    ymT = [cx.sb(f"ymT{i}", [128, 512], BF16) for i in range(2)]
    pTs = cx.sb("pTs", [64, 512], BF16)
    ypT = [cx.sb(f"ypT{i}", [64, 512], BF16) for i in range(2)]
    ysT = [cx.sb(f"ysT{i}", [64, 512], BF16) for i in range(2)]
    Eb = [cx.sb(f"Eb{i}", [128, 512], F32) for i in range(2)]
    L32 = cx.sb("L32", [128, 512], F32)
    Lb = [cx.sb(f"Lb{i}", [128, 512], BF16) for i in range(2)]
    S32 = cx.sb("S32", [128, 512], F32)
    Sb = [cx.sb(f"Sb{i}", [128, 512], BF16) for i in range(2)]
    At = [cx.sb(f"At{i}", [128, 512], BF16) for i in range(2)]
    Am = [cx.sb(f"Am{i}", [128, 512], BF16) for i in range(2)]
    mb = banks[MB]
    cnt = dict(z=0, l=0, u=0, g=0, p=0)
    pend = dict(f=None)
    PR = [PB, 0, 1, 2, 3]

    def nextpb():
        i = PR[cnt['p'] % len(PR)]
        cnt['p'] += 1
        return banks[i], f"bank{i}"
    KSCALE = 128.0 ** -0.5
    nxc = dict(n=0)

    def norm_prefetch(tl):
        for tb in range(4):
            t0 = tl * 512 + tb * 128
            xr = nxc['n'] % 2
            nxc['n'] += 1
            k.dma('sp', xt[xr][:], a['x'](t0), reads=(a['xreads'](t0) if 'xreads' in a else ()), writes=[f"xt{xr}"])
            emit_norm_part1(cx, xt[xr][:], f"xt{xr}", tb, nslot=4)

    for tl in range(ntile if STAGE >= 2 else 0):
        r = tl % 2
        if tl == 0:
            norm_prefetch(0)
        for tb in range(4):
            emit_norm_part2(cx, tb, tb, s1, s2, ident, tp, 'tpb', hT[r], f"hT{r}")
        hk = f"hT{r}"
        for g in range(2):
            pb, pk = nextpb()
            for kc in range(8):
                k.op('pe', lambda e, g=g, kc=kc, r=r: e.matmul(
                    pb[:, :], lhsT=wfm[:, kc, g * 128:(g + 1) * 128], rhs=hT[r][:, kc, :],
                    start=(kc == 0), stop=(kc == 7)), reads=[f'wfm{kc}', hk], writes=[pk])
            k.op('pool', lambda e, g=g: e.tensor_copy(out=qkr[g][:, 0:3], in_=qkr[g][:, 512:515]),
                 reads=[f"qkr{g}"], writes=[f"qkr{g}h"])
            k.op('act', lambda e, g=g: e.activation(out=qkr[g][:, 3:515], in_=pb[:, :], func=AF.Identity),
                 reads=[pk, f"qkr{g}h"], writes=[f"qkr{g}"])
            k.op('pool', lambda e, g=g: e.tensor_scalar(
                out=cacc[g][:], in0=qkr[g][:, 0:512], scalar1=cw[:, 4 * g:4 * g + 1], scalar2=cb[:, g:g + 1],
                op0=ALU.mult, op1=ALU.add), reads=[f"qkr{g}", f"qkr{g}h", 'cw', 'cb'], writes=[f"cacc{g}"])
            for j in range(1, 4):
                k.op('dve', lambda e, g=g, j=j: e.scalar_tensor_tensor(
                    out=cacc[g][:], in0=qkr[g][:, j:j + 512], scalar=cw[:, 4 * g + j:4 * g + j + 1],
                    in1=cacc[g][:], op0=ALU.mult, op1=ALU.add),
                    reads=[f"qkr{g}", f"qkr{g}h", f"cacc{g}", 'cw'], writes=[f"cacc{g}"])
            k.op('act', lambda e, g=g: e.activation(out=csg[g][:], in_=cacc[g][:], func=AF.Sigmoid),
                 reads=[f"cacc{g}"], writes=[f"csg{g}"])
            dst, dk, scl = (qT, 'qT', 1.0) if g == 0 else (kT, 'kT', KSCALE)
            k.op('dve', lambda e, g=g, dst=dst, scl=scl: e.scalar_tensor_tensor(
                out=dst[:], in0=cacc[g][:], scalar=scl, in1=csg[g][:], op0=ALU.mult, op1=ALU.mult),
                reads=[f"cacc{g}", f"csg{g}"], writes=[dk])
        for i4, nm in enumerate(('pz', 'sz', 'sq', 'sk')):
            c0 = 256 + i4 * 64
            pb, pk = nextpb()
            for kc in range(8):
                k.op('pe', lambda e, kc=kc, r=r, c0=c0: e.matmul(
                    pb[0:64, :], lhsT=wfm[:, kc, c0:c0 + 64], rhs=hT[r][:, kc, :],
                    start=(kc == 0), stop=(kc == 7)), reads=[f'wfm{kc}', hk], writes=[pk])
            if nm in ('pz', 'sz'):
                dst, dk = (spz, 'spz') if nm == 'pz' else (ssz, 'ssz')
                k.op('act', lambda e: e.activation(out=sgz[:], in_=pb[0:64, :], func=AF.Sigmoid),
                     reads=[pk], writes=['sgz'])
                k.op('dve', lambda e, dst=dst: e.tensor_tensor(out=dst[:], in0=pb[0:64, :], in1=sgz[:], op=ALU.mult),
                     reads=[pk, 'sgz'], writes=[dk])
            elif nm == 'sq':
                k.op('act', lambda e, tl=tl: e.activation(out=sqT[:, tl * 512:(tl + 1) * 512], in_=pb[0:64, :],
                                                           func=AF.Identity, scale=0.125),
                     reads=[pk], writes=[f"sqT{tl}"])
            else:
                k.op('dve', lambda e, tl=tl: e.tensor_copy(out=skT[:, tl * 512:(tl + 1) * 512], in_=pb[0:64, :]),
                     reads=[pk], writes=[f"skT{tl}"])
        if STAGE < 3:
            continue
        for tb in range(4):
            n = tl * 4 + tb
            tsl = slice(tb * 128, (tb + 1) * 128)
            pb, pk = nextpb()
            for kc in range(8):
                k.op('pe', lambda e, kc=kc, r=r, tsl=tsl: e.matmul(
                    pb[:, :], lhsT=hT[r][:, kc, tsl], rhs=wtm[:, kc, :], start=(kc == 0), stop=(kc == 7)),
                    reads=[f'wtm{kc}', hk], writes=[pk])
            for kc in range(8):
                k.op('pe', lambda e, kc=kc, r=r, tsl=tsl: e.matmul(
                    mb[:, 400:416], lhsT=hT[r][:, kc, tsl], rhs=wgt[:, kc, :], start=(kc == 0), stop=(kc == 7)),
                    reads=[f'wgt{kc}', hk], writes=['bank6'])
            gi = cnt['g'] % 2
            cnt['g'] += 1
            G = gsb[gi]
            gk = f"gsb{gi}"
            vi = n % 2
            k.op('pe', lambda e, tsl=tsl: e.transpose(tp[:, 0:128], kT[:, tsl], ident[:]),
                 reads=['kT', 'ident'], writes=['tpb'])
            k.op('dve', lambda e, vi=vi: e.tensor_copy(out=Ktm[vi][:], in_=tp[:, 0:128]),
                 reads=['tpb'], writes=[f"Ktm{vi}"])
            k.op('pe', lambda e, tsl=tsl: e.matmul(mb[:, 0:128], lhsT=kT[:, tsl], rhs=qT[:, tsl], start=True, stop=True),
                 reads=['kT', 'qT'], writes=['bank6'])
            k.op('dve', lambda e, vi=vi: e.tensor_tensor(out=Sm[vi][:], in0=mb[:, 0:128], in1=mtri[:], op=ALU.mult),
                 reads=['bank6', 'mtri'], writes=[f"Sm{vi}"])
            if pend['f'] is not None:
                pend['f']()
                pend['f'] = None
            ts = tmS[n % 2]
            tk = f"tmS{n % 2}"
            k.op('act', lambda e, ts=ts, pb=pb: e.activation(out=ts[:], in_=pb[:, :], func=AF.Identity),
                 reads=[pk], writes=[tk])
            k.op('pool', lambda e, n=n, ts=ts: e.tensor_copy(out=SV[:, n, :], in_=ts[:, 448:512]),
                 reads=[tk], writes=[f"SV{n}"])
            ui = n % 3
            k.op('pool', lambda e, ui=ui, ts=ts: e.tensor_copy(out=Ub[ui][:], in_=ts[:, 384:448]),
                 reads=[tk], writes=[f"Ub{ui}"])
            k.op('act', lambda e, gi=gi, ts=ts: e.activation(out=sgo[gi][:], in_=ts[:, 128:384], func=AF.Sigmoid),
                 reads=[tk], writes=[f"sgo{gi}"])
            k.op('dve', lambda e, gi=gi, ts=ts: e.tensor_tensor(out=gz[gi][:], in0=ts[:, 256:384], in1=sgo[gi][:, 128:256], op=ALU.mult),
                 reads=[tk, f"sgo{gi}"], writes=[f"gz{gi}"])
            k.op('pool', lambda e, gi=gi: e.tensor_tensor(out=gz[gi][:], in0=gz[gi][:], in1=mng[:], op=ALU.mult),
                 reads=[f"gz{gi}", 'mng'], writes=[f"gz{gi}"])
            if SUB < 1:
                continue
            k.op('act', lambda e, G=G: e.activation(out=G[:, 0:2], in_=mb[:, 400:402], func=AF.Identity),
                 reads=['bank6'], writes=[gk])
            k.op('act', lambda e, G=G: e.activation(out=G[:, 2:3], in_=G[:, 1:2], func=AF.Exp, scale=-1.0, bias=nmgb[:, 1:2]),
                 reads=[gk, 'nmgb'], writes=[gk])
            k.op('act', lambda e, G=G: e.activation(out=G[:, 3:4], in_=G[:, 2:3], func=AF.Ln, scale=1.0, bias=1.0),
                 reads=[gk], writes=[gk])
            k.op('pe', lambda e, G=G: e.matmul(mb[:, 404:405], lhsT=mtri[:], rhs=G[:, 3:4], start=True, stop=True),
                 reads=[gk, 'mtri'], writes=['bank6'])
            k.op('pe', lambda e, G=G: e.matmul(mb[:, 405:406], lhsT=onesf[:], rhs=G[:, 3:4], start=True, stop=True),
                 reads=[gk, 'onesf'], writes=['bank6'])
            k.op('act', lambda e, G=G: e.activation(out=G[:, 4:6], in_=mb[:, 404:406], func=AF.Exp, scale=-1.0),
                 reads=['bank6'], writes=[gk])
            k.op('dve', lambda e, G=G: e.tensor_tensor(out=G[:, 6:7], in0=mb[:, 404:405], in1=G[:, 0:1], op=ALU.add),
                 reads=['bank6', gk], writes=[gk])
            k.op('act', lambda e, G=G: e.activation(out=G[:, 7:8], in_=G[:, 6:7], func=AF.Exp, scale=1.0, bias=mgb[:, 0:1]),
                 reads=[gk, 'mgb'], writes=[gk])
            if SUB < 2:
                continue
            vi = n % 2
            k.op('dve', lambda e, vi=vi, G=G, ts=ts: e.tensor_scalar(out=V2[vi][:, 0:128], in0=ts[:, 0:128], scalar1=G[:, 7:8],
                                                              scalar2=None, op0=ALU.mult),
                 reads=[tk, gk], writes=[f"V2{vi}"])
            k.op('dve', lambda e, vi=vi, G=G: e.tensor_copy(out=V2[vi][:, 128:129], in_=G[:, 7:8]),
                 reads=[gk], writes=[f"V2{vi}"])
            if SUB < 3:
                continue
            k.op('pe', lambda e, tsl=tsl: e.matmul(mb[:, 128:258], lhsT=qT[:, tsl], rhs=Cnb[:, 0:130], start=True, stop=False),
                 reads=['qT', 'Cnb'], writes=['bank6'])
            k.op('pe', lambda e, vi=vi: e.matmul(mb[:, 128:258], lhsT=Sm[vi][:], rhs=V2[vi][:, 0:130], start=False, stop=True),
                 reads=[f"Sm{vi}", f"V2{vi}"], writes=['bank6'])
            k.op('pe', lambda e, vi=vi: e.matmul(mb[:, 260:390], lhsT=Ktm[vi][:], rhs=V2[vi][:, 0:130], start=True, stop=True),
                 reads=[f"Ktm{vi}", f"V2{vi}"], writes=['bank6'])
            k.op('dve', lambda e: e.tensor_tensor(out=Cn32[:, 0:130], in0=mb[:, 260:390], in1=Cn32[:, 0:130], op=ALU.add),
                 reads=['bank6', 'Cn32'], writes=['Cn32'])
            k.op('dve', lambda e, G=G: e.tensor_scalar(out=Cn32[:, 0:130], in0=Cn32[:, 0:130], scalar1=G[:, 5:6],
                                                       scalar2=None, op0=ALU.mult),
                 reads=['Cn32', gk], writes=['Cn32'])
            k.op('pool', lambda e: e.tensor_copy(out=Cnb[:, 0:130], in_=Cn32[:, 0:130]),
                 reads=['Cn32'], writes=['Cnb'])
            if SUB < 4:
                continue
            k.op('dve', lambda e, G=G: e.tensor_tensor(out=G[:, 8:9], in0=mb[:, 256:257], in1=G[:, 4:5], op=ALU.mult),
                 reads=['bank6', gk], writes=[gk])
            k.op('dve', lambda e, G=G: e.tensor_tensor(out=G[:, 8:9], in0=G[:, 8:9], in1=G[:, 8:9], op=ALU.mult),
                 reads=[gk], writes=[gk])
            k.op('dve', lambda e, G=G: e.tensor_scalar(out=G[:, 8:9], in0=G[:, 8:9], scalar1=1.0, scalar2=None, op0=ALU.max),
                 reads=[gk], writes=[gk])
            k.op('act', lambda e, G=G: e.activation(out=G[:, 14:15], in_=G[:, 8:9], func=AF.Ln),
                 reads=[gk], writes=[gk])
            k.op('act', lambda e, G=G: e.activation(out=G[:, 9:10], in_=G[:, 14:15], func=AF.Exp, scale=-0.5),
                 reads=[gk], writes=[gk])
            k.op('dve', lambda e, G=G: e.tensor_tensor(out=G[:, 10:11], in0=G[:, 9:10], in1=G[:, 4:5], op=ALU.mult),
                 reads=[gk], writes=[gk])
            k.op('dve', lambda e, G=G, gi=gi: e.scalar_tensor_tensor(
                out=t1[gi][:], in0=mb[:, 128:256], scalar=G[:, 10:11], in1=sgo[gi][:, 0:128], op0=ALU.mult, op1=ALU.mult),
                reads=['bank6', gk, f"sgo{gi}"], writes=[f"t1{gi}"])
            k.op('pool', lambda e, G=G: e.memset(G[:, 11:12], 0.0), reads=[], writes=[gk + 'a'])
            k.op('act', lambda e, G=G, gi=gi: e.activation(out=t1sq[:], in_=t1[gi][:], func=AF.Square, accum_out=G[:, 11:12]),
                 reads=[f"t1{gi}", gk + 'a'], writes=['t1sq', gk + 'a'])
            k.op('act', lambda e, G=G: e.activation(out=G[:, 12:13], in_=G[:, 11:12], func=AF.Ln, scale=1.0 / 128, bias=EPS),
                 reads=[gk + 'a'], writes=[gk + 'b'])
            k.op('act', lambda e, G=G: e.activation(out=G[:, 13:14], in_=G[:, 12:13], func=AF.Exp, scale=-0.5),
                 reads=[gk + 'b'], writes=[gk + 'c'])
            k.op('dve', lambda e, G=G, gi=gi: e.scalar_tensor_tensor(
                out=ymb[gi][:], in0=t1[gi][:], scalar=G[:, 13:14], in1=gz[gi][:], op0=ALU.mult, op1=ALU.mult),
                reads=[f"t1{gi}", gk + 'c', f"gz{gi}"], writes=[f"ymb{gi}"])
            def _flush(gi=gi, r=r, tsl=tsl):
                k.op('pe', lambda e: e.transpose(tp[:, 128:256], ymb[gi][:], ident[:]),
                     reads=[f"ymb{gi}", 'ident'], writes=['tpb'])
                k.op('act', lambda e: e.activation(out=ymT[r][:, tsl], in_=tp[:, 128:256], func=AF.Identity),
                     reads=['tpb'], writes=[f"ymT{r}"])
            pend['f'] = _flush
            if SUB < 5:
                continue
            bi = 0 if n == 0 else 1
            k.op('pe', lambda e, ui=ui, bi=bi, tsl=tsl, n=n: e.matmul(
                banks[OB][0:64, 0:128], lhsT=Ub[ui][:], rhs=bands[:, bi, :], start=True, stop=(n == 0)),
                reads=[f"Ub{ui}", 'bands'], writes=['bank4'])
            if n > 0:
                up = (n - 1) % 3
                k.op('pe', lambda e, up=up: e.matmul(
                    banks[OB][0:64, 0:128], lhsT=Ub[up][:], rhs=bands[:, 2, :], start=False, stop=True),
                    reads=[f"Ub{up}", 'bands'], writes=['bank4'])
            k.op('dve', lambda e, tsl=tsl: e.tensor_copy(out=pTs[:, tsl], in_=banks[OB][0:64, 0:128]),
                 reads=['bank4'], writes=['pTs'])
        if os.environ.get('A_SKIPPOST'):
            continue
        if pend['f'] is not None:
            pend['f']()
            pend['f'] = None
        k.dma('sp', yT_o('m', tl), ymT[r][:], reads=[f"ymT{r}"], writes=[f"oym{tl}"])
        cx.outkeys.append(f"oym{tl}")
        pb, pk = nextpb()
        k.op('pe', lambda e: e.matmul(pb[0:64, :], lhsT=poolw[:], rhs=pTs[:], start=True, stop=True),
             reads=['poolw', 'pTs'], writes=[pk])
        k.op('dve', lambda e, r=r: e.scalar_tensor_tensor(
            out=ypT[r][:], in0=pb[0:64, :], scalar=pscale[:, 0:1], in1=spz[:], op0=ALU.mult, op1=ALU.mult),
            reads=[pk, 'pscale', 'spz'], writes=[f"ypT{r}"])
        k.dma('sp', yT_o('p', tl), ypT[r][:], reads=[f"ypT{r}"], writes=[f"oyp{tl}"])
        cx.outkeys.append(f"oyp{tl}")
        if tl + 1 < ntile:
            norm_prefetch(tl + 1)
        top = 4 * tl + 3
        qsl = slice(tl * 512, (tl + 1) * 512)
        sq_keys = [f"sqT{tl}"]
        U = top + 1

        def st_Z(u):
            kb = top - u
            ci = u % 2
            zi = ZB[ci]
            zb = banks[zi]
            ksl = slice(kb * 128, (kb + 1) * 128)
            kkey = f"skT{kb // 4}"
            diag = kb >= 4 * tl
            k.op('pe', lambda e: e.matmul(zb[:, :], lhsT=skT[:, ksl], rhs=sqT[:, qsl], start=True, stop=True),
                 reads=[kkey] + sq_keys, writes=[f"bank{zi}"])
            k.op('act', lambda e: e.activation(out=Eb[ci][:], in_=zb[:, :], func=AF.Exp),
                 reads=[f"bank{zi}"], writes=[f"Eb{ci}"])
            if diag:
                k.op('act', lambda e: e.activation(out=L32[:], in_=Eb[ci][:], func=AF.Ln, scale=1.0, bias=1.0),
                     reads=[f"Eb{ci}"], writes=['L32'])
                k.op('dve', lambda e: e.tensor_tensor(out=Lb[ci][:], in0=L32[:], in1=sbm[:, kb - 4 * tl, :], op=ALU.mult),
                     reads=['L32', 'sbm'], writes=[f"Lb{ci}"])
            else:
                k.op('act', lambda e: e.activation(out=Lb[ci][:], in_=Eb[ci][:], func=AF.Ln, scale=1.0, bias=1.0),
                     reads=[f"Eb{ci}"], writes=[f"Lb{ci}"])
        def st_S(u, tl=tl, top=top):
            kb = top - u
            ci = u % 2
            if kb > 0:
                if kb == top:
                    k.op('dve', lambda e: e.tensor_copy(out=S32[:], in_=Lb[ci][:]), reads=[f"Lb{ci}"], writes=['S32'])
                else:
                    k.op('dve', lambda e: e.tensor_tensor(out=S32[:], in0=S32[:], in1=Lb[ci][:], op=ALU.add),
                         reads=['S32', f"Lb{ci}"], writes=['S32'])
                k.op('dve', lambda e: e.tensor_copy(out=Sb[1 - ci][:], in_=S32[:]), reads=['S32'], writes=[f"Sb{1 - ci}"])

        def st_L(u):
            kb = top - u
            ci = u % 2
            li = LB[ci]
            lb = banks[li]
            ksl = slice(kb * 128, (kb + 1) * 128)
            kkey = f"skT{kb // 4}"
            diag = kb >= 4 * tl
            k.op('pe', lambda e: e.matmul(lb[:, :], lhsT=skT[:, ksl], rhs=sqT[:, qsl], start=True, stop=False),
                 reads=[kkey] + sq_keys, writes=[f"bank{li}"])
            k.op('pe', lambda e: e.matmul(lb[:, :], lhsT=nui[:], rhs=Lb[ci][:], start=False, stop=(kb == top)),
                 reads=['nui', f"Lb{ci}"], writes=[f"bank{li}"])
            if kb != top:
                k.op('pe', lambda e: e.matmul(lb[:, :], lhsT=nones[:], rhs=Sb[ci][:], start=False, stop=True),
                     reads=['nones', f"Sb{ci}"], writes=[f"bank{li}"])
            k.op('act', lambda e: e.activation(out=At[ci][:], in_=lb[:, :], func=AF.Exp),
                 reads=[f"bank{li}"], writes=[f"At{ci}"])
            if diag:
                k.op('dve', lambda e: e.tensor_tensor(out=Am[ci][:], in0=At[ci][:], in1=sbm[:, kb - 4 * tl, :], op=ALU.mult),
                     reads=[f"At{ci}", 'sbm'], writes=[f"Am{ci}"])

        def st_V(u):
            kb = top - u
            ci = u % 2
            diag = kb >= 4 * tl
            asrc, akey = (Am[ci], f"Am{ci}") if diag else (At[ci], f"At{ci}")
            k.op('pe', lambda e: e.matmul(banks[OB][0:64, :], lhsT=SV[:, kb, :], rhs=asrc[:],
                                          start=(kb == top), stop=(kb == 0)),
                 reads=[f"SV{kb}", akey], writes=['bank4'])

        for step in range(U + 2):
            if step < U:
                st_Z(step)
            if 1 <= step <= U:
                st_L(step - 1)
            if step < U:
                st_S(step)
            if step >= 2:
                st_V(step - 2)
        k.op('dve', lambda e, r=r: e.tensor_tensor(out=ysT[r][:], in0=banks[OB][0:64, :], in1=ssz[:], op=ALU.mult),
             reads=['bank4', 'ssz'], writes=[f"ysT{r}"])
        k.dma('sp', yT_o('s', tl), ysT[r][:], reads=[f"ysT{r}"], writes=[f"oys{tl}"])
        cx.outkeys.append(f"oys{tl}")


def _consts():
    s = np.arange(128)
    mtri = (s[:, None] <= s[None, :]).astype(np.float32)
    nui = -(s[:, None] >= s[None, :]).astype(np.float32)
    t = np.arange(512)
    sbm = np.stack([((i * 128 + s)[:, None] < t[None, :]).astype(np.float32) for i in range(4)])
    return mtri, nui, sbm


def _bands(w):
    s = np.arange(128)
    out = np.zeros((3, 128, 128), np.float32)
    for t in range(128):
        lo = max(t + 1 - w, 0)
        out[0, lo:t + 1, t] += 1.0 / (t + 1 - lo)
        out[0, t, t] -= 1.0
        lo = max(t + 1 - w, 0)
        out[1, lo:t + 1, t] += 1.0 / w
        out[1, t, t] -= 1.0
        nprev = w - (t + 1)
        if nprev > 0:
            out[2, 128 - nprev:, t] += 1.0 / w
    return out


def hostA_inputs(inp, l, x_full):
    mtri, nui, sbm = _consts()
    ident = np.eye(128, dtype=np.float32)
    w_in = inp["w_in"][l]
    maps = []
    for core in range(8):
        b, hh = core // 4, core % 4
        c128 = slice(hh * 128, (hh + 1) * 128)
        c64 = slice(hh * 64, (hh + 1) * 64)
        off = dict(mq=0, mk=512, mv=1024, mi=1536, mf=1540, mo=1544, mz=2056, pu=2568, pz=2824,
                   sq=3080, sk=3336, sv=3592, sz=3848)
        col = lambda nm, sl: w_in[:, off[nm] + sl.start: off[nm] + sl.stop]
        wtm = np.concatenate([col('mv', c128), col('mo', c128), col('mz', c128), col('pu', c64), col('sv', c64)], 1)
        wgt = np.zeros((D, 16), np.float32)
        wgt[:, 0] = w_in[:, off['mi'] + hh]
        wgt[:, 1] = w_in[:, off['mf'] + hh]
        wfm = np.concatenate([col('mq', c128), col('mk', c128), col('pz', c64), col('sz', c64),
                              col('sq', c64), col('sk', c64)], 1)
        mg = inp["m_gate_b"][l]
        mgb = np.tile(np.array([[mg[hh], mg[4 + hh]]], np.float32), (128, 1))
        cwl = inp["conv_w"][l]
        cw = np.concatenate([cwl[:, c128].T, cwl[:, 512 + hh * 128: 512 + (hh + 1) * 128].T], 1)
        cbl = inp["conv_b"][l]
        cb = np.stack([cbl[c128], cbl[512 + hh * 128: 512 + (hh + 1) * 128]], 1)
        maps.append(dict(
            cT=np.ascontiguousarray(inp["c"][b].reshape(8, 128).T),
            ng=np.ascontiguousarray(inp["norm_g"][l].reshape(8, 128).T),
            w_ada=inp["w_ada"][l], b_ada=inp["b_ada"][l].reshape(1, -1),
            wtm=np.ascontiguousarray(wtm), wgt=np.ascontiguousarray(wgt), wfm=np.ascontiguousarray(wfm),
            mgb=mgb, cw=np.ascontiguousarray(cw), cb=np.ascontiguousarray(cb),
            mng=np.ascontiguousarray(inp["m_norm_g"][l][c128].reshape(1, 128)),
            poolw=np.ascontiguousarray(inp["pool_w"][l][hh]),
            pscale=np.ascontiguousarray(inp["pool_scale"][l][c64].reshape(64, 1)),
            bands=_bands(POOL_WINDOWS[hh]), mtri=mtri, sbm=sbm, nui=nui, ident=ident))
    return maps


def hostA_gather(results):
    out = np.zeros((NB, D, SEQ), dtype=ml_dtypes.bfloat16)
    for core in range(8):
        b, hh = core // 4, core % 4
        y = results[core]["yTo"]
        out[b, hh * 128:(hh + 1) * 128] = y[0:128]
        out[b, 512 + hh * 64:512 + (hh + 1) * 64] = y[128:192]
        out[b, 768 + hh * 64:768 + (hh + 1) * 64] = y[192:256]
    return out


def hostB_inputs(inp, l, x_full, yT_full):
    maps = []
    ident = np.eye(128, dtype=np.float32)
    wbr = np.concatenate([inp["w_br_m"][l], inp["w_br_p"][l], inp["w_br_s"][l]], 0)
    for core in range(8):
        b, j = core // 4, core % 4
        sl = slice(j * NTB, (j + 1) * NTB)
        maps.append(dict(
            cT=np.ascontiguousarray(inp["c"][b].reshape(8, 128).T),
            ng=np.ascontiguousarray(inp["norm_g"][l].reshape(8, 128).T),
            w_ada=inp["w_ada"][l], b_ada=inp["b_ada"][l].reshape(1, -1),
            wg=np.ascontiguousarray(inp["w_in"][l][:, 4104:]),
            gb=np.ascontiguousarray(inp["gate_b"][l].reshape(24, 128).T),
            wbr=wbr, wout=inp["w_out"][l], fg=inp["final_g"].reshape(1, -1), ident=ident))
    return maps


RG4 = [[0, 1, 2, 3], [4, 5, 6, 7]]
A_NAMES = dict(cT=[128, 8], ng=[128, 8], w_ada=[D, 3 * D], b_ada=[1, 3 * D], wtm=[D, 512], wgt=[D, 16],
               wfm=[D, 512], mgb=[128, 2], cw=[128, 8], cb=[128, 2], mng=[1, 128], poolw=[64, 64],
               pscale=[64, 1], bands=[3, 128, 128])
B_NAMES = dict(wg=[D, 3 * D], gb=[128, 24], wbr=[D, D], wout=[D, D])
C_NAMES = dict(mtri=[128, 128], sbm=[4, 128, 512], nui=[128, 128], ident=[128, 128], fg=[1, D])


def build_fused(ntile=SEQ // 512):
    nc = bass.Bass("TRN2", target_bir_lowering=False)
    dt_in = lambda name, shape, dt=F32: nc.dram_tensor(name, list(shape), dt, kind="ExternalInput").ap()
    x_in = dt_in("x", [SEQ, D])
    cst = {n: dt_in(n, sh) for n, sh in C_NAMES.items()}
    lay = []
    for l in range(DEPTH):
        d = {n: dt_in(f"{n}_{l}", sh) for n, sh in A_NAMES.items()}
        d.update({n: dt_in(f"{n}_{l}", sh) for n, sh in B_NAMES.items()})
        lay.append(d)
    out_d = nc.dram_tensor("out", [NTB, D], F32, kind="ExternalOutput").ap()
    internal = lambda name, shape, dt: nc.dram_tensor(name, list(shape), dt).ap()
    ys = internal("ys", [4 * 256, NTB], BF16)
    yr = internal("yr", [4 * 1024, NTB], BF16)
    yown = internal("yown", [D, NTB], BF16)
    xown = internal("xown", [NTB, D], F32)
    xs1 = internal("xs1", [NTB, D], F32)
    xg1 = internal("xg1", [8 * 1024, D], F32)
    ROW = dict(m=(0, 128), p=(128, 64), s=(192, 64))

    def ys_out(nm, tl):
        r0, nr = ROW[nm]
        q = tl // 4
        return ys[q * 256 + r0:q * 256 + r0 + nr, (tl % 4) * 512:(tl % 4 + 1) * 512]

    with contextlib.ExitStack() as outer:
        k = K(nc, outer)
        PID = nc.partition_id()

        def quarter():
            return PID % 4

        for l in range(DEPTH):
            last = (l == DEPTH - 1)
            with contextlib.ExitStack() as st:
                cx = Ctx(nc, st, k, prefix=f"A{l}_")
                a = dict(lay[l])
                a.update(cst)
                if l == 0:
                    a['x'] = lambda t0: x_in[t0:t0 + 128, :]
                else:
                    def xg_tile(t0):
                        rank, c, r0 = t0 // NTB, (t0 % NTB) // 256, t0 % 256
                        return xg1[c * 1024 + rank * 256 + r0:c * 1024 + rank * 256 + r0 + 128, :]
                    a['x'] = xg_tile
                    a['xreads'] = lambda t0: [f'xg1_{(t0 % NTB) // 256}']
                emit_A(cx, a, ys_out, ntile)
                akeys = list(cx.outkeys)
                k.barrier()
                k.emit()
            def pool_hook(q, akeys=akeys):
                k.coll("AllGather", RG4, ys[q * 256:(q + 1) * 256, :], yr[q * 1024:(q + 1) * 1024, :],
                       reads=akeys, writes=[f'yrecv{q}'])

            def after_prologue():
                for i in range(2):
                    k.dma('sp', yown[i * 512:(i + 1) * 512, :],
                          (lambda i=i: yr.rearrange("(q r) t -> q r t", q=4)[
                              bass.ds(quarter(), 1), i * 512:(i + 1) * 512, :].squeeze(0)),
                          reads=[f'yrecv{q}' for q in range(4)], writes=[f'yown_{i}'])
            ykeys = ['yown_0', 'yown_1']
            if l == 0:
                for i in range(4):
                    k.dma('sp', xown[i * 512:(i + 1) * 512, :],
                          (lambda i=i: x_in.rearrange("(q r) d -> q r d", q=4)[
                              bass.ds(quarter(), 1), i * 512:(i + 1) * 512, :].squeeze(0)), writes=[f'xown{i}'])
            with contextlib.ExitStack() as st:
                cx = Ctx(nc, st, k, prefix=f"B{l}_")

                def yT_d(tl):
                    cs = slice(tl * 512, (tl + 1) * 512)
                    res = []
                    for kk in range(4):
                        res.append((kk, 0, 128, yown[kk * 256:kk * 256 + 128, cs]))
                    for j, r0 in ((0, 128), (1, 192)):
                        for half in range(2):
                            kk = 4 + 2 * j + half
                            for hh2 in range(2):
                                rank = 2 * half + hh2
                                res.append((kk, hh2 * 64, 64, yown[rank * 256 + r0:rank * 256 + r0 + 64, cs]))
                    return res

                if l == 0:
                    x_d = lambda t0: xown[t0:t0 + 128, :]
                    x_reads = [f'xown{i}' for i in range(4)]
                else:
                    x_d = lambda t0: xs1[t0:t0 + 128, :]
                    x_reads = ['xs1']
                d = lay[l]
                def after_block(t0):
                    if (t0 + 128) % 256 == 0:
                        c = t0 // 256
                        k.coll("AllGather", RG4, xs1[c * 256:(c + 1) * 256, :], xg1[c * 1024:(c + 1) * 1024, :],
                               reads=[f"xo{t0 - 128}", f"xo{t0}"], writes=[f'xg1_{c}'])
                emit_B(cx, last, x_d, yT_d, d['cT'], d['ng'], d['w_ada'], d['b_ada'], d['wg'], d['gb'], d['wbr'],
                       d['wout'], cst['fg'], cst['ident'], out_d if last else xs1, x_reads=x_reads, y_reads=ykeys,
                       after_block=None if last else after_block, after_prologue=after_prologue, pool_hook=pool_hook)
                bkeys = list(cx.outkeys)
                if last:
                    k.wait_all('sp', bkeys)
                else:
                    k.barrier()
                k.emit()
            if not last:
                k.last_w['xs1'] = k.last_w[bkeys[-1]]
    return nc


def host_inputs(inp):
    mtri, nui, sbm = _consts()
    ident = np.eye(128, dtype=np.float32)
    maps = [dict(mtri=mtri, nui=nui, sbm=sbm, ident=ident, fg=inp["final_g"].reshape(1, -1).astype(np.float32))
            for _ in range(8)]
    for core in range(8):
        maps[core]["x"] = np.ascontiguousarray(inp["x"][core // 4])
    dummy_x = None
    for l in range(DEPTH):
        ma = hostA_inputs(inp, l, None)
        mb = hostB_inputs(inp, l, None, None)
        for core in range(8):
            for n in A_NAMES:
                maps[core][f"{n}_{l}"] = ma[core][n]
            for n in B_NAMES:
                maps[core][f"{n}_{l}"] = mb[core][n]
    return maps


_NC = {}


def kernel(**inputs):
    inp = {k: np.asarray(v) for k, v in inputs.items()}
    if 'nc' not in _NC:
        _NC['nc'] = build_fused()
    res = run_bass_kernel_spmd(_NC['nc'], host_inputs(inp), core_ids=list(range(8)))
    out = np.stack([np.concatenate([res.results[b * 4 + j]["out"] for j in range(4)], 0) for b in range(NB)])
    return out.astype(np.float32)
```
